# Optimizing a Trainium2 kernel written in Bass

```python
import jax, jax.numpy as jnp
from jax import lax
import numpy as np

D_MODEL = 1024
BATCH = 2
SEQ = 8192
DEPTH = 2

HEAD_DIM = 64
BLOCK_Q = 128
ROPE_THETA = 10000.0
GRID_W = 64
EPS = 1e-6
D_FF = 2816
N_EVEN = (DEPTH + 1) // 2
N_ODD = DEPTH // 2
GMLP_GROUPS = 8
GMLP_GROUP_DIM = 64
GMLP_CHUNK = 128
GMLP_WIDTH = GMLP_GROUPS * GMLP_GROUP_DIM
DIFF_HEADS = 4
DIFF_D = 64
DIFF_V = 2 * DIFF_D
MLA_HEADS = 8
MLA_Q_RANK = 256
MLA_KV_RANK = 128
MLA_NOPE = 64
MLA_ROPE = 32
MLA_V = 64
GQA_Q_HEADS = 8
GQA_KV_HEADS = 2
GQA_GROUP = GQA_Q_HEADS // GQA_KV_HEADS
GQA_DIM = 64

EVEN_IN = 2 * GMLP_WIDTH + 2 * DIFF_HEADS * 2 * DIFF_D + DIFF_HEADS * DIFF_V
EVEN_OUT = GMLP_WIDTH + DIFF_HEADS * DIFF_V
ODD_IN = (MLA_Q_RANK + MLA_KV_RANK + MLA_ROPE
          + GQA_Q_HEADS * GQA_DIM + 2 * GQA_KV_HEADS * GQA_DIM)
ODD_OUT = MLA_HEADS * MLA_V + GQA_Q_HEADS * GQA_DIM

kernel_name = "hybrid_gmlp_diffattn_mla_axialgqa_macaron"


def rmsnorm(x, g):
    xf = x.astype(jnp.float32)
    y = xf * lax.rsqrt(jnp.mean(xf * xf, axis=-1, keepdims=True) + EPS)
    return (y * g.astype(jnp.float32)).astype(x.dtype)


def rope_angles(pos, dim):
    inv = ROPE_THETA ** (-jnp.arange(0, dim, 2, dtype=jnp.float32) / dim)
    ang = pos.astype(jnp.float32)[:, None] * inv[None, :]
    return jnp.cos(ang), jnp.sin(ang)


def apply_rope(x, cos, sin):
    shape = (cos.shape[0],) + (1,) * (x.ndim - 3) + (cos.shape[-1],)
    c = cos.reshape(shape)
    s = sin.reshape(shape)
    xf = x.astype(jnp.float32)
    x1, x2 = jnp.split(xf, 2, axis=-1)
    return jnp.concatenate([x1 * c - x2 * s, x1 * s + x2 * c], axis=-1).astype(x.dtype)


def axial_rope(x, row_cs, col_cs):
    half = x.shape[-1] // 2
    return jnp.concatenate([apply_rope(x[..., :half], *row_cs),
                            apply_rope(x[..., half:], *col_cs)], axis=-1)


def to_blocks(t):
    b, s = t.shape[:2]
    t = t.reshape((b, s // BLOCK_Q, BLOCK_Q) + t.shape[2:])
    return jnp.moveaxis(t, 1, 0)


def from_blocks(t):
    t = jnp.moveaxis(t, 0, 1)
    return t.reshape((t.shape[0], t.shape[1] * t.shape[2]) + t.shape[3:])


def gqa_attention(q, k, v):
    scale = q.shape[-1] ** -0.5

    def blk(qb):
        s = jnp.einsum('bqhgd,bshd->bhgqs', qb, k).astype(jnp.float32) * scale
        p = jax.nn.softmax(s, axis=-1).astype(v.dtype)
        return jnp.einsum('bhgqs,bshe->bqhge', p, v)

    return from_blocks(lax.map(blk, to_blocks(q)))


def swiglu(h, w_gu, w_down):
    g, u = jnp.split(h @ w_gu, 2, axis=-1)
    return (jax.nn.silu(g) * u) @ w_down


def mixer_gmlp(u, v, sgu_norm, w_s, b_s):
    b, s, _ = v.shape
    vc = v.reshape(b, s // GMLP_CHUNK, GMLP_CHUNK, GMLP_GROUPS, GMLP_GROUP_DIM)
    vc = rmsnorm(vc, sgu_norm)
    mixed = jnp.einsum('gij,bcjgd->bcigd', w_s, vc) + b_s.T[:, :, None]
    return u * mixed.reshape(b, s, GMLP_WIDTH)


def mixer_diff(q, k, v, q_norm, k_norm, lam_q1, lam_k1, lam_q2, lam_k2,
               sub_norm, lam_init, cos, sin):
    b, s = q.shape[:2]
    q = apply_rope(rmsnorm(q, q_norm), cos, sin)
    k = apply_rope(rmsnorm(k, k_norm), cos, sin)
    f32 = jnp.float32
    lam = (jnp.exp(jnp.sum(lam_q1.astype(f32) * lam_k1.astype(f32)))
           - jnp.exp(jnp.sum(lam_q2.astype(f32) * lam_k2.astype(f32))) + lam_init)
    scale = DIFF_D ** -0.5

    def blk(qb):
        sc = jnp.einsum('bqhmd,bshmd->bhmqs', qb, k).astype(f32) * scale
        p = jax.nn.softmax(sc, axis=-1)
        w = (p[:, :, 0] - lam * p[:, :, 1]).astype(v.dtype)
        return jnp.einsum('bhqs,bshe->bqhe', w, v)

    o = from_blocks(lax.map(blk, to_blocks(q)))
    o = rmsnorm(o, sub_norm) * (1.0 - lam_init)
    return o.reshape(b, s, DIFF_HEADS * DIFF_V)


def setup_inputs(seed: int = 0) -> dict:
    key = jax.random.key(seed)
    ks = iter(jax.random.split(key, 40))
    f32 = jnp.float32

    def w(shape, fan_in):
        return jax.random.normal(next(ks), shape, f32) * (fan_in ** -0.5)

    def gain(shape):
        return 1.0 + 0.02 * jax.random.normal(next(ks), shape, f32)

    def small(shape, scale):
        return scale * jax.random.normal(next(ks), shape, f32)

    d = D_MODEL
    return {
        "x": jax.random.normal(next(ks), (BATCH, SEQ, d), f32),
        "ffn1_norm": gain((DEPTH, d)),
        "ffn1_w_gu": w((DEPTH, d, 2 * D_FF), d),
        "ffn1_w_down": w((DEPTH, D_FF, d), D_FF),
        "ffn2_norm": gain((DEPTH, d)),
        "ffn2_w_gu": w((DEPTH, d, 2 * D_FF), d),
        "ffn2_w_down": w((DEPTH, D_FF, d), D_FF),
        "ev_norm": gain((N_EVEN, d)),
        "ev_w_in": w((N_EVEN, d, EVEN_IN), d),
        "ev_sgu_norm": gain((N_EVEN, GMLP_GROUPS, GMLP_GROUP_DIM)),
        "ev_w_s": w((N_EVEN, GMLP_GROUPS, GMLP_CHUNK, GMLP_CHUNK), GMLP_CHUNK),
        "ev_b_s": gain((N_EVEN, GMLP_GROUPS, GMLP_CHUNK)),
        "ev_q_norm": gain((N_EVEN, DIFF_D)),
        "ev_k_norm": gain((N_EVEN, DIFF_D)),
        "ev_lam_q1": small((N_EVEN, DIFF_D), 0.1),
        "ev_lam_k1": small((N_EVEN, DIFF_D), 0.1),
        "ev_lam_q2": small((N_EVEN, DIFF_D), 0.1),
        "ev_lam_k2": small((N_EVEN, DIFF_D), 0.1),
        "ev_sub_norm": gain((N_EVEN, DIFF_V)),
        "ev_w_out": w((N_EVEN, EVEN_OUT, d), EVEN_OUT),
        "od_norm": gain((N_ODD, d)),
        "od_w_in": w((N_ODD, d, ODD_IN), d),
        "od_cq_norm": gain((N_ODD, MLA_Q_RANK)),
        "od_ckv_norm": gain((N_ODD, MLA_KV_RANK)),
        "od_w_uq": w((N_ODD, MLA_Q_RANK, MLA_HEADS * (MLA_NOPE + MLA_ROPE)), MLA_Q_RANK),
        "od_w_ukv": w((N_ODD, MLA_KV_RANK, MLA_HEADS * (MLA_NOPE + MLA_V)), MLA_KV_RANK),
        "od_mla_q_norm": gain((N_ODD, MLA_NOPE + MLA_ROPE)),
        "od_mla_k_norm": gain((N_ODD, MLA_NOPE + MLA_ROPE)),
        "od_gqa_q_norm": gain((N_ODD, GQA_DIM)),
        "od_gqa_k_norm": gain((N_ODD, GQA_DIM)),
        "od_w_out": w((N_ODD, ODD_OUT, d), ODD_OUT),
    }


def reference(x, ffn1_norm, ffn1_w_gu, ffn1_w_down, ffn2_norm, ffn2_w_gu, ffn2_w_down,
              ev_norm, ev_w_in, ev_sgu_norm, ev_w_s, ev_b_s, ev_q_norm, ev_k_norm,
              ev_lam_q1, ev_lam_k1, ev_lam_q2, ev_lam_k2, ev_sub_norm, ev_w_out,
              od_norm, od_w_in, od_cq_norm, od_ckv_norm, od_w_uq, od_w_ukv,
              od_mla_q_norm, od_mla_k_norm, od_gqa_q_norm, od_gqa_k_norm, od_w_out):
    b, s, _ = x.shape
    rows = s // GRID_W
    pos = jnp.arange(s, dtype=jnp.int32)
    row_idx = jnp.repeat(jnp.arange(rows, dtype=jnp.int32), GRID_W)
    col_idx = jnp.tile(jnp.arange(GRID_W, dtype=jnp.int32), rows)
    cs_full = rope_angles(pos, DIFF_D)
    cs_mla = rope_angles(pos, MLA_ROPE)
    cs_row = rope_angles(row_idx, GQA_DIM // 2)
    cs_col = rope_angles(col_idx, GQA_DIM // 2)

    for l in range(DEPTH):
        x = x + 0.5 * swiglu(rmsnorm(x, ffn1_norm[l]), ffn1_w_gu[l], ffn1_w_down[l])
        if l % 2 == 0:
            i = l // 2
            proj = rmsnorm(x, ev_norm[i]) @ ev_w_in[i]
            a_uv, bq, bk, bv = jnp.split(
                proj, [2 * GMLP_WIDTH, 2 * GMLP_WIDTH + 2 * DIFF_HEADS * DIFF_D,
                       2 * GMLP_WIDTH + 4 * DIFF_HEADS * DIFF_D], axis=-1)
            u, v = jnp.split(jax.nn.gelu(a_uv), 2, axis=-1)
            out_a = mixer_gmlp(u, v, ev_sgu_norm[i], ev_w_s[i], ev_b_s[i])
            lam_init = 0.8 - 0.6 * float(np.exp(-0.3 * l))
            out_b = mixer_diff(
                bq.reshape(b, s, DIFF_HEADS, 2, DIFF_D),
                bk.reshape(b, s, DIFF_HEADS, 2, DIFF_D),
                bv.reshape(b, s, DIFF_HEADS, DIFF_V),
                ev_q_norm[i], ev_k_norm[i], ev_lam_q1[i], ev_lam_k1[i],
                ev_lam_q2[i], ev_lam_k2[i], ev_sub_norm[i], lam_init, *cs_full)
            mix = jnp.concatenate([out_a, out_b], axis=-1) @ ev_w_out[i]
        else:
            i = l // 2
            proj = rmsnorm(x, od_norm[i]) @ od_w_in[i]
            o1 = MLA_Q_RANK
            o2 = o1 + MLA_KV_RANK
            o3 = o2 + MLA_ROPE
            o4 = o3 + GQA_Q_HEADS * GQA_DIM
            o5 = o4 + GQA_KV_HEADS * GQA_DIM
            c_q, c_kv, k_pe, gq, gk, gv = jnp.split(proj, [o1, o2, o3, o4, o5], axis=-1)
            q = (rmsnorm(c_q, od_cq_norm[i]) @ od_w_uq[i]).reshape(
                b, s, MLA_HEADS, MLA_NOPE + MLA_ROPE)
            kv = (rmsnorm(c_kv, od_ckv_norm[i]) @ od_w_ukv[i]).reshape(
                b, s, MLA_HEADS, MLA_NOPE + MLA_V)
            k_nope, v_c = jnp.split(kv, [MLA_NOPE], axis=-1)
            k = jnp.concatenate(
                [k_nope, jnp.broadcast_to(k_pe[:, :, None, :], (b, s, MLA_HEADS, MLA_ROPE))],
                axis=-1)
            q = rmsnorm(q, od_mla_q_norm[i])
            k = rmsnorm(k, od_mla_k_norm[i])
            q = jnp.concatenate([q[..., :MLA_NOPE], apply_rope(q[..., MLA_NOPE:], *cs_mla)], axis=-1)
            k = jnp.concatenate([k[..., :MLA_NOPE], apply_rope(k[..., MLA_NOPE:], *cs_mla)], axis=-1)
            out_c = gqa_attention(q[:, :, :, None, :], k, v_c).reshape(b, s, MLA_HEADS * MLA_V)
            qd = rmsnorm(gq.reshape(b, s, GQA_KV_HEADS, GQA_GROUP, GQA_DIM), od_gqa_q_norm[i])
            kd = rmsnorm(gk.reshape(b, s, GQA_KV_HEADS, GQA_DIM), od_gqa_k_norm[i])
            vd = gv.reshape(b, s, GQA_KV_HEADS, GQA_DIM)
            qd = axial_rope(qd, cs_row, cs_col)
            kd = axial_rope(kd, cs_row, cs_col)
            out_d = gqa_attention(qd, kd, vd).reshape(b, s, GQA_Q_HEADS * GQA_DIM)
            mix = jnp.concatenate([out_c, out_d], axis=-1) @ od_w_out[i]
        x = x + mix
        x = x + 0.5 * swiglu(rmsnorm(x, ffn2_norm[l]), ffn2_w_gu[l], ffn2_w_down[l])
    return x
```

```python
import contextlib
import numpy as np
import ml_dtypes
import concourse.bass as bass
import concourse.mybir as mybir
from concourse.bass_utils import run_bass_kernel_spmd

F32 = mybir.dt.float32
BF16 = mybir.dt.bfloat16
AF = mybir.ActivationFunctionType
ALU = mybir.AluOpType
AX = mybir.AxisListType

D = 1024
DFF = 2816
NJ = 22
NJH = 11
EPS = 1e-6
THETA = 10000.0
GRID_W = 64
ENGS = ("pe", "act", "dve", "pool", "sp")


class Res:
    __slots__ = ("name", "w", "rs")

    def __init__(self, name=""):
        self.name = name
        self.w = None
        self.rs = []


class Op:
    __slots__ = ("eng", "fn", "deps", "pos", "dma", "signal", "sem", "target", "prev_target", "inc", "cc")

    def __init__(self, eng, fn, dma, inc):
        self.eng = eng
        self.fn = fn
        self.dma = dma
        self.deps = []
        self.pos = -1
        self.signal = False
        self.sem = None
        self.target = 0
        self.prev_target = 0
        self.inc = inc
        self.cc = False


class Prog:
    NDMA_SEMS = 8

    def __init__(self, nc):
        self.nc = nc
        self.ops = {e: [] for e in ENGS}
        self.pending_barrier = {e: [] for e in ENGS}

    def op(self, eng, fn, reads=(), writes=(), dma=False, inc=16):
        o = Op(eng, fn, dma, inc)
        o.pos = len(self.ops[eng])
        deps = set(self.pending_barrier[eng])
        self.pending_barrier[eng] = []
        rawset = set()
        for r in reads:
            if r.w is not None:
                deps.add(r.w)
                rawset.add(r.w)
        for w in writes:
            if w.w is not None:
                deps.add(w.w)
            for rd in w.rs:
                deps.add(rd)
        best = {}
        red = []
        for d in deps:
            if d.dma:
                red.append(d)
            elif d.eng not in best or best[d.eng].pos < d.pos:
                best[d.eng] = d
        red.extend(best.values())
        for d in red:
            if d is o:
                continue
            if d.dma or o.dma or d.eng != eng:
                o.deps.append(d)
                d.signal = True
            elif eng != "pe" and (o.pos - d.pos) <= 3 and d in rawset:
                o.deps.append(d)
                d.signal = True
        for r in reads:
            r.rs.append(o)
        for w in writes:
            w.w = o
            w.rs = []
        self.ops[eng].append(o)
        return o

    def barrier(self):
        lasts = []
        for e in ENGS:
            comp = [o for o in self.ops[e] if not o.dma]
            if comp:
                lasts.append(comp[-1])
            dmas = [o for o in self.ops[e] if o.dma and not o.cc]
            lasts.extend(dmas[-self.NDMA_SEMS:])
            lasts.extend([o for o in self.ops[e] if o.cc])
        for e in ENGS:
            self.pending_barrier[e] = list(lasts)

    def emit(self):
        nc = self.nc
        with contextlib.ExitStack() as st:
            esem = {e: st.enter_context(nc.semaphore("s_" + e)) for e in ENGS}
            dsem = {}
            for q in ENGS:
                nd = sum(1 for o in self.ops[q] if o.dma and not o.cc)
                if nd:
                    dsem[q] = [st.enter_context(nc.semaphore("d_%s%d" % (q, i)))
                               for i in range(min(nd, self.NDMA_SEMS))]
            for e in ENGS:
                c = 0
                rr = 0
                nsem = len(dsem.get(e, []))
                tot = [0] * max(nsem, 1)
                for o in self.ops[e]:
                    if o.cc:
                        o.sem = st.enter_context(nc.semaphore("cc_%d" % o.pos))
                        o.prev_target = 0
                        o.target = o.inc
                    elif o.dma:
                        o.sem = dsem[e][rr]
                        o.prev_target = tot[rr]
                        tot[rr] += o.inc
                        o.target = tot[rr]
                        rr = (rr + 1) % nsem
                    elif o.signal:
                        c += 1
                        o.sem = esem[e]
                        o.target = c
            block = st.enter_context(nc.Block())

            def run(e, eng):
                seen = {}
                for o in self.ops[e]:
                    waits = {}
                    for d in o.deps:
                        key = id(d.sem)
                        if seen.get(key, 0) >= d.target:
                            continue
                        if key not in waits or waits[key][1] < d.target:
                            waits[key] = (d.sem, d.target)
                    if o.dma and o.prev_target > 0:
                        key = id(o.sem)
                        if seen.get(key, 0) < o.prev_target:
                            if key not in waits or waits[key][1] < o.prev_target:
                                waits[key] = (o.sem, o.prev_target)
                    for key, (s, v) in waits.items():
                        eng.wait_ge(s, v)
                        seen[key] = v
                    ins = o.fn(eng)
                    if o.dma:
                        ins.then_inc(o.sem, o.inc)
                    elif o.signal:
                        ins.then_inc(o.sem, 1)
                if e in dsem or any(o.cc for o in self.ops[e]):
                    tot = {}
                    for o in self.ops[e]:
                        if o.dma:
                            tot[id(o.sem)] = (o.sem, o.target)
                    for key, (s, v) in tot.items():
                        if seen.get(key, 0) < v:
                            eng.wait_ge(s, v)

            block.tensor(lambda eng: run("pe", eng))
            block.scalar(lambda eng: run("act", eng))
            block.vector(lambda eng: run("dve", eng))
            block.gpsimd(lambda eng: run("pool", eng))
            block.sync(lambda eng: run("sp", eng))


def _chunk_fm(w, cols):
    sub = w[:, cols]
    m = sub.shape[1]
    return np.ascontiguousarray(sub.reshape(8, 128, m).transpose(1, 0, 2).reshape(128, 8 * m))


def _rope_inv(dim):
    return (np.float32(THETA) ** (-np.arange(0, dim, 2, dtype=np.float32) / np.float32(dim))).astype(np.float32)


def host_consts(T, S, core):
    r = core % 4
    pos = (r * T + np.arange(T)).astype(np.int32)
    eye = np.eye(128, dtype=np.float32)
    ones = np.ones((128, 128), np.float32)
    bd = np.zeros((128, 128), np.float32)
    bd[:64, :64] = 1
    bd[64:, 64:] = 1
    ones96 = np.zeros((128, 128), np.float32)
    ones96[:96, :96] = 1

    def rot(blocks):
        m = np.zeros((128, 128), np.float32)
        for (b0, half) in blocks:
            for d in range(half):
                m[b0 + d + half, b0 + d] = -1.0
                m[b0 + d, b0 + d + half] = 1.0
        return m
    rm_e = rot([(0, 32), (64, 32)])
    rm_mla = rot([(64, 16)])
    rm_gqa = rot([(0, 16), (32, 16), (64, 16), (96, 16)])
    sel = np.zeros((128, 128), np.float32)
    for i in range(32):
        sel[i, 64 + i] = 1.0
    cmat = np.concatenate([eye, ones, bd, ones96, rm_e, rm_mla, rm_gqa, sel], axis=1)

    def angles(p, dim):
        inv = _rope_inv(dim)
        return p.astype(np.float32)[:, None] * inv[None, :]
    a = angles(pos, 64)
    idx = (np.arange(128) % 64) % 32
    cs_e = np.stack([np.cos(a)[:, idx].T, np.sin(a)[:, idx].T]).astype(np.float32)
    a = angles(pos, 32)
    cs_m = np.zeros((2, 128, T), np.float32)
    cs_m[0, :64] = 1.0
    idx = (np.arange(32)) % 16
    cs_m[0, 64:96] = np.cos(a)[:, idx].T
    cs_m[1, 64:96] = np.sin(a)[:, idx].T
    ar = angles(pos // GRID_W, 32)
    ac = angles(pos % GRID_W, 32)
    cs_g = np.zeros((2, 128, T), np.float32)
    for p in range(128):
        d = p % 64
        src = ar if d < 32 else ac
        f = (d % 32) % 16
        cs_g[0, p] = np.cos(src)[:, f]
        cs_g[1, p] = np.sin(src)[:, f]
    return cmat, cs_e, cs_m, cs_g


class Cfg:
    def __init__(self, T, TG):
        self.T = T
        self.S = 4 * T
        self.TG = TG
        self.NT = T // 128
        self.NG = T // TG
        self.TPG = TG // 128
        self.KT = self.S // 128
        self.arena = max(34 * T + 2048, 44000)


V_FFN = 0
V_EQ = 48
V_EK = 49
V_CQ = 50
V_CKV = 52
V_MQ = 53
V_MK = 54
V_GQ = 55
V_GK = 56
NVEC = 57
R_SGU = 0
R_SUB = 512
R_LAM = 640
NROW = 896
C_ID, C_ONES, C_BD, C_O96, C_RME, C_RMM, C_RMG, C_SEL = range(8)


class KVScr:
    def __init__(self, segs):
        self.segs = list(segs)
        self.b0 = [0]
        for n in self.segs:
            self.b0.append(self.b0[-1] + n)
        self.loc_t = [None] * len(self.segs)
        self.gat_t = [None] * len(self.segs)

    def _find(self, a, b):
        for i, n in enumerate(self.segs):
            if self.b0[i] <= a and b <= self.b0[i + 1]:
                return i
        raise AssertionError(("kv rows cross a segment", a, b))

    def loc(self, a, b):
        i = self._find(a, b)
        return self.loc_t[i][a - self.b0[i]: b - self.b0[i], :]

    def gat(self, r, a, b):
        i = self._find(a, b)
        n = self.segs[i]
        return self.gat_t[i][r * n + a - self.b0[i]: r * n + b - self.b0[i], :]


class Builder:
    def __init__(self, cfg, stages, fused):
        self.cfg = cfg
        self.stages = stages
        self.fused = fused
        self.nc = bass.Bass("TRN2", target_bir_lowering=False)
        self.P = Prog(self.nc)
        self.st = contextlib.ExitStack()
        self.din = {}
        self.dout = {}
        self.res = {}

    def R(self, name):
        if name not in self.res:
            self.res[name] = Res(name)
        return self.res[name]

    def inp(self, name, shape, dt=F32):
        t = self.nc.dram_tensor(name, list(shape), dt, kind="ExternalInput")
        self.din[name] = (tuple(shape), dt)
        return t.ap()

    def outp(self, name, shape, dt=F32):
        t = self.nc.dram_tensor(name, list(shape), dt, kind="ExternalOutput")
        self.dout[name] = (tuple(shape), dt)
        return t.ap()

    def sb(self, name, shape, dt):
        return self.st.enter_context(self.nc.sbuf_tensor(name, list(shape), dt))

    def mm(self, out, lhsT, rhs, start, stop, reads, writes):
        self.P.op("pe", lambda e: e.matmul(out, lhsT=lhsT, rhs=rhs, start=start, stop=stop),
                  reads, writes)

    def act(self, out, in_, func, reads, writes, scale=1.0, bias=None, accum_out=None, eng="act"):
        kw = {}
        if bias is not None:
            kw["bias"] = bias
        if accum_out is not None:
            kw["accum_out"] = accum_out
        self.P.op("act", lambda e: e.activation(out=out, in_=in_, func=func, scale=scale, **kw),
                  reads, writes)

    def tt(self, eng, out, in0, in1, op, reads, writes):
        self.P.op(eng, lambda e: e.tensor_tensor(out=out, in0=in0, in1=in1, op=op), reads, writes)

    def stt(self, eng, out, in0, scalar, in1, op0, op1, reads, writes):
        self.P.op(eng, lambda e: e.scalar_tensor_tensor(out=out, in0=in0, scalar=scalar, in1=in1,
                                                        op0=op0, op1=op1), reads, writes)

    def ts(self, eng, out, in0, s1, op0, reads, writes, s2=None, op1=None):
        if op1 is None:
            self.P.op(eng, lambda e: e.tensor_scalar(out=out, in0=in0, scalar1=s1, scalar2=None, op0=op0),
                      reads, writes)
        else:
            self.P.op(eng, lambda e: e.tensor_scalar(out=out, in0=in0, scalar1=s1, scalar2=s2, op0=op0,
                                                     op1=op1), reads, writes)

    def cp(self, eng, out, in_, reads, writes):
        if eng == "act":
            self.P.op("act", lambda e: e.copy(out=out, in_=in_), reads, writes)
        else:
            self.P.op(eng, lambda e: e.tensor_copy(out=out, in_=in_), reads, writes)

    def recip(self, out, in_, reads, writes):
        self.P.op("dve", lambda e: e.reciprocal(out=out, in_=in_), reads, writes)

    def dma(self, q, out, in_, reads, writes):
        self.P.op(q, lambda e: e.dma_start(out=out, in_=in_), reads, writes, dma=True)

    def arena_reset(self, lo=0, hi=None):
        self.a_lo = lo
        self.a_hi = self.cfg.arena if hi is None else hi

    def take(self, n, dt=BF16):
        nb = n * (2 if dt == F32 else 1)
        nb = (nb + 15) // 16 * 16
        off = self.a_lo
        self.a_lo += nb
        assert self.a_lo <= self.a_hi, ("arena overflow", self.a_lo, self.a_hi)
        ap = self.arena[:, off:off + n * (2 if dt == F32 else 1)]
        if dt == F32:
            ap = ap.bitcast(F32)
        return ap

    def fixed(self, off, n):
        return self.arena[:, off:off + n]

    def setup(self):
        cfg = self.cfg
        T = cfg.T
        self.xres = self.sb("xres", [128, 8, T], F32)
        self.arena = self.sb("arena", [128, cfg.arena], BF16)[:]
        self.cmat = self.sb("cmat_sb", [128, 8 * 128], BF16)
        self.vecs = self.sb("vecs_sb", [128, NVEC], F32)
        self.epsc = self.sb("epsc", [128, 1], F32)
        self.onescol = self.sb("onescol", [128, 1], BF16)
        self.bank = [self.st.enter_context(self.nc.psum_tensor("bank%d" % i, [128, 512], F32))
                     for i in range(8)]
        self.bres = [self.R("bank%d" % i) for i in range(8)]
        self.Rx = [[self.R("x_%d_%d" % (c, g)) for g in range(cfg.NG)] for c in range(8)]
        d_cmat = self.inp("cmat", [128, 8 * 128])
        d_vecs = self.inp("vecs", [128, NVEC])
        self.d_rows = self.inp("rows", [1, NROW])
        self.dma("pool", self.cmat[:], d_cmat, [], [self.R("cmat")])
        self.dma("sp", self.vecs[:], d_vecs, [], [self.R("vecs")])
        self.P.op("dve", lambda e: e.memset(self.epsc[:], EPS), [], [self.R("epsc")])
        self.P.op("dve", lambda e: e.memset(self.onescol[:], 1.0), [], [self.R("onescol")])

    def cm(self, idx, rows=128, cols=128):
        return self.cmat[0:rows, idx * 128: idx * 128 + cols]

    def norm_to_hT(self, hT, vcol, tmp_sq, tmp_rstd, tmp_sd):
        cfg = self.cfg
        T, TG, NG = cfg.T, cfg.TG, cfg.NG
        Rh = self.R("hT")
        Rrs = self.R("rstd")
        for c in range(8):
            sq = tmp_sq[c % 2]
            Rsq = self.R("sq%d" % (c % 2))
            self.act(sq, self.xres[:, c, :], AF.Square, self.Rx[c], [Rsq])
            for g in range(NG):
                self.mm(self.bank[g][:, 0:TG], self.cm(C_ONES), sq[:, g * TG:(g + 1) * TG],
                        c == 0, c == 7, [Rsq, self.R("cmat")], [self.bres[g]])
        for g in range(NG):
            sl = slice(g * TG, (g + 1) * TG)
            self.act(tmp_sd[:, sl], self.bank[g][:, 0:TG], AF.Sqrt, [self.bres[g], self.R("epsc")],
                     [self.R("sd")], scale=1.0 / D, bias=self.epsc[:])
        self.recip(tmp_rstd, tmp_sd, [self.R("sd")], [Rrs])
        for c in range(8):
            self.stt("dve", hT[:, c, :], self.xres[:, c, :], self.vecs[:, vcol + c: vcol + c + 1], tmp_rstd,
                     ALU.mult, ALU.mult, self.Rx[c] + [Rrs, self.R("vecs")], [Rh])

    def ffn(self, fi, vcol):
        cfg = self.cfg
        T, TG, NG = cfg.T, cfg.TG, cfg.NG
        P = self.P
        P.barrier()
        self.arena_reset()
        hT = self.take(8 * T).rearrange("p (c t) -> p c t", c=8)
        actA = self.take(NJH * T).rearrange("p (j t) -> p j t", j=NJH)
        wg = [self.take(2048).rearrange("p (c m) -> p c m", c=8) for _ in range(3)]
        wdb = [self.take(NJH * 128).rearrange("p (j m) -> p j m", j=NJH) for _ in range(3)]
        rstd = self.take(T, F32)
        sd = self.take(T, F32)
        sq = [self.take(T) for _ in range(2)]
        sg = [self.take(TG, F32) for _ in range(2)]
        Rwg = [self.R("wg%d" % i) for i in range(3)]
        Rwd = [self.R("wd%d" % i) for i in range(3)]
        Rsg = [self.R("sg%d" % i) for i in range(2)]
        Rh, Ra = self.R("hT"), self.R("actA")
        self.norm_to_hT(hT, vcol, sq, rstd, sd)
        d_wgu = self.d_wgu
        d_wd = self.d_wd
        step = 0
        wi = 0
        di = 0
        for half in range(2):
            for j in range(NJH):
                jj = half * NJH + j
                b = wi % 3
                wi += 1
                self.dma("pool", wg[b].rearrange("p c m -> p (c m)"), d_wgu[fi, jj], [], [Rwg[b]])
                for g in range(NG):
                    sl = slice(g * TG, (g + 1) * TG)
                    pb = (step % 2) * 2
                    step += 1
                    for c in range(8):
                        self.mm(self.bank[pb][:, 0:TG], wg[b][:, c, 0:128], hT[:, c, sl], c == 0, c == 7,
                                [Rwg[b], Rh], [self.bres[pb]])
                    for c in range(8):
                        self.mm(self.bank[pb + 1][:, 0:TG], wg[b][:, c, 128:256], hT[:, c, sl], c == 0, c == 7,
                                [Rwg[b], Rh], [self.bres[pb + 1]])
                    s = sg[g % 2]
                    self.act(s, self.bank[pb][:, 0:TG], AF.Silu, [self.bres[pb]], [Rsg[g % 2]])
                    self.tt("dve", actA[:, j, sl], s, self.bank[pb + 1][:, 0:TG], ALU.mult,
                            [Rsg[g % 2], self.bres[pb + 1]], [Ra])
            for dc in range(8):
                b = di % 3
                di += 1
                self.dma("pool", wdb[b].rearrange("p j m -> p (j m)"), d_wd[fi, half, dc], [], [Rwd[b]])
                for g in range(NG):
                    sl = slice(g * TG, (g + 1) * TG)
                    pb = 4 + (step % 2)
                    step += 1
                    for j in range(NJH):
                        self.mm(self.bank[pb][:, 0:TG], wdb[b][:, j, :], actA[:, j, sl], j == 0, j == NJH - 1,
                                [Rwd[b], Ra], [self.bres[pb]])
                    self.stt("dve", self.xres[:, dc, sl], self.bank[pb][:, 0:TG], 0.5, self.xres[:, dc, sl],
                             ALU.mult, ALU.add, [self.bres[pb], self.Rx[dc][g]], [self.Rx[dc][g]])

    def nr_alloc(self):
        TG = self.cfg.TG
        self.nr = []
        for k in range(2):
            self.nr.append(dict(qf=self.take(TG, F32), sd=self.take(TG, F32), t2=self.take(TG, F32),
                                sqb=self.take(TG), qnb=self.take(TG)))
        self.nrk = 0

    def normrope(self, ps, Rps, R, sl, onesidx, nnorm, gcol, rmidx, cosT, sinT, Rtab, out, Rout):
        TG = self.cfg.TG
        k = self.nrk % 2
        self.nrk += 1
        t = self.nr[k]
        Rn = lambda n: self.R("nr_%s%d" % (n, k))
        bs = 2 + k
        br = 4 + k
        qf, sd, t2, sqb, qnb = (t["qf"][0:R], t["sd"][0:R], t["t2"][0:R], t["sqb"][0:R], t["qnb"][0:R])
        self.cp("act", qf, ps, [Rps], [Rn("qf")])
        self.act(sqb, ps, AF.Square, [Rps], [Rn("sqb")])
        self.mm(self.bank[bs][0:R, 0:TG], self.cm(onesidx, R, R), sqb, True, True,
                [Rn("sqb"), self.R("cmat")], [self.bres[bs]])
        self.act(sd, self.bank[bs][0:R, 0:TG], AF.Sqrt, [self.bres[bs], self.R("epsc")], [Rn("sd")],
                 scale=1.0 / nnorm, bias=self.epsc[0:R, :])
        self.recip(sd, sd, [Rn("sd")], [Rn("sd")])
        self.stt("dve", qf, qf, self.vecs[0:R, gcol:gcol + 1], sd, ALU.mult, ALU.mult,
                 [Rn("qf"), Rn("sd"), self.R("vecs")], [Rn("qf")])
        self.cp("pool", qnb, qf, [Rn("qf")], [Rn("qnb")])
        self.mm(self.bank[br][0:R, 0:TG], self.cm(rmidx, R, R), qnb, True, True,
                [Rn("qnb"), self.R("cmat")], [self.bres[br]])
        self.tt("dve", t2, self.bank[br][0:R, 0:TG], sinT[0:R, sl], ALU.mult, [self.bres[br], Rtab], [Rn("t2")])
        self.tt("pool", qf, qf, cosT[0:R, sl], ALU.mult, [Rn("qf"), Rtab], [Rn("qf")])
        self.tt("pool", out, qf, t2, ALU.add, [Rn("qf"), Rn("t2")], [Rout])

    def off_qT(self, odd):
        return self.cfg.arena - (12 if odd else 4) * self.cfg.T

    def off_mixT(self):
        return self.cfg.arena - 12 * self.cfg.T

    def off_otok(self, odd):
        return self.off_mixT() - (8 if odd else 4) * self.cfg.T

    def even_prep(self, vcol, kv_scr):
        cfg = self.cfg
        T, TG, NG, NT = cfg.T, cfg.TG, cfg.NG, cfg.NT
        P = self.P
        P.barrier()
        qT = self.fixed(self.off_qT(False), 4 * T).rearrange("p (c t) -> p c t", c=4)
        mixT = self.fixed(self.off_mixT(), 8 * T).rearrange("p (c t) -> p c t", c=8)
        Rq, Rm_, Rh = self.R("qT"), self.R("mixT"), self.R("hT")
        self.arena_reset(0, self.off_mixT())
        hT = self.take(8 * T).rearrange("p (c t) -> p c t", c=8)
        mark = self.a_lo
        rstd = self.take(T, F32)
        sd = self.take(T, F32)
        sq = [self.take(T) for _ in range(2)]
        self.norm_to_hT(hT, vcol, sq, rstd, sd)
        P.barrier()
        self.arena_reset(mark, self.off_mixT())
        cosT = self.take(T, F32)
        sinT = self.take(T, F32)
        self.nr_alloc()
        wb = [self.take(1024).rearrange("p (c m) -> p c m", c=8) for _ in range(3)]
        kst = [self.take(TG) for _ in range(2)]
        Rwb = [self.R("wb%d" % i) for i in range(3)]
        Rks = [self.R("kst%d" % i) for i in range(2)]
        Rtab = self.R("tab")
        self.kvres = []

        def Rkv_new():
            r = Res("kvw")
            self.kvres.append(r)
            return r
        self.dma("sp", cosT, self.d_cs_e[0], [], [Rtab])
        self.dma("sp", sinT, self.d_cs_e[1], [], [Rtab])
        step = 0
        ks = 0
        for ci in range(12):
            b = ci % 3
            self.dma("pool", wb[b].rearrange("p c m -> p (c m)"), self.d_evw_fm[ci], [], [Rwb[b]])
            kind, h = ("u", "q", "k")[ci // 4], ci % 4
            for g in range(NG):
                sl = slice(g * TG, (g + 1) * TG)
                pb = step % 2
                step += 1
                for c in range(8):
                    self.mm(self.bank[pb][:, 0:TG], wb[b][:, c, :], hT[:, c, sl], c == 0, c == 7,
                            [Rwb[b], Rh], [self.bres[pb]])
                ps = self.bank[pb][:, 0:TG]
                if kind == "u":
                    self.act(mixT[:, h, sl], ps, AF.Gelu_apprx_tanh, [self.bres[pb]], [Rm_])
                elif kind == "q":
                    self.normrope(ps, self.bres[pb], 128, sl, C_BD, 64, V_EQ, C_RME, cosT, sinT, Rtab,
                                  qT[:, h, sl], Rq)
                else:
                    kb = ks % 2
                    ks += 1
                    self.normrope(ps, self.bres[pb], 128, sl, C_BD, 64, V_EK, C_RME, cosT, sinT, Rtab,
                                  kst[kb], Rks[kb])
                    self.dma("sp", kv_scr.loc(h * 128, (h + 1) * 128)[:, sl], kst[kb], [Rks[kb]], [Rkv_new()])
        P.barrier()
        self.arena_reset(mark, self.off_mixT())
        wtm = self.take(4096).rearrange("p (c m) -> p c m", c=8)
        wsT = self.take(1024).rearrange("p (g i) -> p g i", g=8)
        Gt = self.take(512, F32)
        biast = self.take(512, F32)
        vg = [self.take(512, F32) for _ in range(2)]
        sqv = self.take(512, F32)
        ss8 = [self.take(8, F32) for _ in range(2)]
        vc = [self.take(512) for _ in range(2)]
        tmpm = [self.take(512, F32) for _ in range(2)]
        vst = [self.take(512) for _ in range(2)]
        Rwtm, Rws, Rg = self.R("wtm"), self.R("wsT"), self.R("Gt")
        self.dma("pool", wtm.rearrange("p c m -> p (c m)").rearrange("p (a b) -> p a b", b=2048),
                 self.d_evw_tm[0].rearrange("p (a b) -> p a b", b=2048), [], [Rwtm])
        self.dma("pool", wsT.rearrange("p g i -> p (g i)"), self.d_wsT, [], [Rws])
        self.dma("sp", Gt, self.d_rows[0:1, R_SGU:R_SGU + 512].partition_broadcast(128), [], [Rg])
        self.dma("sp", biast, self.d_bias_t, [], [Rg])
        for tt_ in range(NT):
            k = tt_ % 2
            tsl = slice(tt_ * 128, (tt_ + 1) * 128)
            pb = 6 + k
            Rvg, Rss, Rvc, Rtm = (self.R("vg%d" % k), self.R("ss8%d" % k), self.R("vc%d" % k), self.R("tmpm%d" % k))
            for c in range(8):
                self.mm(self.bank[pb][:, :], hT[:, c, tsl], wtm[:, c, :], c == 0, c == 7, [Rh, Rwtm], [self.bres[pb]])
            self.act(vg[k], self.bank[pb][:, :], AF.Gelu_apprx_tanh, [self.bres[pb]], [Rvg])
            self.tt("dve", sqv, vg[k], vg[k], ALU.mult, [Rvg], [self.R("sqv")])
            self.P.op("dve", (lambda o, i: (lambda e: e.reduce_sum(out=o, in_=i, axis=AX.X)))(
                ss8[k], sqv.rearrange("p (g d) -> p g d", g=8)), [self.R("sqv")], [Rss])
            self.act(ss8[k], ss8[k], AF.Sqrt, [Rss, self.R("epsc")], [Rss], scale=1.0 / 64, bias=self.epsc[:])
            self.recip(ss8[k], ss8[k], [Rss], [Rss])
            vg3 = vg[k].rearrange("p (g d) -> p g d", g=8)
            self.tt("dve", vg3, vg3, ss8[k].unsqueeze(2).to_broadcast([128, 8, 64]), ALU.mult, [Rvg, Rss], [Rvg])
            self.tt("pool", vc[k], vg[k], Gt, ALU.mult, [Rvg, Rg], [Rvc])
            bm = 4 + k
            for g8 in range(8):
                po = (g8 % 2) * 64
                self.mm(self.bank[bm][po:po + 64, (g8 // 2) * 128:(g8 // 2) * 128 + 128],
                        vc[k][:, g8 * 64:(g8 + 1) * 64], wsT[:, g8, :], True, True, [Rvc, Rws], [self.bres[bm]])
            self.tt("dve", tmpm[k], self.bank[bm][:, :], biast, ALU.add, [self.bres[bm], Rg], [Rtm])
            self.tt("pool", mixT[:, 0:4, tsl], tmpm[k].rearrange("p (c i) -> p c i", c=4), mixT[:, 0:4, tsl],
                    ALU.mult, [Rtm, Rm_], [Rm_])
        self.dma("pool", wtm.rearrange("p c m -> p (c m)").rearrange("p (a b) -> p a b", b=2048),
                 self.d_evw_tm[1].rearrange("p (a b) -> p a b", b=2048), [], [Rwtm])
        vs = [kv_scr.loc(512 + 256 * s_, 768 + 256 * s_).rearrange("(h p) (t c) -> h p t c", h=2, c=128) for s_ in range(2)]
        for tt_ in range(NT):
            k = tt_ % 2
            tsl = slice(tt_ * 128, (tt_ + 1) * 128)
            pb = 6 + k
            Rvs = self.R("vst%d" % k)
            for c in range(8):
                self.mm(self.bank[pb][:, :], hT[:, c, tsl], wtm[:, c, :], c == 0, c == 7, [Rh, Rwtm], [self.bres[pb]])
            self.cp("act", vst[k], self.bank[pb][:, :], [self.bres[pb]], [Rvs])
            for s_ in range(2):
                self.dma("sp", vs[s_][:, :, tt_, :].rearrange("h p c -> p h c"),
                         vst[k][:, s_ * 256:(s_ + 1) * 256].rearrange("p (h c) -> p h c", h=2), [Rvs], [Rkv_new()])

    def attn_alloc(self, dvp1, n_pt=3):
        cfg = self.cfg
        self.Kb = [self.take(cfg.S) for _ in range(2)]
        self.Vb = [self.take(cfg.KT * dvp1).rearrange("p (k c) -> p k c", c=dvp1) for _ in range(2)]
        self.pT = [self.take(cfg.TG) for _ in range(n_pt)]
        self.RK = [self.R("Kb%d" % i) for i in range(2)]
        self.RV = [self.R("Vb%d" % i) for i in range(2)]
        self.RpT = [self.R("pT%d" % i) for i in range(n_pt)]
        self.pti = 0
        self.sbi = 0
        for i in range(2):
            self.P.op("pool", (lambda v: (lambda e: e.memset(v, 1.0)))(self.Vb[i][:, :, dvp1 - 1:dvp1]),
                      [], [self.RV[i]])

    def attn_map(self, kb, krows, qap, Rq, scale, dvp1, acc_of_j, accres_of_j, first_of_bank, last_of_bank):
        cfg = self.cfg
        TG, KT, TPG = cfg.TG, cfg.KT, cfg.TPG
        for kt in range(KT):
            sb_ = self.sbi % 2
            self.sbi += 1
            self.mm(self.bank[sb_][:, 0:TG], self.Kb[kb][krows, kt * 128:(kt + 1) * 128], qap, True, True,
                    [self.RK[kb], Rq], [self.bres[sb_]])
            pi = self.pti % len(self.pT)
            self.pti += 1
            self.act(self.pT[pi], self.bank[sb_][:, 0:TG], AF.Exp, [self.bres[sb_]], [self.RpT[pi]], scale=scale)
            for j in range(TPG):
                self.mm(acc_of_j(j), self.pT[pi][:, j * 128:(j + 1) * 128], self.Vb[kb][:, kt, 0:dvp1],
                        kt == 0 and first_of_bank(j), kt == KT - 1 and last_of_bank(j),
                        [self.RpT[pi], self.RV[kb]], [accres_of_j(j)])

    def even_attn(self, kvall):
        cfg = self.cfg
        T, TG, NG, NT, KT, TPG = cfg.T, cfg.TG, cfg.NG, cfg.NT, cfg.KT, cfg.TPG
        P = self.P
        P.barrier()
        lam_init = 0.8 - 0.6 * float(np.exp(-0.3 * 0))
        qT = self.fixed(self.off_qT(False), 4 * T).rearrange("p (c t) -> p c t", c=4)
        otok = self.fixed(self.off_otok(False), 4 * T).rearrange("p (t c) -> p t c", c=512)
        Rq, Ro = self.R("qT"), self.R("otok")
        self.arena_reset(0, self.off_otok(False))
        self.attn_alloc(129)
        lamt = self.take(256, F32)
        lp = self.take(128, F32)
        subg = self.take(128, F32)
        sm = self.take(8, F32)
        fin = [dict(r0=self.take(1, F32), r1=self.take(1, F32), O0=self.take(128, F32), od=self.take(128, F32),
                    ss=self.take(1, F32), junk=self.take(128)) for _ in range(2)]
        Rl = self.R("lam")
        self.dma("sp", lamt, self.d_rows[0:1, R_LAM:R_LAM + 256].partition_broadcast(128), [], [Rl])
        self.dma("sp", subg, self.d_rows[0:1, R_SUB:R_SUB + 128].partition_broadcast(128), [], [self.R("subg")])
        self.tt("dve", lp[:, 0:64], lamt[:, 0:64], lamt[:, 64:128], ALU.mult, [Rl], [self.R("lp")])
        self.tt("dve", lp[:, 64:128], lamt[:, 128:192], lamt[:, 192:256], ALU.mult, [Rl], [self.R("lp")])
        self.P.op("dve", lambda e: e.reduce_sum(out=sm[:, 0:2], in_=lp.rearrange("p (a d) -> p a d", a=2), axis=AX.X),
                  [self.R("lp")], [self.R("sm")])
        self.act(sm[:, 2:4], sm[:, 0:2], AF.Exp, [self.R("sm")], [self.R("sm2")])
        self.tt("dve", sm[:, 4:5], sm[:, 3:4], sm[:, 2:3], ALU.subtract, [self.R("sm2")], [self.R("sm3")])
        self.ts("dve", sm[:, 5:6], sm[:, 4:5], -lam_init, ALU.add, [self.R("sm3")], [self.R("neglam")])
        neglam = sm[:, 5:6]
        self.ts("dve", subg, subg, 1.0 - lam_init, ALU.mult, [self.R("subg")], [self.R("subg")])
        accsets = [(2, 3), (4, 5), (6, 7)]
        ai = 0
        fi_ = 0
        for h in range(4):
            kb = h % 2
            for r in range(4):
                self.dma("sp", self.Kb[kb][:, r * T:(r + 1) * T], kvall.gat(r, h * 128, (h + 1) * 128),
                         [self.R("kvall")], [self.RK[kb]])
                self.dma("sp", self.Vb[kb][:, r * NT:(r + 1) * NT, 0:128],
                         kvall.gat(r, 512 + h * 128, 512 + (h + 1) * 128).rearrange("p (t c) -> p t c", c=128),
                         [self.R("kvall")], [self.RV[kb]])
            for g in range(NG):
                sl = slice(g * TG, (g + 1) * TG)
                sets = []
                for m in range(2):
                    bA, bB = accsets[ai % 3]
                    ai += 1
                    sets.append((bA, bB))
                    rows = slice(m * 64, (m + 1) * 64)
                    self.attn_map(kb, rows, qT[rows, h, sl], Rq, 0.125, 129,
                                  lambda j, bA=bA, bB=bB: (self.bank[bA][:, j * 129:(j + 1) * 129] if j < 3
                                                           else self.bank[bB][:, 0:129]),
                                  lambda j, bA=bA, bB=bB: self.bres[bA] if j < 3 else self.bres[bB],
                                  lambda j: j == 0 or j == 3, lambda j: j == min(TPG, 3) - 1 or j == 3)
                for j in range(TPG):
                    f = fin[fi_ % 2]
                    k = fi_ % 2
                    fi_ += 1
                    Rf = lambda n: self.R("fin_%s%d" % (n, k))
                    (a0, r0b), (a1, r1b) = [((self.bank[s[0]][:, j * 129:(j + 1) * 129], self.bres[s[0]]) if j < 3
                                             else (self.bank[s[1]][:, 0:129], self.bres[s[1]])) for s in sets]
                    self.recip(f["r0"], a0[:, 128:129], [r0b], [Rf("r0")])
                    self.recip(f["r1"], a1[:, 128:129], [r1b], [Rf("r1")])
                    self.ts("dve", f["O0"], a0[:, 0:128], f["r0"][:, 0:1], ALU.mult, [r0b, Rf("r0")], [Rf("O0")])
                    self.tt("dve", f["r1"], f["r1"], neglam, ALU.mult, [Rf("r1"), self.R("neglam")], [Rf("r1")])
                    self.stt("dve", f["od"], a1[:, 0:128], f["r1"][:, 0:1], f["O0"], ALU.mult, ALU.add,
                             [r1b, Rf("r1"), Rf("O0")], [Rf("od")])
                    self.act(f["junk"], f["od"], AF.Square, [Rf("od")], [Rf("junk"), Rf("ss")], accum_out=f["ss"])
                    self.act(f["ss"], f["ss"], AF.Sqrt, [Rf("ss"), self.R("epsc")], [Rf("ss")], scale=1.0 / 128,
                             bias=self.epsc[:])
                    self.recip(f["ss"], f["ss"], [Rf("ss")], [Rf("ss")])
                    tt_ = g * TPG + j
                    self.stt("dve", otok[:, tt_, h * 128:(h + 1) * 128], f["od"], f["ss"][:, 0:1], subg,
                             ALU.mult, ALU.mult, [Rf("od"), Rf("ss"), self.R("subg")], [Ro])

    def out_proj(self, odd, d_wo):
        cfg = self.cfg
        T, TG, NG, NT, TPG = cfg.T, cfg.TG, cfg.NG, cfg.NT, cfg.TPG
        P = self.P
        P.barrier()
        ncol = 1024 if odd else 512
        otok = self.fixed(self.off_otok(odd), (8 if odd else 4) * T).rearrange("p (t c) -> p t c", c=ncol)
        mixT = self.fixed(self.off_mixT(), 8 * T).rearrange("p (c t) -> p c t", c=8)
        Ro, Rm_ = self.R("otok"), self.R("mixT")
        self.arena_reset(0, self.off_otok(odd))
        wo = [self.take(1024).rearrange("p (c m) -> p c m", c=8) for _ in range(3)]
        Rwo = [self.R("wo%d" % i) for i in range(3)]
        c0 = 0 if odd else 4
        step = 0
        for c in range(ncol // 128):
            for g in range(NG):
                pb = step % 2
                step += 1
                pst = self.bank[pb].bitcast(BF16)
                for j in range(TPG):
                    tt_ = g * TPG + j
                    self.P.op("pe", (lambda o, i: (lambda e: e.transpose(o, i, self.cm(C_ID))))(
                        pst[:, j * 128:(j + 1) * 128], otok[:, tt_, c * 128:(c + 1) * 128]),
                        [Ro, self.R("cmat")], [self.bres[pb]])
                eng = "act" if step % 2 == 0 else "dve"
                self.cp(eng, mixT[:, c0 + c, g * TG:(g + 1) * TG], pst[:, 0:TG], [self.bres[pb]], [Rm_])
        for dc in range(8):
            b = dc % 3
            self.dma("pool", wo[b].rearrange("p c m -> p (c m)"), d_wo[dc], [], [Rwo[b]])
            for g in range(NG):
                sl = slice(g * TG, (g + 1) * TG)
                pb = 2 + step % 2
                step += 1
                for hc in range(8):
                    self.mm(self.bank[pb][:, 0:TG], wo[b][:, hc, :], mixT[:, hc, sl], hc == 0, hc == 7,
                            [Rwo[b], Rm_], [self.bres[pb]])
                self.tt("dve", self.xres[:, dc, sl], self.bank[pb][:, 0:TG], self.xres[:, dc, sl], ALU.add,
                        [self.bres[pb], self.Rx[dc][g]], [self.Rx[dc][g]])

    def odd_prep(self, vcol, kv_scr):
        cfg = self.cfg
        T, TG, NG, NT, TPG = cfg.T, cfg.TG, cfg.NG, cfg.NT, cfg.TPG
        P = self.P
        P.barrier()
        oq = self.off_qT(True)
        qTm = self.fixed(oq, 8 * T).rearrange("p (c t) -> p c t", c=8)
        qTg = self.fixed(oq + 8 * T, 4 * T).rearrange("p (c t) -> p c t", c=4)
        Rq, Rh = self.R("qT"), self.R("hT")
        self.arena_reset(0, oq)
        hT = self.take(8 * T).rearrange("p (c t) -> p c t", c=8)
        mark = self.a_lo
        rstd = self.take(T, F32)
        sd = self.take(T, F32)
        sq = [self.take(T) for _ in range(2)]
        self.norm_to_hT(hT, vcol, sq, rstd, sd)
        self.kvres = []

        def Rkv_new():
            r = Res("kvw")
            self.kvres.append(r)
            return r
        P.barrier()
        self.arena_reset(mark, oq)
        cosT = self.take(T, F32)
        sinT = self.take(T, F32)
        self.nr_alloc()
        wfm = [self.take(1024).rearrange("p (c m) -> p c m", c=8) for _ in range(3)]
        wkpe = self.take(256).rearrange("p (c m) -> p c m", c=8)
        wuq = self.take(1536).rearrange("p (c m) -> p c m", c=2)
        wkp = self.take(768)
        wvm = self.take(512)
        cqf = [self.take(TG, F32) for _ in range(2)]
        cqs = [self.take(TG) for _ in range(2)]
        csd = self.take(TG, F32)
        cqn = self.take(2 * TG).rearrange("p (c t) -> p c t", c=2)
        ckvn = self.take(TG)
        kpeb = self.take(TG)
        kst = [self.take(TG) for _ in range(2)]
        vst = [self.take(512) for _ in range(2)]
        Rw, Rtab = self.R("odw"), self.R("tab")
        for i in range(3):
            self.dma("pool", wfm[i].rearrange("p c m -> p (c m)"), self.d_odw_fm[i], [], [Rw])
        self.dma("pool", wkpe.rearrange("p c m -> p (c m)"), self.d_odw_kpe, [], [Rw])
        self.dma("pool", wuq.rearrange("p c m -> p (c m)"), self.d_wuq, [], [Rw])
        self.dma("pool", wkp, self.d_wkp, [], [Rw])
        self.dma("pool", wvm, self.d_wvm, [], [Rw])
        self.dma("sp", cosT, self.d_cs_m[0], [], [Rtab])
        self.dma("sp", sinT, self.d_cs_m[1], [], [Rtab])
        vsm = [kv_scr.loc(896 + 256 * s_, 1152 + 256 * s_).rearrange("r (b x) -> (r b) x", b=2).rearrange(
            "(h p) (t c) -> h p t c", h=4, c=64) for s_ in range(2)]
        step = 0
        ks = 0
        vi = 0
        for g in range(NG):
            sl = slice(g * TG, (g + 1) * TG)
            for c2 in range(2):
                for c in range(8):
                    self.mm(self.bank[c2][:, 0:TG], wfm[c2][:, c, :], hT[:, c, sl], c == 0, c == 7, [Rw, Rh],
                            [self.bres[c2]])
                self.cp("act", cqf[c2], self.bank[c2][:, 0:TG], [self.bres[c2]], [self.R("cqf%d" % c2)])
                self.act(cqs[c2], self.bank[c2][:, 0:TG], AF.Square, [self.bres[c2]], [self.R("cqs%d" % c2)])
            for c2 in range(2):
                self.mm(self.bank[6][:, 0:TG], self.cm(C_ONES), cqs[c2], c2 == 0, c2 == 1,
                        [self.R("cqs%d" % c2), self.R("cmat")], [self.bres[6]])
            self.act(csd, self.bank[6][:, 0:TG], AF.Sqrt, [self.bres[6], self.R("epsc")], [self.R("csd")],
                     scale=1.0 / 256, bias=self.epsc[:])
            self.recip(csd, csd, [self.R("csd")], [self.R("csd")])
            for c2 in range(2):
                self.stt("dve", cqn[:, c2, :], cqf[c2], self.vecs[:, V_CQ + c2:V_CQ + c2 + 1], csd, ALU.mult, ALU.mult,
                         [self.R("cqf%d" % c2), self.R("csd"), self.R("vecs")], [self.R("cqn")])
            for c in range(8):
                self.mm(self.bank[7][:, 0:TG], wfm[2][:, c, :], hT[:, c, sl], c == 0, c == 7, [Rw, Rh], [self.bres[7]])
            self.cp("act", cqf[0], self.bank[7][:, 0:TG], [self.bres[7]], [self.R("cqf0")])
            self.act(cqs[0], self.bank[7][:, 0:TG], AF.Square, [self.bres[7]], [self.R("cqs0")])
            self.mm(self.bank[6][:, 0:TG], self.cm(C_ONES), cqs[0], True, True, [self.R("cqs0"), self.R("cmat")],
                    [self.bres[6]])
            self.act(csd, self.bank[6][:, 0:TG], AF.Sqrt, [self.bres[6], self.R("epsc")], [self.R("csd")],
                     scale=1.0 / 128, bias=self.epsc[:])
            self.recip(csd, csd, [self.R("csd")], [self.R("csd")])
            self.stt("dve", ckvn, cqf[0], self.vecs[:, V_CKV:V_CKV + 1], csd, ALU.mult, ALU.mult,
                     [self.R("cqf0"), self.R("csd"), self.R("vecs")], [self.R("ckvn")])
            for c in range(8):
                self.mm(self.bank[7][0:32, 0:TG], wkpe[:, c, :], hT[:, c, sl], c == 0, c == 7, [Rw, Rh], [self.bres[7]])
            self.cp("act", kpeb[0:32], self.bank[7][0:32, 0:TG], [self.bres[7]], [self.R("kpeb")])
            for h in range(8):
                pb = step % 2
                step += 1
                for c2 in range(2):
                    self.mm(self.bank[pb][0:96, 0:TG], wuq[:, c2, h * 96:(h + 1) * 96], cqn[:, c2, :], c2 == 0, c2 == 1,
                            [Rw, self.R("cqn")], [self.bres[pb]])
                self.normrope(self.bank[pb][0:96, 0:TG], self.bres[pb], 96, sl, C_O96, 96, V_MQ, C_RMM, cosT, sinT,
                              Rtab, qTm[0:96, h, sl], Rq)
            for h in range(8):
                pb = step % 2
                step += 1
                self.mm(self.bank[pb][0:96, 0:TG], wkp[:, h * 96:(h + 1) * 96], ckvn, True, False,
                        [Rw, self.R("ckvn")], [self.bres[pb]])
                self.mm(self.bank[pb][0:96, 0:TG], self.cm(C_SEL, 32, 96), kpeb[0:32], False, True,
                        [self.R("cmat"), self.R("kpeb")], [self.bres[pb]])
                kb = ks % 2
                ks += 1
                self.normrope(self.bank[pb][0:96, 0:TG], self.bres[pb], 96, sl, C_O96, 96, V_MK, C_RMM, cosT, sinT,
                              Rtab, kst[kb][0:96], self.R("kst%d" % kb))
                self.dma("sp", kv_scr.loc(h * 96, (h + 1) * 96)[:, sl], kst[kb][0:96], [self.R("kst%d" % kb)], [Rkv_new()])
            for j in range(TPG):
                tt_ = g * TPG + j
                k = vi % 2
                vi += 1
                self.mm(self.bank[6][:, :], ckvn[:, j * 128:(j + 1) * 128], wvm, True, True, [self.R("ckvn"), Rw],
                        [self.bres[6]])
                self.cp("act", vst[k], self.bank[6][:, :], [self.bres[6]], [self.R("vst%d" % k)])
                for s_ in range(2):
                    self.dma("sp", vsm[s_][:, :, tt_, :].rearrange("h p c -> p h c"),
                             vst[k][:, s_ * 256:(s_ + 1) * 256].rearrange("p (h c) -> p h c", h=4),
                             [self.R("vst%d" % k)], [Rkv_new()])
        P.barrier()
        self.arena_reset(mark, oq)
        cosT = self.take(T, F32)
        sinT = self.take(T, F32)
        self.nr_alloc()
        wb = [self.take(1024).rearrange("p (c m) -> p c m", c=8) for _ in range(3)]
        wgv = self.take(1024).rearrange("p (c m) -> p c m", c=8)
        kst = [self.take(TG) for _ in range(2)]
        vst = [self.take(128) for _ in range(2)]
        Rwb = [self.R("wb%d" % i) for i in range(3)]
        self.dma("sp", cosT, self.d_cs_g[0], [], [Rtab])
        self.dma("sp", sinT, self.d_cs_g[1], [], [Rtab])
        self.dma("pool", wgv.rearrange("p c m -> p (c m)"), self.d_odw_gv, [], [self.R("wgv")])
        for ci in range(5):
            b = ci % 3
            self.dma("pool", wb[b].rearrange("p c m -> p (c m)"), self.d_odw_fm[3 + ci], [], [Rwb[b]])
            for g in range(NG):
                sl = slice(g * TG, (g + 1) * TG)
                pb = step % 2
                step += 1
                for c in range(8):
                    self.mm(self.bank[pb][:, 0:TG], wb[b][:, c, :], hT[:, c, sl], c == 0, c == 7, [Rwb[b], Rh],
                            [self.bres[pb]])
                if ci < 4:
                    self.normrope(self.bank[pb][:, 0:TG], self.bres[pb], 128, sl, C_BD, 64, V_GQ, C_RMG, cosT, sinT,
                                  Rtab, qTg[:, ci, sl], Rq)
                else:
                    kb = ks % 2
                    ks += 1
                    self.normrope(self.bank[pb][:, 0:TG], self.bres[pb], 128, sl, C_BD, 64, V_GK, C_RMG, cosT, sinT,
                                  Rtab, kst[kb], self.R("kst%d" % kb))
                    self.dma("sp", kv_scr.loc(768, 896)[:, sl], kst[kb], [self.R("kst%d" % kb)], [Rkv_new()])
        vsg = kv_scr.loc(1408, 1536).rearrange("r (b x) -> (r b) x", b=2).rearrange("(h p) (t c) -> h p t c", h=2, c=64)
        for tt_ in range(NT):
            k = tt_ % 2
            tsl = slice(tt_ * 128, (tt_ + 1) * 128)
            pb = 6 + k
            for c in range(8):
                self.mm(self.bank[pb][:, 0:128], hT[:, c, tsl], wgv[:, c, :], c == 0, c == 7, [Rh, self.R("wgv")],
                        [self.bres[pb]])
            self.cp("act", vst[k], self.bank[pb][:, 0:128], [self.bres[pb]], [self.R("vstg%d" % k)])
            self.dma("sp", vsg[:, :, tt_, :].rearrange("h p c -> p h c"), vst[k].rearrange("p (h c) -> p h c", h=2),
                     [self.R("vstg%d" % k)], [Rkv_new()])

    def odd_attn(self, kvall):
        cfg = self.cfg
        T, TG, NG, NT, KT, TPG = cfg.T, cfg.TG, cfg.NG, cfg.NT, cfg.KT, cfg.TPG
        P = self.P
        P.barrier()
        oq = self.off_qT(True)
        qTm = self.fixed(oq, 8 * T).rearrange("p (c t) -> p c t", c=8)
        qTg = self.fixed(oq + 8 * T, 4 * T).rearrange("p (c t) -> p c t", c=4)
        otok = self.fixed(self.off_otok(True), 8 * T).rearrange("p (t c) -> p t c", c=1024)
        Rq, Ro = self.R("qT"), self.R("otok")
        self.arena_reset(0, self.off_otok(True))
        self.attn_alloc(65)
        r4 = [self.take(4, F32) for _ in range(2)]
        NR = 1536
        accb = [2, 3, 4, 5, 6, 7]
        ai = 0

        def run_map(kb, krows, qap, scale, col0):
            nonlocal ai
            for g in range(NG):
                sl = slice(g * TG, (g + 1) * TG)
                a = accb[ai % 6]
                k = ai % 2
                ai += 1
                self.attn_map(kb, krows, qap(sl), Rq, scale, 65,
                              lambda j, a=a: self.bank[a][:, j * 65:(j + 1) * 65],
                              lambda j, a=a: self.bres[a], lambda j: j == 0, lambda j: j == TPG - 1)
                acc3 = self.bank[a][:, 0:TPG * 65].rearrange("p (j c) -> p j c", c=65)
                self.recip(r4[k][:, 0:TPG], acc3[:, :, 64], [self.bres[a]], [self.R("r4%d" % k)])
                self.tt("dve", otok[:, g * TPG:(g + 1) * TPG, col0:col0 + 64], acc3[:, :, 0:64],
                        r4[k][:, 0:TPG].unsqueeze(2).to_broadcast([128, TPG, 64]), ALU.mult,
                        [self.bres[a], self.R("r4%d" % k)], [Ro])

        for u in range(10):
            kb = u % 2
            if u < 8:
                h = u
                for r in range(4):
                    self.dma("sp", self.Kb[kb][0:96, r * T:(r + 1) * T], kvall.gat(r, h * 96, (h + 1) * 96),
                             [self.R("kvall")], [self.RK[kb]])
                    vseg = 896 + 256 * (h // 4)
                    vsrc = kvall.gat(r, vseg, vseg + 256).rearrange("r (b x) -> (r b) x", b=2)[(h % 4) * 128:(h % 4 + 1) * 128, :]
                    self.dma("sp", self.Vb[kb][:, r * NT:(r + 1) * NT, 0:64], vsrc.rearrange("p (t c) -> p t c", c=64),
                             [self.R("kvall")], [self.RV[kb]])
                run_map(kb, slice(0, 96), lambda sl, h=h: qTm[0:96, h, sl], 96 ** -0.5, h * 64)
            else:
                hk = u - 8
                for r in range(4):
                    for dup in range(2):
                        self.dma("sp", self.Kb[kb][dup * 64:(dup + 1) * 64, r * T:(r + 1) * T],
                                 kvall.gat(r, 768 + hk * 64, 768 + (hk + 1) * 64),
                                 [self.R("kvall")], [self.RK[kb]])
                    vsrc = kvall.gat(r, 1408, 1536).rearrange("r (b x) -> (r b) x", b=2)[hk * 128:(hk + 1) * 128, :]
                    self.dma("sp", self.Vb[kb][:, r * NT:(r + 1) * NT, 0:64], vsrc.rearrange("p (t c) -> p t c", c=64),
                             [self.R("kvall")], [self.RV[kb]])
                for g4 in range(4):
                    qh = hk * 4 + g4
                    rows = slice((qh % 2) * 64, (qh % 2) * 64 + 64)
                    run_map(kb, rows, lambda sl, rows=rows, qh=qh: qTg[rows, qh // 2, sl], 0.125, 512 + qh * 64)

    def build(self):
        cfg = self.cfg
        T = cfg.T
        stages = self.stages
        self.setup()
        self.d_wgu = self.inp("wgu", [4, NJ, 128, 2048])
        self.d_wd = self.inp("wd", [4, 2, 8, 128, NJH * 128])
        xv = lambda ap: ap.rearrange("(c p) t -> p c t", p=128)
        allx = [r for row in self.Rx for r in row]
        if 1 in stages:
            d_x = self.inp("xT", [D, T])
            self.d_evw_fm = self.inp("evw_fm", [12, 128, 1024])
            self.d_evw_tm = self.inp("evw_tm", [2, 128, 4096])
            self.d_wsT = self.inp("wsT", [128, 1024])
            self.d_bias_t = self.inp("bias_t", [128, 512])
            self.d_cs_e = self.inp("cs_e", [2, 128, T])
            for c in range(8):
                self.dma("sp", self.xres[:, c, :], d_x[c * 128:(c + 1) * 128, :], [], self.Rx[c])
            self.ffn(0, V_FFN + 0)
            kv_e = self.make_kv("e", [256] * 4 if self.fused else [1024])
            self.even_prep(V_FFN + 8, kv_e)
        if 2 in stages:
            self.d_evwo = self.inp("evwo", [8, 128, 1024])
            self.d_odw_fm = self.inp("odw_fm", [8, 128, 1024])
            self.d_odw_kpe = self.inp("odw_kpe", [128, 256])
            self.d_odw_gv = self.inp("odw_gv", [128, 1024])
            self.d_wuq = self.inp("wuq", [128, 1536])
            self.d_wkp = self.inp("wkp", [128, 768])
            self.d_wvm = self.inp("wvm", [128, 512])
            self.d_cs_m = self.inp("cs_m", [2, 128, T])
            self.d_cs_g = self.inp("cs_g", [2, 128, T])
            if self.fused:
                kvall_e = self.gather(kv_e)
            else:
                kvall_e = KVScr([1024])
                kvall_e.gat_t[0] = self.inp("kvall_e", [4 * 1024, T], BF16)
                self.P.barrier()
                d_xs = self.inp("xs_in", [D, T])
                d_q = self.inp("q_in", [512, T], BF16)
                d_m = self.inp("m_in", [512, T], BF16)
                for c in range(8):
                    self.dma("sp", self.xres[:, c, :], d_xs[c * 128:(c + 1) * 128, :], [], self.Rx[c])
                qT = self.fixed(self.off_qT(False), 4 * T).rearrange("p (c t) -> p c t", c=4)
                mixT = self.fixed(self.off_mixT(), 8 * T).rearrange("p (c t) -> p c t", c=8)
                self.dma("sp", qT, xv(d_q), [], [self.R("qT")])
                self.dma("sp", mixT[:, 0:4, :], xv(d_m), [], [self.R("mixT")])
            self.even_attn(kvall_e)
            self.out_proj(False, self.d_evwo)
            self.ffn(1, V_FFN + 16)
            self.ffn(2, V_FFN + 24)
            kv_o = self.make_kv("o", [192] * 4 + [128] + [256] * 2 + [128] if self.fused else [1536])
            self.odd_prep(V_FFN + 32, kv_o)
        if 3 in stages:
            self.d_odwo = self.inp("odwo", [8, 128, 1024])
            if self.fused:
                kvall_o = self.gather(kv_o)
            else:
                kvall_o = KVScr([1536])
                kvall_o.gat_t[0] = self.inp("kvall_o", [4 * 1536, T], BF16)
                self.P.barrier()
                d_xs = self.inp("xs_in", [D, T])
                d_q = self.inp("qo_in", [12 * 128, T], BF16)
                for c in range(8):
                    self.dma("sp", self.xres[:, c, :], d_xs[c * 128:(c + 1) * 128, :], [], self.Rx[c])
                qTo = self.fixed(self.off_qT(True), 12 * T).rearrange("p (c t) -> p c t", c=12)
                self.dma("sp", qTo, xv(d_q), [], [self.R("qT")])
            self.odd_attn(kvall_o)
            self.out_proj(True, self.d_odwo)
            self.ffn(3, V_FFN + 40)
        self.P.barrier()
        last = max(stages)
        if last == 3:
            d_o = self.outp("outT", [D, T])
            for c in range(8):
                self.dma("sp", d_o[c * 128:(c + 1) * 128, :], self.xres[:, c, :], self.Rx[c], [Res()])
        else:
            d_o = self.outp("xs_out", [D, T])
            for c in range(8):
                self.dma("sp", d_o[c * 128:(c + 1) * 128, :], self.xres[:, c, :], self.Rx[c], [Res()])
            if last == 1:
                qT = self.fixed(self.off_qT(False), 4 * T).rearrange("p (c t) -> p c t", c=4)
                mixT = self.fixed(self.off_mixT(), 8 * T).rearrange("p (c t) -> p c t", c=8)
                self.dma("sp", xv(self.outp("q_out", [512, T], BF16)), qT, [self.R("qT")], [Res()])
                self.dma("sp", xv(self.outp("m_out", [512, T], BF16)), mixT[:, 0:4, :], [self.R("mixT")], [Res()])
            else:
                qTo = self.fixed(self.off_qT(True), 12 * T).rearrange("p (c t) -> p c t", c=12)
                self.dma("sp", xv(self.outp("qo_out", [12 * 128, T], BF16)), qTo, [self.R("qT")], [Res()])
        self.P.emit()
        self.st.close()
        return self.nc

    def make_kv(self, tag, segs):
        T = self.cfg.T
        kv = KVScr(segs)
        for i, n in enumerate(segs):
            if self.fused:
                kv.loc_t[i] = self.nc.dram_tensor("kv%s%d" % (tag, i), [n, T], BF16).ap()
                kv.gat_t[i] = self.nc.dram_tensor("kvall%s%d" % (tag, i), [4 * n, T], BF16).ap()
            else:
                kv.loc_t[i] = self.outp("kv_" + tag, [n, T], BF16)
        return kv

    def gather(self, kv):
        for i in range(len(kv.segs)):
            o = self.P.op("pool", (lambda a, b_: (lambda e: e.collective_compute(
                "AllGather", ALU.bypass, replica_groups=[[0, 1, 2, 3], [4, 5, 6, 7]],
                ins=[a.opt()], outs=[b_.opt()])))(kv.loc_t[i], kv.gat_t[i]),
                list(self.kvres), [self.R("kvall")], dma=True, inc=1)
            o.cc = True
        return kv


def prep_shared(inp):
    f = lambda a: np.ascontiguousarray(np.asarray(a, dtype=np.float32))
    g = {k: f(v) for k, v in inp.items() if k != "x"}
    out = {}
    wgu = []
    wd = []
    for l in range(2):
        for nm in ("ffn1", "ffn2"):
            w = g[nm + "_w_gu"][l]
            wgu.append(w.reshape(8, 128, 2, NJ, 128).transpose(3, 1, 0, 2, 4).reshape(NJ, 128, 2048))
            w = g[nm + "_w_down"][l]
            wd.append(w.reshape(2, NJH, 128, 8, 128).transpose(0, 3, 2, 1, 4).reshape(2, 8, 128, NJH * 128))
    out["wgu"] = np.ascontiguousarray(np.stack(wgu))
    out["wd"] = np.ascontiguousarray(np.stack(wd))
    w = g["ev_w_in"][0]
    cols = [np.arange(i * 128, (i + 1) * 128) for i in range(4)]
    cols += [np.arange(1024 + h * 128, 1024 + (h + 1) * 128) for h in range(4)]
    cols += [np.arange(1536 + h * 128, 1536 + (h + 1) * 128) for h in range(4)]
    out["evw_fm"] = np.stack([_chunk_fm(w, c) for c in cols])
    out["evw_tm"] = np.stack([_chunk_fm(w, np.arange(512, 1024)), _chunk_fm(w, np.arange(2048, 2560))])
    out["evwo"] = np.stack([_chunk_fm(g["ev_w_out"][0], np.arange(dc * 128, (dc + 1) * 128)) for dc in range(8)])
    out["wsT"] = np.ascontiguousarray(g["ev_w_s"][0].transpose(2, 0, 1).reshape(128, 1024))
    bs = g["ev_b_s"][0]
    bt = np.zeros((128, 4, 128), np.float32)
    for cg in range(4):
        bt[:64, cg, :] = bs[2 * cg][None, :]
        bt[64:, cg, :] = bs[2 * cg + 1][None, :]
    out["bias_t"] = bt.reshape(128, 512)
    w = g["od_w_in"][0]
    cols = [np.arange(0, 128), np.arange(128, 256), np.arange(256, 384)]
    cols += [np.arange(416 + i * 128, 416 + (i + 1) * 128) for i in range(4)]
    cols += [np.arange(928, 1056)]
    out["odw_fm"] = np.stack([_chunk_fm(w, c) for c in cols])
    out["odw_kpe"] = _chunk_fm(w, np.arange(384, 416))
    out["odw_gv"] = _chunk_fm(w, np.arange(1056, 1184))
    out["wuq"] = np.ascontiguousarray(g["od_w_uq"][0].reshape(2, 128, 768).transpose(1, 0, 2).reshape(128, 1536))
    wukv = g["od_w_ukv"][0]
    wkp = np.zeros((128, 8, 96), np.float32)
    wvm = np.zeros((128, 8, 64), np.float32)
    for h in range(8):
        wkp[:, h, :64] = wukv[:, h * 128: h * 128 + 64]
        wvm[:, h, :] = wukv[:, h * 128 + 64: h * 128 + 128]
    out["wkp"] = wkp.reshape(128, 768)
    out["wvm"] = wvm.reshape(128, 512)
    out["odwo"] = np.stack([_chunk_fm(g["od_w_out"][0], np.arange(dc * 128, (dc + 1) * 128)) for dc in range(8)])
    vecs = np.zeros((128, NVEC), np.float32)
    norms = [g["ffn1_norm"][0], g["ev_norm"][0], g["ffn2_norm"][0], g["ffn1_norm"][1], g["od_norm"][0], g["ffn2_norm"][1]]
    for i, nv in enumerate(norms):
        vecs[:, V_FFN + 8 * i: V_FFN + 8 * i + 8] = nv.reshape(8, 128).T
    vecs[:, V_EQ] = np.tile(g["ev_q_norm"][0], 2)
    vecs[:, V_EK] = np.tile(g["ev_k_norm"][0], 2)
    vecs[:, V_CQ:V_CQ + 2] = g["od_cq_norm"][0].reshape(2, 128).T
    vecs[:, V_CKV] = g["od_ckv_norm"][0]
    vecs[:96, V_MQ] = g["od_mla_q_norm"][0]
    vecs[:96, V_MK] = g["od_mla_k_norm"][0]
    vecs[:, V_GQ] = np.tile(g["od_gqa_q_norm"][0], 2)
    vecs[:, V_GK] = np.tile(g["od_gqa_k_norm"][0], 2)
    out["vecs"] = vecs
    rows = np.zeros((1, NROW), np.float32)
    rows[0, R_SGU:R_SGU + 512] = g["ev_sgu_norm"][0].reshape(512)
    rows[0, R_SUB:R_SUB + 128] = g["ev_sub_norm"][0]
    rows[0, R_LAM:R_LAM + 256] = np.concatenate([g["ev_lam_q1"][0], g["ev_lam_k1"][0], g["ev_lam_q2"][0], g["ev_lam_k2"][0]])
    out["rows"] = rows
    return out


_NP = {F32: np.float32, BF16: ml_dtypes.bfloat16}


def run_stage(cfg, stages, fused, shared, percore, runner=None):
    b = Builder(cfg, stages, fused)
    nc = b.build()
    in_maps = []
    for c in range(8):
        m = {}
        for name, (shape, dt) in b.din.items():
            a = percore[c][name] if name in percore[c] else shared[name]
            assert tuple(a.shape) == tuple(shape), (name, a.shape, shape)
            m[name] = np.ascontiguousarray(a, dtype=_NP[dt])
        in_maps.append(m)
    if runner is None:
        res = run_bass_kernel_spmd(nc, in_maps, core_ids=list(range(8)))
        return res.results
    return runner(nc, in_maps, {k: (s, _NP[d]) for k, (s, d) in b.dout.items()})


def run_model(inp, T=2048, TG=512, fused=False, runner=None):
    cfg = Cfg(T, TG)
    x = np.asarray(inp["x"], dtype=np.float32)
    B, S, _ = x.shape
    assert B == 2 and S == 4 * T
    shared = prep_shared(inp)
    xt = x.reshape(8, T, D)
    percore = []
    for c in range(8):
        cmat, cs_e, cs_m, cs_g = host_consts(T, S, c)
        percore.append({"xT": np.ascontiguousarray(xt[c].T), "cmat": cmat, "cs_e": cs_e, "cs_m": cs_m, "cs_g": cs_g})
    if fused:
        res = run_stage(cfg, (1, 2, 3), True, shared, percore, runner)
    else:
        def regroup(res, key):
            full = [np.concatenate([res[g * 4 + r][key] for r in range(4)], axis=0) for g in range(2)]
            return [full[c // 4] for c in range(8)]
        r1 = run_stage(cfg, (1,), False, shared, percore, runner)
        kva = regroup(r1, "kv_e")
        for c in range(8):
            percore[c].update({"kvall_e": kva[c], "xs_in": r1[c]["xs_out"], "q_in": r1[c]["q_out"], "m_in": r1[c]["m_out"]})
        r2 = run_stage(cfg, (2,), False, shared, percore, runner)
        kva = regroup(r2, "kv_o")
        for c in range(8):
            percore[c].update({"kvall_o": kva[c], "xs_in": r2[c]["xs_out"], "qo_in": r2[c]["qo_out"]})
        res = run_stage(cfg, (3,), False, shared, percore, runner)
    out = np.stack([np.asarray(res[c]["outT"], dtype=np.float32).T for c in range(8)])
    return np.ascontiguousarray(out.reshape(B, S, D))


def kernel(**inputs):
    return run_model(inputs, T=2048, TG=512, fused=True)
```

```python
import contextlib
import numpy as np
import ml_dtypes
import concourse.bass as bass
import concourse.mybir as mybir
from concourse.bass_utils import run_bass_kernel_spmd

F32 = mybir.dt.float32
BF16 = mybir.dt.bfloat16
AF = mybir.ActivationFunctionType
ALU = mybir.AluOpType
AX = mybir.AxisListType

D = 1024
DFF = 2816
NJ = 22
NJH = 11
EPS = 1e-6
THETA = 10000.0
GRID_W = 64
ENGS = ("pe", "act", "dve", "pool", "sp")


class Res:
    __slots__ = ("name", "w", "rs")

    def __init__(self, name=""):
        self.name = name
        self.w = None
        self.rs = []


class Op:
    __slots__ = ("eng", "fn", "deps", "pos", "dma", "signal", "sem", "target", "prev_target", "inc", "cc")

    def __init__(self, eng, fn, dma, inc):
        self.eng = eng
        self.fn = fn
        self.dma = dma
        self.deps = []
        self.pos = -1
        self.signal = False
        self.sem = None
        self.target = 0
        self.prev_target = 0
        self.inc = inc
        self.cc = False


class Prog:
    NDMA_SEMS = 8

    def __init__(self, nc):
        self.nc = nc
        self.ops = {e: [] for e in ENGS}
        self.pending_barrier = {e: [] for e in ENGS}

    def op(self, eng, fn, reads=(), writes=(), dma=False, inc=16):
        o = Op(eng, fn, dma, inc)
        o.pos = len(self.ops[eng])
        deps = set(self.pending_barrier[eng])
        self.pending_barrier[eng] = []
        rawset = set()
        for r in reads:
            if r.w is not None:
                deps.add(r.w)
                rawset.add(r.w)
        for w in writes:
            if w.w is not None:
                deps.add(w.w)
            for rd in w.rs:
                deps.add(rd)
        best = {}
        red = []
        for d in deps:
            if d.dma:
                red.append(d)
            elif d.eng not in best or best[d.eng].pos < d.pos:
                best[d.eng] = d
        red.extend(best.values())
        for d in red:
            if d is o:
                continue
            if d.dma or o.dma or d.eng != eng:
                o.deps.append(d)
                d.signal = True
            elif eng != "pe" and (o.pos - d.pos) <= 3 and d in rawset:
                o.deps.append(d)
                d.signal = True
        for r in reads:
            r.rs.append(o)
        for w in writes:
            w.w = o
            w.rs = []
        self.ops[eng].append(o)
        return o

    def barrier(self):
        lasts = []
        for e in ENGS:
            comp = [o for o in self.ops[e] if not o.dma]
            if comp:
                lasts.append(comp[-1])
            dmas = [o for o in self.ops[e] if o.dma and not o.cc]
            lasts.extend(dmas[-self.NDMA_SEMS:])
            lasts.extend([o for o in self.ops[e] if o.cc])
        for e in ENGS:
            self.pending_barrier[e] = list(lasts)

    def emit(self):
        nc = self.nc
        with contextlib.ExitStack() as st:
            esem = {e: st.enter_context(nc.semaphore("s_" + e)) for e in ENGS}
            dsem = {}
            for q in ENGS:
                nd = sum(1 for o in self.ops[q] if o.dma and not o.cc)
                if nd:
                    dsem[q] = [st.enter_context(nc.semaphore("d_%s%d" % (q, i)))
                               for i in range(min(nd, self.NDMA_SEMS))]
            for e in ENGS:
                c = 0
                rr = 0
                nsem = len(dsem.get(e, []))
                tot = [0] * max(nsem, 1)
                for o in self.ops[e]:
                    if o.cc:
                        o.sem = st.enter_context(nc.semaphore("cc_%d" % o.pos))
                        o.prev_target = 0
                        o.target = o.inc
                    elif o.dma:
                        o.sem = dsem[e][rr]
                        o.prev_target = tot[rr]
                        tot[rr] += o.inc
                        o.target = tot[rr]
                        rr = (rr + 1) % nsem
                    elif o.signal:
                        c += 1
                        o.sem = esem[e]
                        o.target = c
            block = st.enter_context(nc.Block())

            def run(e, eng):
                seen = {}
                for o in self.ops[e]:
                    waits = {}
                    for d in o.deps:
                        key = id(d.sem)
                        if seen.get(key, 0) >= d.target:
                            continue
                        if key not in waits or waits[key][1] < d.target:
                            waits[key] = (d.sem, d.target)
                    if o.dma and o.prev_target > 0:
                        key = id(o.sem)
                        if seen.get(key, 0) < o.prev_target:
                            if key not in waits or waits[key][1] < o.prev_target:
                                waits[key] = (o.sem, o.prev_target)
                    for key, (s, v) in waits.items():
                        eng.wait_ge(s, v)
                        seen[key] = v
                    ins = o.fn(eng)
                    if o.dma:
                        ins.then_inc(o.sem, o.inc)
                    elif o.signal:
                        ins.then_inc(o.sem, 1)
                if e in dsem or any(o.cc for o in self.ops[e]):
                    tot = {}
                    for o in self.ops[e]:
                        if o.dma:
                            tot[id(o.sem)] = (o.sem, o.target)
                    for key, (s, v) in tot.items():
                        if seen.get(key, 0) < v:
                            eng.wait_ge(s, v)

            block.tensor(lambda eng: run("pe", eng))
            block.scalar(lambda eng: run("act", eng))
            block.vector(lambda eng: run("dve", eng))
            block.gpsimd(lambda eng: run("pool", eng))
            block.sync(lambda eng: run("sp", eng))


def _chunk_fm(w, cols):
    sub = w[:, cols]
    m = sub.shape[1]
    return np.ascontiguousarray(sub.reshape(8, 128, m).transpose(1, 0, 2).reshape(128, 8 * m))


def _rope_inv(dim):
    return (np.float32(THETA) ** (-np.arange(0, dim, 2, dtype=np.float32) / np.float32(dim))).astype(np.float32)


def host_consts(T, S, core):
    r = core % 4
    pos = (r * T + np.arange(T)).astype(np.int32)
    eye = np.eye(128, dtype=np.float32)
    ones = np.ones((128, 128), np.float32)
    bd = np.zeros((128, 128), np.float32)
    bd[:64, :64] = 1
    bd[64:, 64:] = 1
    ones96 = np.zeros((128, 128), np.float32)
    ones96[:96, :96] = 1

    def rot(blocks):
        m = np.zeros((128, 128), np.float32)
        for (b0, half) in blocks:
            for d in range(half):
                m[b0 + d + half, b0 + d] = -1.0
                m[b0 + d, b0 + d + half] = 1.0
        return m
    rm_e = rot([(0, 32), (64, 32)])
    rm_mla = rot([(64, 16)])
    rm_gqa = rot([(0, 16), (32, 16), (64, 16), (96, 16)])
    sel = np.zeros((128, 128), np.float32)
    for i in range(32):
        sel[i, 64 + i] = 1.0
    cmat = np.concatenate([eye, ones, bd, ones96, rm_e, rm_mla, rm_gqa, sel], axis=1)

    def angles(p, dim):
        inv = _rope_inv(dim)
        return p.astype(np.float32)[:, None] * inv[None, :]
    a = angles(pos, 64)
    idx = (np.arange(128) % 64) % 32
    cs_e = np.stack([np.cos(a)[:, idx].T, np.sin(a)[:, idx].T]).astype(np.float32)
    a = angles(pos, 32)
    cs_m = np.zeros((2, 128, T), np.float32)
    cs_m[0, :64] = 1.0
    idx = (np.arange(32)) % 16
    cs_m[0, 64:96] = np.cos(a)[:, idx].T
    cs_m[1, 64:96] = np.sin(a)[:, idx].T
    ar = angles(pos // GRID_W, 32)
    ac = angles(pos % GRID_W, 32)
    cs_g = np.zeros((2, 128, T), np.float32)
    for p in range(128):
        d = p % 64
        src = ar if d < 32 else ac
        f = (d % 32) % 16
        cs_g[0, p] = np.cos(src)[:, f]
        cs_g[1, p] = np.sin(src)[:, f]
    return cmat, cs_e, cs_m, cs_g


class Cfg:
    def __init__(self, T, TG):
        self.T = T
        self.S = 4 * T
        self.TG = TG
        self.NT = T // 128
        self.NG = T // TG
        self.TPG = TG // 128
        self.KT = self.S // 128
        self.arena = max(34 * T + 2048, 44000)


V_FFN = 0
V_EQ = 48
V_EK = 49
V_CQ = 50
V_CKV = 52
V_MQ = 53
V_MK = 54
V_GQ = 55
V_GK = 56
NVEC = 57
R_SGU = 0
R_SUB = 512
R_LAM = 640
NROW = 896
C_ID, C_ONES, C_BD, C_O96, C_RME, C_RMM, C_RMG, C_SEL = range(8)


class KVScr:
    def __init__(self, segs):
        self.segs = list(segs)
        self.b0 = [0]
        for n in self.segs:
            self.b0.append(self.b0[-1] + n)
        self.loc_t = [None] * len(self.segs)
        self.gat_t = [None] * len(self.segs)

    def _find(self, a, b):
        for i, n in enumerate(self.segs):
            if self.b0[i] <= a and b <= self.b0[i + 1]:
                return i
        raise AssertionError(("kv rows cross a segment", a, b))

    def loc(self, a, b):
        i = self._find(a, b)
        return self.loc_t[i][a - self.b0[i]: b - self.b0[i], :]

    def gat(self, r, a, b):
        i = self._find(a, b)
        n = self.segs[i]
        return self.gat_t[i][r * n + a - self.b0[i]: r * n + b - self.b0[i], :]


class Builder:
    def __init__(self, cfg, stages, fused):
        self.cfg = cfg
        self.stages = stages
        self.fused = fused
        self.nc = bass.Bass("TRN2", target_bir_lowering=False)
        self.P = Prog(self.nc)
        self.st = contextlib.ExitStack()
        self.din = {}
        self.dout = {}
        self.res = {}

    def R(self, name):
        if name not in self.res:
            self.res[name] = Res(name)
        return self.res[name]

    def inp(self, name, shape, dt=F32):
        t = self.nc.dram_tensor(name, list(shape), dt, kind="ExternalInput")
        self.din[name] = (tuple(shape), dt)
        return t.ap()

    def outp(self, name, shape, dt=F32):
        t = self.nc.dram_tensor(name, list(shape), dt, kind="ExternalOutput")
        self.dout[name] = (tuple(shape), dt)
        return t.ap()

    def sb(self, name, shape, dt):
        return self.st.enter_context(self.nc.sbuf_tensor(name, list(shape), dt))

    def mm(self, out, lhsT, rhs, start, stop, reads, writes):
        self.P.op("pe", lambda e: e.matmul(out, lhsT=lhsT, rhs=rhs, start=start, stop=stop),
                  reads, writes)

    def act(self, out, in_, func, reads, writes, scale=1.0, bias=None, accum_out=None, eng="act"):
        kw = {}
        if bias is not None:
            kw["bias"] = bias
        if accum_out is not None:
            kw["accum_out"] = accum_out
        self.P.op("act", lambda e: e.activation(out=out, in_=in_, func=func, scale=scale, **kw),
                  reads, writes)

    def tt(self, eng, out, in0, in1, op, reads, writes):
        self.P.op(eng, lambda e: e.tensor_tensor(out=out, in0=in0, in1=in1, op=op), reads, writes)

    def stt(self, eng, out, in0, scalar, in1, op0, op1, reads, writes):
        self.P.op(eng, lambda e: e.scalar_tensor_tensor(out=out, in0=in0, scalar=scalar, in1=in1,
                                                        op0=op0, op1=op1), reads, writes)

    def ts(self, eng, out, in0, s1, op0, reads, writes, s2=None, op1=None):
        if op1 is None:
            self.P.op(eng, lambda e: e.tensor_scalar(out=out, in0=in0, scalar1=s1, scalar2=None, op0=op0),
                      reads, writes)
        else:
            self.P.op(eng, lambda e: e.tensor_scalar(out=out, in0=in0, scalar1=s1, scalar2=s2, op0=op0,
                                                     op1=op1), reads, writes)

    def cp(self, eng, out, in_, reads, writes):
        if eng == "act":
            self.P.op("act", lambda e: e.copy(out=out, in_=in_), reads, writes)
        else:
            self.P.op(eng, lambda e: e.tensor_copy(out=out, in_=in_), reads, writes)

    def recip(self, out, in_, reads, writes):
        self.P.op("dve", lambda e: e.reciprocal(out=out, in_=in_), reads, writes)

    def dma(self, q, out, in_, reads, writes):
        self.P.op(q, lambda e: e.dma_start(out=out, in_=in_), reads, writes, dma=True)

    def arena_reset(self, lo=0, hi=None):
        self.a_lo = lo
        self.a_hi = self.cfg.arena if hi is None else hi

    def take(self, n, dt=BF16):
        nb = n * (2 if dt == F32 else 1)
        nb = (nb + 15) // 16 * 16
        off = self.a_lo
        self.a_lo += nb
        assert self.a_lo <= self.a_hi, ("arena overflow", self.a_lo, self.a_hi)
        ap = self.arena[:, off:off + n * (2 if dt == F32 else 1)]
        if dt == F32:
            ap = ap.bitcast(F32)
        return ap

    def fixed(self, off, n):
        return self.arena[:, off:off + n]

    def setup(self):
        cfg = self.cfg
        T = cfg.T
        self.xres = self.sb("xres", [128, 8, T], F32)
        self.arena = self.sb("arena", [128, cfg.arena], BF16)[:]
        self.cmat = self.sb("cmat_sb", [128, 8 * 128], BF16)
        self.vecs = self.sb("vecs_sb", [128, NVEC], F32)
        self.epsc = self.sb("epsc", [128, 1], F32)
        self.onescol = self.sb("onescol", [128, 1], BF16)
        self.bank = [self.st.enter_context(self.nc.psum_tensor("bank%d" % i, [128, 512], F32))
                     for i in range(8)]
        self.bres = [self.R("bank%d" % i) for i in range(8)]
        self.Rx = [[self.R("x_%d_%d" % (c, g)) for g in range(cfg.NG)] for c in range(8)]
        d_cmat = self.inp("cmat", [128, 8 * 128])
        d_vecs = self.inp("vecs", [128, NVEC])
        self.d_rows = self.inp("rows", [1, NROW])
        self.dma("pool", self.cmat[:], d_cmat, [], [self.R("cmat")])
        self.dma("sp", self.vecs[:], d_vecs, [], [self.R("vecs")])
        self.P.op("dve", lambda e: e.memset(self.epsc[:], EPS), [], [self.R("epsc")])
        self.P.op("dve", lambda e: e.memset(self.onescol[:], 1.0), [], [self.R("onescol")])

    def cm(self, idx, rows=128, cols=128):
        return self.cmat[0:rows, idx * 128: idx * 128 + cols]

    def norm_to_hT(self, hT, vcol, tmp_sq, tmp_rstd, tmp_sd):
        cfg = self.cfg
        T, TG, NG = cfg.T, cfg.TG, cfg.NG
        Rh = self.R("hT")
        Rrs = self.R("rstd")
        for c in range(8):
            sq = tmp_sq[c % 2]
            Rsq = self.R("sq%d" % (c % 2))
            self.act(sq, self.xres[:, c, :], AF.Square, self.Rx[c], [Rsq])
            for g in range(NG):
                self.mm(self.bank[g][:, 0:TG], self.cm(C_ONES), sq[:, g * TG:(g + 1) * TG],
                        c == 0, c == 7, [Rsq, self.R("cmat")], [self.bres[g]])
        for g in range(NG):
            sl = slice(g * TG, (g + 1) * TG)
            self.act(tmp_sd[:, sl], self.bank[g][:, 0:TG], AF.Sqrt, [self.bres[g], self.R("epsc")],
                     [self.R("sd")], scale=1.0 / D, bias=self.epsc[:])
        self.recip(tmp_rstd, tmp_sd, [self.R("sd")], [Rrs])
        for c in range(8):
            self.stt("dve", hT[:, c, :], self.xres[:, c, :], self.vecs[:, vcol + c: vcol + c + 1], tmp_rstd,
                     ALU.mult, ALU.mult, self.Rx[c] + [Rrs, self.R("vecs")], [Rh])

    def ffn(self, fi, vcol):
        cfg = self.cfg
        T, TG, NG = cfg.T, cfg.TG, cfg.NG
        P = self.P
        P.barrier()
        self.arena_reset()
        hT = self.take(8 * T).rearrange("p (c t) -> p c t", c=8)
        actA = self.take(NJH * T).rearrange("p (j t) -> p j t", j=NJH)
        wg = [self.take(2048).rearrange("p (c m) -> p c m", c=8) for _ in range(3)]
        wdb = [self.take(NJH * 128).rearrange("p (j m) -> p j m", j=NJH) for _ in range(3)]
        rstd = self.take(T, F32)
        sd = self.take(T, F32)
        sq = [self.take(T) for _ in range(2)]
        sg = [self.take(TG, F32) for _ in range(2)]
        Rwg = [self.R("wg%d" % i) for i in range(3)]
        Rwd = [self.R("wd%d" % i) for i in range(3)]
        Rsg = [self.R("sg%d" % i) for i in range(2)]
        Rh, Ra = self.R("hT"), self.R("actA")
        self.norm_to_hT(hT, vcol, sq, rstd, sd)
        d_wgu = self.d_wgu
        d_wd = self.d_wd
        step = 0
        wi = 0
        di = 0
        for half in range(2):
            for j in range(NJH):
                jj = half * NJH + j
                b = wi % 3
                wi += 1
                self.dma("pool", wg[b].rearrange("p c m -> p (c m)"), d_wgu[fi, jj], [], [Rwg[b]])
                for g in range(NG):
                    sl = slice(g * TG, (g + 1) * TG)
                    pb = (step % 2) * 2
                    step += 1
                    for c in range(8):
                        self.mm(self.bank[pb][:, 0:TG], wg[b][:, c, 0:128], hT[:, c, sl], c == 0, c == 7,
                                [Rwg[b], Rh], [self.bres[pb]])
                    for c in range(8):
                        self.mm(self.bank[pb + 1][:, 0:TG], wg[b][:, c, 128:256], hT[:, c, sl], c == 0, c == 7,
                                [Rwg[b], Rh], [self.bres[pb + 1]])
                    s = sg[g % 2]
                    self.act(s, self.bank[pb][:, 0:TG], AF.Silu, [self.bres[pb]], [Rsg[g % 2]])
                    self.tt("dve", actA[:, j, sl], s, self.bank[pb + 1][:, 0:TG], ALU.mult,
                            [Rsg[g % 2], self.bres[pb + 1]], [Ra])
            for dc in range(8):
                b = di % 3
                di += 1
                self.dma("pool", wdb[b].rearrange("p j m -> p (j m)"), d_wd[fi, half, dc], [], [Rwd[b]])
                for g in range(NG):
                    sl = slice(g * TG, (g + 1) * TG)
                    pb = 4 + (step % 2)
                    step += 1
                    for j in range(NJH):
                        self.mm(self.bank[pb][:, 0:TG], wdb[b][:, j, :], actA[:, j, sl], j == 0, j == NJH - 1,
                                [Rwd[b], Ra], [self.bres[pb]])
                    self.stt("dve", self.xres[:, dc, sl], self.bank[pb][:, 0:TG], 0.5, self.xres[:, dc, sl],
                             ALU.mult, ALU.add, [self.bres[pb], self.Rx[dc][g]], [self.Rx[dc][g]])

    def nr_alloc(self):
        TG = self.cfg.TG
        self.nr = []
        for k in range(2):
            self.nr.append(dict(qf=self.take(TG, F32), sd=self.take(TG, F32), t2=self.take(TG, F32),
                                sqb=self.take(TG), qnb=self.take(TG)))
        self.nrk = 0

    def normrope(self, ps, Rps, R, sl, onesidx, nnorm, gcol, rmidx, cosT, sinT, Rtab, out, Rout):
        TG = self.cfg.TG
        k = self.nrk % 2
        self.nrk += 1
        t = self.nr[k]
        Rn = lambda n: self.R("nr_%s%d" % (n, k))
        bs = 2 + k
        br = 4 + k
        qf, sd, t2, sqb, qnb = (t["qf"][0:R], t["sd"][0:R], t["t2"][0:R], t["sqb"][0:R], t["qnb"][0:R])
        self.cp("act", qf, ps, [Rps], [Rn("qf")])
        self.act(sqb, ps, AF.Square, [Rps], [Rn("sqb")])
        self.mm(self.bank[bs][0:R, 0:TG], self.cm(onesidx, R, R), sqb, True, True,
                [Rn("sqb"), self.R("cmat")], [self.bres[bs]])
        self.act(sd, self.bank[bs][0:R, 0:TG], AF.Sqrt, [self.bres[bs], self.R("epsc")], [Rn("sd")],
                 scale=1.0 / nnorm, bias=self.epsc[0:R, :])
        self.recip(sd, sd, [Rn("sd")], [Rn("sd")])
        self.stt("dve", qf, qf, self.vecs[0:R, gcol:gcol + 1], sd, ALU.mult, ALU.mult,
                 [Rn("qf"), Rn("sd"), self.R("vecs")], [Rn("qf")])
        self.cp("pool", qnb, qf, [Rn("qf")], [Rn("qnb")])
        self.mm(self.bank[br][0:R, 0:TG], self.cm(rmidx, R, R), qnb, True, True,
                [Rn("qnb"), self.R("cmat")], [self.bres[br]])
        self.tt("dve", t2, self.bank[br][0:R, 0:TG], sinT[0:R, sl], ALU.mult, [self.bres[br], Rtab], [Rn("t2")])
        self.tt("pool", qf, qf, cosT[0:R, sl], ALU.mult, [Rn("qf"), Rtab], [Rn("qf")])
        self.tt("pool", out, qf, t2, ALU.add, [Rn("qf"), Rn("t2")], [Rout])

    def off_qT(self, odd):
        return self.cfg.arena - (12 if odd else 4) * self.cfg.T

    def off_mixT(self):
        return self.cfg.arena - 12 * self.cfg.T

    def off_otok(self, odd):
        return self.off_mixT() - (8 if odd else 4) * self.cfg.T

    def even_prep(self, vcol, kv_scr):
        cfg = self.cfg
        T, TG, NG, NT = cfg.T, cfg.TG, cfg.NG, cfg.NT
        P = self.P
        P.barrier()
        qT = self.fixed(self.off_qT(False), 4 * T).rearrange("p (c t) -> p c t", c=4)
        mixT = self.fixed(self.off_mixT(), 8 * T).rearrange("p (c t) -> p c t", c=8)
        Rq, Rm_, Rh = self.R("qT"), self.R("mixT"), self.R("hT")
        self.arena_reset(0, self.off_mixT())
        hT = self.take(8 * T).rearrange("p (c t) -> p c t", c=8)
        mark = self.a_lo
        rstd = self.take(T, F32)
        sd = self.take(T, F32)
        sq = [self.take(T) for _ in range(2)]
        self.norm_to_hT(hT, vcol, sq, rstd, sd)
        P.barrier()
        self.arena_reset(mark, self.off_mixT())
        cosT = self.take(T, F32)
        sinT = self.take(T, F32)
        self.nr_alloc()
        wb = [self.take(1024).rearrange("p (c m) -> p c m", c=8) for _ in range(3)]
        kst = [self.take(TG) for _ in range(2)]
        Rwb = [self.R("wb%d" % i) for i in range(3)]
        Rks = [self.R("kst%d" % i) for i in range(2)]
        Rtab = self.R("tab")
        self.kvres = []

        def Rkv_new():
            r = Res("kvw")
            self.kvres.append(r)
            return r
        self.dma("sp", cosT, self.d_cs_e[0], [], [Rtab])
        self.dma("sp", sinT, self.d_cs_e[1], [], [Rtab])
        step = 0
        ks = 0
        for ci in range(12):
            b = ci % 3
            self.dma("pool", wb[b].rearrange("p c m -> p (c m)"), self.d_evw_fm[ci], [], [Rwb[b]])
            kind, h = ("u", "q", "k")[ci // 4], ci % 4
            for g in range(NG):
                sl = slice(g * TG, (g + 1) * TG)
                pb = step % 2
                step += 1
                for c in range(8):
                    self.mm(self.bank[pb][:, 0:TG], wb[b][:, c, :], hT[:, c, sl], c == 0, c == 7,
                            [Rwb[b], Rh], [self.bres[pb]])
                ps = self.bank[pb][:, 0:TG]
                if kind == "u":
                    self.act(mixT[:, h, sl], ps, AF.Gelu_apprx_tanh, [self.bres[pb]], [Rm_])
                elif kind == "q":
                    self.normrope(ps, self.bres[pb], 128, sl, C_BD, 64, V_EQ, C_RME, cosT, sinT, Rtab,
                                  qT[:, h, sl], Rq)
                else:
                    kb = ks % 2
                    ks += 1
                    self.normrope(ps, self.bres[pb], 128, sl, C_BD, 64, V_EK, C_RME, cosT, sinT, Rtab,
                                  kst[kb], Rks[kb])
                    self.dma("sp", kv_scr.loc(h * 128, (h + 1) * 128)[:, sl], kst[kb], [Rks[kb]], [Rkv_new()])
        P.barrier()
        self.arena_reset(mark, self.off_mixT())
        wtm = self.take(4096).rearrange("p (c m) -> p c m", c=8)
        wsT = self.take(1024).rearrange("p (g i) -> p g i", g=8)
        Gt = self.take(512, F32)
        biast = self.take(512, F32)
        vg = [self.take(512, F32) for _ in range(2)]
        sqv = self.take(512, F32)
        ss8 = [self.take(8, F32) for _ in range(2)]
        vc = [self.take(512) for _ in range(2)]
        tmpm = [self.take(512, F32) for _ in range(2)]
        vst = [self.take(512) for _ in range(2)]
        Rwtm, Rws, Rg = self.R("wtm"), self.R("wsT"), self.R("Gt")
        self.dma("pool", wtm.rearrange("p c m -> p (c m)").rearrange("p (a b) -> p a b", b=2048),
                 self.d_evw_tm[0].rearrange("p (a b) -> p a b", b=2048), [], [Rwtm])
        self.dma("pool", wsT.rearrange("p g i -> p (g i)"), self.d_wsT, [], [Rws])
        self.dma("sp", Gt, self.d_rows[0:1, R_SGU:R_SGU + 512].partition_broadcast(128), [], [Rg])
        self.dma("sp", biast, self.d_bias_t, [], [Rg])
        for tt_ in range(NT):
            k = tt_ % 2
            tsl = slice(tt_ * 128, (tt_ + 1) * 128)
            pb = 6 + k
            Rvg, Rss, Rvc, Rtm = (self.R("vg%d" % k), self.R("ss8%d" % k), self.R("vc%d" % k), self.R("tmpm%d" % k))
            for c in range(8):
                self.mm(self.bank[pb][:, :], hT[:, c, tsl], wtm[:, c, :], c == 0, c == 7, [Rh, Rwtm], [self.bres[pb]])
            self.act(vg[k], self.bank[pb][:, :], AF.Gelu_apprx_tanh, [self.bres[pb]], [Rvg])
            self.tt("dve", sqv, vg[k], vg[k], ALU.mult, [Rvg], [self.R("sqv")])
            self.P.op("dve", (lambda o, i: (lambda e: e.reduce_sum(out=o, in_=i, axis=AX.X)))(
                ss8[k], sqv.rearrange("p (g d) -> p g d", g=8)), [self.R("sqv")], [Rss])
            self.act(ss8[k], ss8[k], AF.Sqrt, [Rss, self.R("epsc")], [Rss], scale=1.0 / 64, bias=self.epsc[:])
            self.recip(ss8[k], ss8[k], [Rss], [Rss])
            vg3 = vg[k].rearrange("p (g d) -> p g d", g=8)
            self.tt("dve", vg3, vg3, ss8[k].unsqueeze(2).to_broadcast([128, 8, 64]), ALU.mult, [Rvg, Rss], [Rvg])
            self.tt("pool", vc[k], vg[k], Gt, ALU.mult, [Rvg, Rg], [Rvc])
            bm = 4 + k
            for g8 in range(8):
                po = (g8 % 2) * 64
                self.mm(self.bank[bm][po:po + 64, (g8 // 2) * 128:(g8 // 2) * 128 + 128],
                        vc[k][:, g8 * 64:(g8 + 1) * 64], wsT[:, g8, :], True, True, [Rvc, Rws], [self.bres[bm]])
            self.tt("dve", tmpm[k], self.bank[bm][:, :], biast, ALU.add, [self.bres[bm], Rg], [Rtm])
            self.tt("pool", mixT[:, 0:4, tsl], tmpm[k].rearrange("p (c i) -> p c i", c=4), mixT[:, 0:4, tsl],
                    ALU.mult, [Rtm, Rm_], [Rm_])
        self.dma("pool", wtm.rearrange("p c m -> p (c m)").rearrange("p (a b) -> p a b", b=2048),
                 self.d_evw_tm[1].rearrange("p (a b) -> p a b", b=2048), [], [Rwtm])
        vs = [kv_scr.loc(512 + 256 * s_, 768 + 256 * s_).rearrange("(h p) (t c) -> h p t c", h=2, c=128) for s_ in range(2)]
        for tt_ in range(NT):
            k = tt_ % 2
            tsl = slice(tt_ * 128, (tt_ + 1) * 128)
            pb = 6 + k
            Rvs = self.R("vst%d" % k)
            for c in range(8):
                self.mm(self.bank[pb][:, :], hT[:, c, tsl], wtm[:, c, :], c == 0, c == 7, [Rh, Rwtm], [self.bres[pb]])
            self.cp("act", vst[k], self.bank[pb][:, :], [self.bres[pb]], [Rvs])
            for s_ in range(2):
                self.dma("sp", vs[s_][:, :, tt_, :].rearrange("h p c -> p h c"),
                         vst[k][:, s_ * 256:(s_ + 1) * 256].rearrange("p (h c) -> p h c", h=2), [Rvs], [Rkv_new()])

    def attn_alloc(self, dvp1, n_pt=3):
        cfg = self.cfg
        self.Kb = [self.take(cfg.S) for _ in range(2)]
        self.Vb = [self.take(cfg.KT * dvp1).rearrange("p (k c) -> p k c", c=dvp1) for _ in range(2)]
        self.pT = [self.take(cfg.TG) for _ in range(n_pt)]
        self.RK = [self.R("Kb%d" % i) for i in range(2)]
        self.RV = [self.R("Vb%d" % i) for i in range(2)]
        self.RpT = [self.R("pT%d" % i) for i in range(n_pt)]
        self.maps = []
        for i in range(2):
            self.P.op("pool", (lambda v: (lambda e: e.memset(v, 1.0)))(self.Vb[i][:, :, dvp1 - 1:dvp1]),
                      [], [self.RV[i]])

    def attn_map(self, kb, krows, qap, Rq, scale, dvp1, acc_of_j, accres_of_j, first_of_bank, last_of_bank,
                 pre=None, post=None):
        self.maps.append(dict(kb=kb, krows=krows, qap=qap, Rq=Rq, scale=scale, dvp1=dvp1, acc=acc_of_j,
                              accres=accres_of_j, first=first_of_bank, last=last_of_bank, pre=pre, post=post))

    def run_attn(self):
        cfg = self.cfg
        TG, KT, TPG = cfg.TG, cfg.KT, cfg.TPG
        maps = self.maps
        tiles = [(mi, kt) for mi in range(len(maps)) for kt in range(KT)]
        N = len(tiles)
        delay = min(4, KT - 1)
        pending = []

        def rec_qk(i):
            mi, kt = tiles[i]
            m = maps[mi]
            if kt == 0 and m["pre"] is not None:
                m["pre"]()
            sb_ = i % 2
            self.mm(self.bank[sb_][:, 0:TG], self.Kb[m["kb"]][m["krows"], kt * 128:(kt + 1) * 128], m["qap"], True, True,
                    [self.RK[m["kb"]], m["Rq"]], [self.bres[sb_]])

        def rec_exp_pv(i):
            mi, kt = tiles[i]
            m = maps[mi]
            sb_ = i % 2
            pi = i % len(self.pT)
            self.act(self.pT[pi], self.bank[sb_][:, 0:TG], AF.Exp, [self.bres[sb_]], [self.RpT[pi]], scale=m["scale"])
            for j in range(TPG):
                self.mm(m["acc"](j), self.pT[pi][:, j * 128:(j + 1) * 128], self.Vb[m["kb"]][:, kt, 0:m["dvp1"]],
                        kt == 0 and m["first"](j), kt == KT - 1 and m["last"](j),
                        [self.RpT[pi], self.RV[m["kb"]]], [m["accres"](j)])
            if kt == KT - 1 and m["post"] is not None:
                pending.append((i + delay, m["post"]))

        rec_qk(0)
        for i in range(N):
            if i + 1 < N:
                rec_qk(i + 1)
            rec_exp_pv(i)
            while pending and pending[0][0] <= i:
                pending.pop(0)[1]()
        for _, p in pending:
            p()
        self.maps = []

    def even_attn(self, kvall):
        cfg = self.cfg
        T, TG, NG, NT, KT, TPG = cfg.T, cfg.TG, cfg.NG, cfg.NT, cfg.KT, cfg.TPG
        P = self.P
        P.barrier()
        lam_init = 0.8 - 0.6 * float(np.exp(-0.3 * 0))
        qT = self.fixed(self.off_qT(False), 4 * T).rearrange("p (c t) -> p c t", c=4)
        otok = self.fixed(self.off_otok(False), 4 * T).rearrange("p (t c) -> p t c", c=512)
        Rq, Ro = self.R("qT"), self.R("otok")
        self.arena_reset(0, self.off_otok(False))
        self.attn_alloc(129)
        lamt = self.take(256, F32)
        lp = self.take(128, F32)
        subg = self.take(128, F32)
        sm = self.take(8, F32)
        fin = [dict(r0=self.take(1, F32), r1=self.take(1, F32), O0=self.take(128, F32), od=self.take(128, F32),
                    ss=self.take(1, F32), junk=self.take(128)) for _ in range(2)]
        Rl = self.R("lam")
        self.dma("sp", lamt, self.d_rows[0:1, R_LAM:R_LAM + 256].partition_broadcast(128), [], [Rl])
        self.dma("sp", subg, self.d_rows[0:1, R_SUB:R_SUB + 128].partition_broadcast(128), [], [self.R("subg")])
        self.tt("dve", lp[:, 0:64], lamt[:, 0:64], lamt[:, 64:128], ALU.mult, [Rl], [self.R("lp")])
        self.tt("dve", lp[:, 64:128], lamt[:, 128:192], lamt[:, 192:256], ALU.mult, [Rl], [self.R("lp")])
        self.P.op("dve", lambda e: e.reduce_sum(out=sm[:, 0:2], in_=lp.rearrange("p (a d) -> p a d", a=2), axis=AX.X),
                  [self.R("lp")], [self.R("sm")])
        self.act(sm[:, 2:4], sm[:, 0:2], AF.Exp, [self.R("sm")], [self.R("sm2")])
        self.tt("dve", sm[:, 4:5], sm[:, 3:4], sm[:, 2:3], ALU.subtract, [self.R("sm2")], [self.R("sm3")])
        self.ts("dve", sm[:, 5:6], sm[:, 4:5], -lam_init, ALU.add, [self.R("sm3")], [self.R("neglam")])
        neglam = sm[:, 5:6]
        self.ts("dve", subg, subg, 1.0 - lam_init, ALU.mult, [self.R("subg")], [self.R("subg")])
        accsets = [(2, 3), (4, 5), (6, 7)]
        ai = 0
        fi_ = 0
        def kvload(h, kb):
            for r in range(4):
                self.dma("sp", self.Kb[kb][:, r * T:(r + 1) * T], kvall.gat(r, h * 128, (h + 1) * 128),
                         [self.R("kvall")], [self.RK[kb]])
                self.dma("sp", self.Vb[kb][:, r * NT:(r + 1) * NT, 0:128],
                         kvall.gat(r, 512 + h * 128, 512 + (h + 1) * 128).rearrange("p (t c) -> p t c", c=128),
                         [self.R("kvall")], [self.RV[kb]])

        def finalize(h, g, sets):
            nonlocal fi_
            if True:
                for j in range(TPG):
                    f = fin[fi_ % 2]
                    k = fi_ % 2
                    fi_ += 1
                    Rf = lambda n: self.R("fin_%s%d" % (n, k))
                    (a0, r0b), (a1, r1b) = [((self.bank[s[0]][:, j * 129:(j + 1) * 129], self.bres[s[0]]) if j < 3
                                             else (self.bank[s[1]][:, 0:129], self.bres[s[1]])) for s in sets]
                    self.recip(f["r0"], a0[:, 128:129], [r0b], [Rf("r0")])
                    self.recip(f["r1"], a1[:, 128:129], [r1b], [Rf("r1")])
                    self.ts("dve", f["O0"], a0[:, 0:128], f["r0"][:, 0:1], ALU.mult, [r0b, Rf("r0")], [Rf("O0")])
                    self.tt("dve", f["r1"], f["r1"], neglam, ALU.mult, [Rf("r1"), self.R("neglam")], [Rf("r1")])
                    self.stt("dve", f["od"], a1[:, 0:128], f["r1"][:, 0:1], f["O0"], ALU.mult, ALU.add,
                             [r1b, Rf("r1"), Rf("O0")], [Rf("od")])
                    self.act(f["junk"], f["od"], AF.Square, [Rf("od")], [Rf("junk"), Rf("ss")], accum_out=f["ss"])
                    self.act(f["ss"], f["ss"], AF.Sqrt, [Rf("ss"), self.R("epsc")], [Rf("ss")], scale=1.0 / 128,
                             bias=self.epsc[:])
                    self.recip(f["ss"], f["ss"], [Rf("ss")], [Rf("ss")])
                    tt_ = g * TPG + j
                    self.stt("dve", otok[:, tt_, h * 128:(h + 1) * 128], f["od"], f["ss"][:, 0:1], subg,
                             ALU.mult, ALU.mult, [Rf("od"), Rf("ss"), self.R("subg")], [Ro])

        for h in range(4):
            kb = h % 2
            for g in range(NG):
                sl = slice(g * TG, (g + 1) * TG)
                sets = []
                for m in range(2):
                    bA, bB = accsets[ai % 3]
                    ai += 1
                    sets.append((bA, bB))
                    rows = slice(m * 64, (m + 1) * 64)
                    self.attn_map(kb, rows, qT[rows, h, sl], Rq, 0.125, 129,
                                  lambda j, bA=bA, bB=bB: (self.bank[bA][:, j * 129:(j + 1) * 129] if j < 3
                                                           else self.bank[bB][:, 0:129]),
                                  lambda j, bA=bA, bB=bB: self.bres[bA] if j < 3 else self.bres[bB],
                                  lambda j: j == 0 or j == 3, lambda j: j == min(TPG, 3) - 1 or j == 3,
                                  pre=(lambda h=h, kb=kb: kvload(h, kb)) if (g == 0 and m == 0) else None,
                                  post=(lambda h=h, g=g, sets=list(sets): finalize(h, g, sets)) if m == 1 else None)
        self.run_attn()

    def out_proj(self, odd, d_wo):
        cfg = self.cfg
        T, TG, NG, NT, TPG = cfg.T, cfg.TG, cfg.NG, cfg.NT, cfg.TPG
        P = self.P
        P.barrier()
        ncol = 1024 if odd else 512
        otok = self.fixed(self.off_otok(odd), (8 if odd else 4) * T).rearrange("p (t c) -> p t c", c=ncol)
        mixT = self.fixed(self.off_mixT(), 8 * T).rearrange("p (c t) -> p c t", c=8)
        Ro, Rm_ = self.R("otok"), self.R("mixT")
        self.arena_reset(0, self.off_otok(odd))
        wo = [self.take(1024).rearrange("p (c m) -> p c m", c=8) for _ in range(3)]
        Rwo = [self.R("wo%d" % i) for i in range(3)]
        c0 = 0 if odd else 4
        step = 0
        for c in range(ncol // 128):
            for g in range(NG):
                pb = step % 2
                step += 1
                pst = self.bank[pb].bitcast(BF16)
                for j in range(TPG):
                    tt_ = g * TPG + j
                    self.P.op("pe", (lambda o, i: (lambda e: e.transpose(o, i, self.cm(C_ID))))(
                        pst[:, j * 128:(j + 1) * 128], otok[:, tt_, c * 128:(c + 1) * 128]),
                        [Ro, self.R("cmat")], [self.bres[pb]])
                eng = "act" if step % 2 == 0 else "dve"
                self.cp(eng, mixT[:, c0 + c, g * TG:(g + 1) * TG], pst[:, 0:TG], [self.bres[pb]], [Rm_])
        for dc in range(8):
            b = dc % 3
            self.dma("pool", wo[b].rearrange("p c m -> p (c m)"), d_wo[dc], [], [Rwo[b]])
            for g in range(NG):
                sl = slice(g * TG, (g + 1) * TG)
                pb = 2 + step % 2
                step += 1
                for hc in range(8):
                    self.mm(self.bank[pb][:, 0:TG], wo[b][:, hc, :], mixT[:, hc, sl], hc == 0, hc == 7,
                            [Rwo[b], Rm_], [self.bres[pb]])
                self.tt("dve", self.xres[:, dc, sl], self.bank[pb][:, 0:TG], self.xres[:, dc, sl], ALU.add,
                        [self.bres[pb], self.Rx[dc][g]], [self.Rx[dc][g]])

    def odd_prep(self, vcol, kv_scr):
        cfg = self.cfg
        T, TG, NG, NT, TPG = cfg.T, cfg.TG, cfg.NG, cfg.NT, cfg.TPG
        P = self.P
        P.barrier()
        oq = self.off_qT(True)
        qTm = self.fixed(oq, 8 * T).rearrange("p (c t) -> p c t", c=8)
        qTg = self.fixed(oq + 8 * T, 4 * T).rearrange("p (c t) -> p c t", c=4)
        Rq, Rh = self.R("qT"), self.R("hT")
        self.arena_reset(0, oq)
        hT = self.take(8 * T).rearrange("p (c t) -> p c t", c=8)
        mark = self.a_lo
        rstd = self.take(T, F32)
        sd = self.take(T, F32)
        sq = [self.take(T) for _ in range(2)]
        self.norm_to_hT(hT, vcol, sq, rstd, sd)
        self.kvres = []

        def Rkv_new():
            r = Res("kvw")
            self.kvres.append(r)
            return r
        P.barrier()
        self.arena_reset(mark, oq)
        cosT = self.take(T, F32)
        sinT = self.take(T, F32)
        self.nr_alloc()
        wfm = [self.take(1024).rearrange("p (c m) -> p c m", c=8) for _ in range(3)]
        wkpe = self.take(256).rearrange("p (c m) -> p c m", c=8)
        wuq = self.take(1536).rearrange("p (c m) -> p c m", c=2)
        wkp = self.take(768)
        wvm = self.take(512)
        cqf = [self.take(TG, F32) for _ in range(2)]
        cqs = [self.take(TG) for _ in range(2)]
        csd = self.take(TG, F32)
        cqn = self.take(2 * TG).rearrange("p (c t) -> p c t", c=2)
        ckvn = self.take(TG)
        kpeb = self.take(TG)
        kst = [self.take(TG) for _ in range(2)]
        vst = [self.take(512) for _ in range(2)]
        Rw, Rtab = self.R("odw"), self.R("tab")
        for i in range(3):
            self.dma("pool", wfm[i].rearrange("p c m -> p (c m)"), self.d_odw_fm[i], [], [Rw])
        self.dma("pool", wkpe.rearrange("p c m -> p (c m)"), self.d_odw_kpe, [], [Rw])
        self.dma("pool", wuq.rearrange("p c m -> p (c m)"), self.d_wuq, [], [Rw])
        self.dma("pool", wkp, self.d_wkp, [], [Rw])
        self.dma("pool", wvm, self.d_wvm, [], [Rw])
        self.dma("sp", cosT, self.d_cs_m[0], [], [Rtab])
        self.dma("sp", sinT, self.d_cs_m[1], [], [Rtab])
        vsm = [kv_scr.loc(896 + 256 * s_, 1152 + 256 * s_).rearrange("r (b x) -> (r b) x", b=2).rearrange(
            "(h p) (t c) -> h p t c", h=4, c=64) for s_ in range(2)]
        step = 0
        ks = 0
        vi = 0
        for g in range(NG):
            sl = slice(g * TG, (g + 1) * TG)
            for c2 in range(2):
                for c in range(8):
                    self.mm(self.bank[c2][:, 0:TG], wfm[c2][:, c, :], hT[:, c, sl], c == 0, c == 7, [Rw, Rh],
                            [self.bres[c2]])
                self.cp("act", cqf[c2], self.bank[c2][:, 0:TG], [self.bres[c2]], [self.R("cqf%d" % c2)])
                self.act(cqs[c2], self.bank[c2][:, 0:TG], AF.Square, [self.bres[c2]], [self.R("cqs%d" % c2)])
            for c2 in range(2):
                self.mm(self.bank[6][:, 0:TG], self.cm(C_ONES), cqs[c2], c2 == 0, c2 == 1,
                        [self.R("cqs%d" % c2), self.R("cmat")], [self.bres[6]])
            self.act(csd, self.bank[6][:, 0:TG], AF.Sqrt, [self.bres[6], self.R("epsc")], [self.R("csd")],
                     scale=1.0 / 256, bias=self.epsc[:])
            self.recip(csd, csd, [self.R("csd")], [self.R("csd")])
            for c2 in range(2):
                self.stt("dve", cqn[:, c2, :], cqf[c2], self.vecs[:, V_CQ + c2:V_CQ + c2 + 1], csd, ALU.mult, ALU.mult,
                         [self.R("cqf%d" % c2), self.R("csd"), self.R("vecs")], [self.R("cqn")])
            for c in range(8):
                self.mm(self.bank[7][:, 0:TG], wfm[2][:, c, :], hT[:, c, sl], c == 0, c == 7, [Rw, Rh], [self.bres[7]])
            self.cp("act", cqf[0], self.bank[7][:, 0:TG], [self.bres[7]], [self.R("cqf0")])
            self.act(cqs[0], self.bank[7][:, 0:TG], AF.Square, [self.bres[7]], [self.R("cqs0")])
            self.mm(self.bank[6][:, 0:TG], self.cm(C_ONES), cqs[0], True, True, [self.R("cqs0"), self.R("cmat")],
                    [self.bres[6]])
            self.act(csd, self.bank[6][:, 0:TG], AF.Sqrt, [self.bres[6], self.R("epsc")], [self.R("csd")],
                     scale=1.0 / 128, bias=self.epsc[:])
            self.recip(csd, csd, [self.R("csd")], [self.R("csd")])
            self.stt("dve", ckvn, cqf[0], self.vecs[:, V_CKV:V_CKV + 1], csd, ALU.mult, ALU.mult,
                     [self.R("cqf0"), self.R("csd"), self.R("vecs")], [self.R("ckvn")])
            for c in range(8):
                self.mm(self.bank[7][0:32, 0:TG], wkpe[:, c, :], hT[:, c, sl], c == 0, c == 7, [Rw, Rh], [self.bres[7]])
            self.cp("act", kpeb[0:32], self.bank[7][0:32, 0:TG], [self.bres[7]], [self.R("kpeb")])
            for h in range(8):
                pb = step % 2
                step += 1
                for c2 in range(2):
                    self.mm(self.bank[pb][0:96, 0:TG], wuq[:, c2, h * 96:(h + 1) * 96], cqn[:, c2, :], c2 == 0, c2 == 1,
                            [Rw, self.R("cqn")], [self.bres[pb]])
                self.normrope(self.bank[pb][0:96, 0:TG], self.bres[pb], 96, sl, C_O96, 96, V_MQ, C_RMM, cosT, sinT,
                              Rtab, qTm[0:96, h, sl], Rq)
            for h in range(8):
                pb = step % 2
                step += 1
                self.mm(self.bank[pb][0:96, 0:TG], wkp[:, h * 96:(h + 1) * 96], ckvn, True, False,
                        [Rw, self.R("ckvn")], [self.bres[pb]])
                self.mm(self.bank[pb][0:96, 0:TG], self.cm(C_SEL, 32, 96), kpeb[0:32], False, True,
                        [self.R("cmat"), self.R("kpeb")], [self.bres[pb]])
                kb = ks % 2
                ks += 1
                self.normrope(self.bank[pb][0:96, 0:TG], self.bres[pb], 96, sl, C_O96, 96, V_MK, C_RMM, cosT, sinT,
                              Rtab, kst[kb][0:96], self.R("kst%d" % kb))
                self.dma("sp", kv_scr.loc(h * 96, (h + 1) * 96)[:, sl], kst[kb][0:96], [self.R("kst%d" % kb)], [Rkv_new()])
            for j in range(TPG):
                tt_ = g * TPG + j
                k = vi % 2
                vi += 1
                self.mm(self.bank[6][:, :], ckvn[:, j * 128:(j + 1) * 128], wvm, True, True, [self.R("ckvn"), Rw],
                        [self.bres[6]])
                self.cp("act", vst[k], self.bank[6][:, :], [self.bres[6]], [self.R("vst%d" % k)])
                for s_ in range(2):
                    self.dma("sp", vsm[s_][:, :, tt_, :].rearrange("h p c -> p h c"),
                             vst[k][:, s_ * 256:(s_ + 1) * 256].rearrange("p (h c) -> p h c", h=4),
                             [self.R("vst%d" % k)], [Rkv_new()])
        P.barrier()
        self.arena_reset(mark, oq)
        cosT = self.take(T, F32)
        sinT = self.take(T, F32)
        self.nr_alloc()
        wb = [self.take(1024).rearrange("p (c m) -> p c m", c=8) for _ in range(3)]
        wgv = self.take(1024).rearrange("p (c m) -> p c m", c=8)
        kst = [self.take(TG) for _ in range(2)]
        vst = [self.take(128) for _ in range(2)]
        Rwb = [self.R("wb%d" % i) for i in range(3)]
        self.dma("sp", cosT, self.d_cs_g[0], [], [Rtab])
        self.dma("sp", sinT, self.d_cs_g[1], [], [Rtab])
        self.dma("pool", wgv.rearrange("p c m -> p (c m)"), self.d_odw_gv, [], [self.R("wgv")])
        for ci in range(5):
            b = ci % 3
            self.dma("pool", wb[b].rearrange("p c m -> p (c m)"), self.d_odw_fm[3 + ci], [], [Rwb[b]])
            for g in range(NG):
                sl = slice(g * TG, (g + 1) * TG)
                pb = step % 2
                step += 1
                for c in range(8):
                    self.mm(self.bank[pb][:, 0:TG], wb[b][:, c, :], hT[:, c, sl], c == 0, c == 7, [Rwb[b], Rh],
                            [self.bres[pb]])
                if ci < 4:
                    self.normrope(self.bank[pb][:, 0:TG], self.bres[pb], 128, sl, C_BD, 64, V_GQ, C_RMG, cosT, sinT,
                                  Rtab, qTg[:, ci, sl], Rq)
                else:
                    kb = ks % 2
                    ks += 1
                    self.normrope(self.bank[pb][:, 0:TG], self.bres[pb], 128, sl, C_BD, 64, V_GK, C_RMG, cosT, sinT,
                                  Rtab, kst[kb], self.R("kst%d" % kb))
                    self.dma("sp", kv_scr.loc(768, 896)[:, sl], kst[kb], [self.R("kst%d" % kb)], [Rkv_new()])
        vsg = kv_scr.loc(1408, 1536).rearrange("r (b x) -> (r b) x", b=2).rearrange("(h p) (t c) -> h p t c", h=2, c=64)
        for tt_ in range(NT):
            k = tt_ % 2
            tsl = slice(tt_ * 128, (tt_ + 1) * 128)
            pb = 6 + k
            for c in range(8):
                self.mm(self.bank[pb][:, 0:128], hT[:, c, tsl], wgv[:, c, :], c == 0, c == 7, [Rh, self.R("wgv")],
                        [self.bres[pb]])
            self.cp("act", vst[k], self.bank[pb][:, 0:128], [self.bres[pb]], [self.R("vstg%d" % k)])
            self.dma("sp", vsg[:, :, tt_, :].rearrange("h p c -> p h c"), vst[k].rearrange("p (h c) -> p h c", h=2),
                     [self.R("vstg%d" % k)], [Rkv_new()])

    def odd_attn(self, kvall):
        cfg = self.cfg
        T, TG, NG, NT, KT, TPG = cfg.T, cfg.TG, cfg.NG, cfg.NT, cfg.KT, cfg.TPG
        P = self.P
        P.barrier()
        oq = self.off_qT(True)
        qTm = self.fixed(oq, 8 * T).rearrange("p (c t) -> p c t", c=8)
        qTg = self.fixed(oq + 8 * T, 4 * T).rearrange("p (c t) -> p c t", c=4)
        otok = self.fixed(self.off_otok(True), 8 * T).rearrange("p (t c) -> p t c", c=1024)
        Rq, Ro = self.R("qT"), self.R("otok")
        self.arena_reset(0, self.off_otok(True))
        self.attn_alloc(65)
        r4 = [self.take(4, F32) for _ in range(2)]
        NR = 1536
        accb = [2, 3, 4, 5, 6, 7]
        ai = 0

        def run_map(kb, krows, qap, scale, col0, pre):
            nonlocal ai
            for g in range(NG):
                sl = slice(g * TG, (g + 1) * TG)
                a = accb[ai % 6]
                k = ai % 2
                ai += 1

                def post(a=a, k=k, g=g, col0=col0):
                    acc3 = self.bank[a][:, 0:TPG * 65].rearrange("p (j c) -> p j c", c=65)
                    self.recip(r4[k][:, 0:TPG], acc3[:, :, 64], [self.bres[a]], [self.R("r4%d" % k)])
                    self.tt("dve", otok[:, g * TPG:(g + 1) * TPG, col0:col0 + 64], acc3[:, :, 0:64],
                            r4[k][:, 0:TPG].unsqueeze(2).to_broadcast([128, TPG, 64]), ALU.mult,
                            [self.bres[a], self.R("r4%d" % k)], [Ro])
                self.attn_map(kb, krows, qap(sl), Rq, scale, 65,
                              lambda j, a=a: self.bank[a][:, j * 65:(j + 1) * 65],
                              lambda j, a=a: self.bres[a], lambda j: j == 0, lambda j: j == TPG - 1,
                              pre=pre if g == 0 else None, post=post)

        def load_mla(h, kb):
            for r in range(4):
                self.dma("sp", self.Kb[kb][0:96, r * T:(r + 1) * T], kvall.gat(r, h * 96, (h + 1) * 96),
                         [self.R("kvall")], [self.RK[kb]])
                vseg = 896 + 256 * (h // 4)
                vsrc = kvall.gat(r, vseg, vseg + 256).rearrange("r (b x) -> (r b) x", b=2)[(h % 4) * 128:(h % 4 + 1) * 128, :]
                self.dma("sp", self.Vb[kb][:, r * NT:(r + 1) * NT, 0:64], vsrc.rearrange("p (t c) -> p t c", c=64),
                         [self.R("kvall")], [self.RV[kb]])

        def load_gqa(hk, kb):
            for r in range(4):
                for dup in range(2):
                    self.dma("sp", self.Kb[kb][dup * 64:(dup + 1) * 64, r * T:(r + 1) * T],
                             kvall.gat(r, 768 + hk * 64, 768 + (hk + 1) * 64),
                             [self.R("kvall")], [self.RK[kb]])
                vsrc = kvall.gat(r, 1408, 1536).rearrange("r (b x) -> (r b) x", b=2)[hk * 128:(hk + 1) * 128, :]
                self.dma("sp", self.Vb[kb][:, r * NT:(r + 1) * NT, 0:64], vsrc.rearrange("p (t c) -> p t c", c=64),
                         [self.R("kvall")], [self.RV[kb]])

        for u in range(10):
            kb = u % 2
            if u < 8:
                h = u
                run_map(kb, slice(0, 96), lambda sl, h=h: qTm[0:96, h, sl], 96 ** -0.5, h * 64,
                        lambda h=h, kb=kb: load_mla(h, kb))
            else:
                hk = u - 8
                for g4 in range(4):
                    qh = hk * 4 + g4
                    rows = slice((qh % 2) * 64, (qh % 2) * 64 + 64)
                    run_map(kb, rows, lambda sl, rows=rows, qh=qh: qTg[rows, qh // 2, sl], 0.125, 512 + qh * 64,
                            (lambda hk=hk, kb=kb: load_gqa(hk, kb)) if g4 == 0 else None)
        self.run_attn()

    def build(self):
        cfg = self.cfg
        T = cfg.T
        stages = self.stages
        self.setup()
        self.d_wgu = self.inp("wgu", [4, NJ, 128, 2048])
        self.d_wd = self.inp("wd", [4, 2, 8, 128, NJH * 128])
        xv = lambda ap: ap.rearrange("(c p) t -> p c t", p=128)
        allx = [r for row in self.Rx for r in row]
        if 1 in stages:
            d_x = self.inp("xT", [D, T])
            self.d_evw_fm = self.inp("evw_fm", [12, 128, 1024])
            self.d_evw_tm = self.inp("evw_tm", [2, 128, 4096])
            self.d_wsT = self.inp("wsT", [128, 1024])
            self.d_bias_t = self.inp("bias_t", [128, 512])
            self.d_cs_e = self.inp("cs_e", [2, 128, T])
            for c in range(8):
                self.dma("sp", self.xres[:, c, :], d_x[c * 128:(c + 1) * 128, :], [], self.Rx[c])
            self.ffn(0, V_FFN + 0)
            kv_e = self.make_kv("e", [256] * 4 if self.fused else [1024])
            self.even_prep(V_FFN + 8, kv_e)
        if 2 in stages:
            self.d_evwo = self.inp("evwo", [8, 128, 1024])
            self.d_odw_fm = self.inp("odw_fm", [8, 128, 1024])
            self.d_odw_kpe = self.inp("odw_kpe", [128, 256])
            self.d_odw_gv = self.inp("odw_gv", [128, 1024])
            self.d_wuq = self.inp("wuq", [128, 1536])
            self.d_wkp = self.inp("wkp", [128, 768])
            self.d_wvm = self.inp("wvm", [128, 512])
            self.d_cs_m = self.inp("cs_m", [2, 128, T])
            self.d_cs_g = self.inp("cs_g", [2, 128, T])
            if self.fused:
                kvall_e = self.gather(kv_e)
            else:
                kvall_e = KVScr([1024])
                kvall_e.gat_t[0] = self.inp("kvall_e", [4 * 1024, T], BF16)
                self.P.barrier()
                d_xs = self.inp("xs_in", [D, T])
                d_q = self.inp("q_in", [512, T], BF16)
                d_m = self.inp("m_in", [512, T], BF16)
                for c in range(8):
                    self.dma("sp", self.xres[:, c, :], d_xs[c * 128:(c + 1) * 128, :], [], self.Rx[c])
                qT = self.fixed(self.off_qT(False), 4 * T).rearrange("p (c t) -> p c t", c=4)
                mixT = self.fixed(self.off_mixT(), 8 * T).rearrange("p (c t) -> p c t", c=8)
                self.dma("sp", qT, xv(d_q), [], [self.R("qT")])
                self.dma("sp", mixT[:, 0:4, :], xv(d_m), [], [self.R("mixT")])
            self.even_attn(kvall_e)
            self.out_proj(False, self.d_evwo)
            self.ffn(1, V_FFN + 16)
            self.ffn(2, V_FFN + 24)
            kv_o = self.make_kv("o", [192] * 4 + [128] + [256] * 2 + [128] if self.fused else [1536])
            self.odd_prep(V_FFN + 32, kv_o)
        if 3 in stages:
            self.d_odwo = self.inp("odwo", [8, 128, 1024])
            if self.fused:
                kvall_o = self.gather(kv_o)
            else:
                kvall_o = KVScr([1536])
                kvall_o.gat_t[0] = self.inp("kvall_o", [4 * 1536, T], BF16)
                self.P.barrier()
                d_xs = self.inp("xs_in", [D, T])
                d_q = self.inp("qo_in", [12 * 128, T], BF16)
                for c in range(8):
                    self.dma("sp", self.xres[:, c, :], d_xs[c * 128:(c + 1) * 128, :], [], self.Rx[c])
                qTo = self.fixed(self.off_qT(True), 12 * T).rearrange("p (c t) -> p c t", c=12)
                self.dma("sp", qTo, xv(d_q), [], [self.R("qT")])
            self.odd_attn(kvall_o)
            self.out_proj(True, self.d_odwo)
            self.ffn(3, V_FFN + 40)
        self.P.barrier()
        last = max(stages)
        if last == 3:
            d_o = self.outp("outT", [D, T])
            for c in range(8):
                self.dma("sp", d_o[c * 128:(c + 1) * 128, :], self.xres[:, c, :], self.Rx[c], [Res()])
        else:
            d_o = self.outp("xs_out", [D, T])
            for c in range(8):
                self.dma("sp", d_o[c * 128:(c + 1) * 128, :], self.xres[:, c, :], self.Rx[c], [Res()])
            if last == 1:
                qT = self.fixed(self.off_qT(False), 4 * T).rearrange("p (c t) -> p c t", c=4)
                mixT = self.fixed(self.off_mixT(), 8 * T).rearrange("p (c t) -> p c t", c=8)
                self.dma("sp", xv(self.outp("q_out", [512, T], BF16)), qT, [self.R("qT")], [Res()])
                self.dma("sp", xv(self.outp("m_out", [512, T], BF16)), mixT[:, 0:4, :], [self.R("mixT")], [Res()])
            else:
                qTo = self.fixed(self.off_qT(True), 12 * T).rearrange("p (c t) -> p c t", c=12)
                self.dma("sp", xv(self.outp("qo_out", [12 * 128, T], BF16)), qTo, [self.R("qT")], [Res()])
        self.P.emit()
        self.st.close()
        return self.nc

    def make_kv(self, tag, segs):
        T = self.cfg.T
        kv = KVScr(segs)
        for i, n in enumerate(segs):
            if self.fused:
                kv.loc_t[i] = self.nc.dram_tensor("kv%s%d" % (tag, i), [n, T], BF16).ap()
                kv.gat_t[i] = self.nc.dram_tensor("kvall%s%d" % (tag, i), [4 * n, T], BF16).ap()
            else:
                kv.loc_t[i] = self.outp("kv_" + tag, [n, T], BF16)
        return kv

    def gather(self, kv):
        for i in range(len(kv.segs)):
            o = self.P.op("pool", (lambda a, b_: (lambda e: e.collective_compute(
                "AllGather", ALU.bypass, replica_groups=[[0, 1, 2, 3], [4, 5, 6, 7]],
                ins=[a.opt()], outs=[b_.opt()])))(kv.loc_t[i], kv.gat_t[i]),
                list(self.kvres), [self.R("kvall")], dma=True, inc=1)
            o.cc = True
        return kv


def prep_shared(inp):
    f = lambda a: np.ascontiguousarray(np.asarray(a, dtype=np.float32))
    g = {k: f(v) for k, v in inp.items() if k != "x"}
    out = {}
    wgu = []
    wd = []
    for l in range(2):
        for nm in ("ffn1", "ffn2"):
            w = g[nm + "_w_gu"][l]
            wgu.append(w.reshape(8, 128, 2, NJ, 128).transpose(3, 1, 0, 2, 4).reshape(NJ, 128, 2048))
            w = g[nm + "_w_down"][l]
            wd.append(w.reshape(2, NJH, 128, 8, 128).transpose(0, 3, 2, 1, 4).reshape(2, 8, 128, NJH * 128))
    out["wgu"] = np.ascontiguousarray(np.stack(wgu))
    out["wd"] = np.ascontiguousarray(np.stack(wd))
    w = g["ev_w_in"][0]
    cols = [np.arange(i * 128, (i + 1) * 128) for i in range(4)]
    cols += [np.arange(1024 + h * 128, 1024 + (h + 1) * 128) for h in range(4)]
    cols += [np.arange(1536 + h * 128, 1536 + (h + 1) * 128) for h in range(4)]
    out["evw_fm"] = np.stack([_chunk_fm(w, c) for c in cols])
    out["evw_tm"] = np.stack([_chunk_fm(w, np.arange(512, 1024)), _chunk_fm(w, np.arange(2048, 2560))])
    out["evwo"] = np.stack([_chunk_fm(g["ev_w_out"][0], np.arange(dc * 128, (dc + 1) * 128)) for dc in range(8)])
    out["wsT"] = np.ascontiguousarray(g["ev_w_s"][0].transpose(2, 0, 1).reshape(128, 1024))
    bs = g["ev_b_s"][0]
    bt = np.zeros((128, 4, 128), np.float32)
    for cg in range(4):
        bt[:64, cg, :] = bs[2 * cg][None, :]
        bt[64:, cg, :] = bs[2 * cg + 1][None, :]
    out["bias_t"] = bt.reshape(128, 512)
    w = g["od_w_in"][0]
    cols = [np.arange(0, 128), np.arange(128, 256), np.arange(256, 384)]
    cols += [np.arange(416 + i * 128, 416 + (i + 1) * 128) for i in range(4)]
    cols += [np.arange(928, 1056)]
    out["odw_fm"] = np.stack([_chunk_fm(w, c) for c in cols])
    out["odw_kpe"] = _chunk_fm(w, np.arange(384, 416))
    out["odw_gv"] = _chunk_fm(w, np.arange(1056, 1184))
    out["wuq"] = np.ascontiguousarray(g["od_w_uq"][0].reshape(2, 128, 768).transpose(1, 0, 2).reshape(128, 1536))
    wukv = g["od_w_ukv"][0]
    wkp = np.zeros((128, 8, 96), np.float32)
    wvm = np.zeros((128, 8, 64), np.float32)
    for h in range(8):
        wkp[:, h, :64] = wukv[:, h * 128: h * 128 + 64]
        wvm[:, h, :] = wukv[:, h * 128 + 64: h * 128 + 128]
    out["wkp"] = wkp.reshape(128, 768)
    out["wvm"] = wvm.reshape(128, 512)
    out["odwo"] = np.stack([_chunk_fm(g["od_w_out"][0], np.arange(dc * 128, (dc + 1) * 128)) for dc in range(8)])
    vecs = np.zeros((128, NVEC), np.float32)
    norms = [g["ffn1_norm"][0], g["ev_norm"][0], g["ffn2_norm"][0], g["ffn1_norm"][1], g["od_norm"][0], g["ffn2_norm"][1]]
    for i, nv in enumerate(norms):
        vecs[:, V_FFN + 8 * i: V_FFN + 8 * i + 8] = nv.reshape(8, 128).T
    vecs[:, V_EQ] = np.tile(g["ev_q_norm"][0], 2)
    vecs[:, V_EK] = np.tile(g["ev_k_norm"][0], 2)
    vecs[:, V_CQ:V_CQ + 2] = g["od_cq_norm"][0].reshape(2, 128).T
    vecs[:, V_CKV] = g["od_ckv_norm"][0]
    vecs[:96, V_MQ] = g["od_mla_q_norm"][0]
    vecs[:96, V_MK] = g["od_mla_k_norm"][0]
    vecs[:, V_GQ] = np.tile(g["od_gqa_q_norm"][0], 2)
    vecs[:, V_GK] = np.tile(g["od_gqa_k_norm"][0], 2)
    out["vecs"] = vecs
    rows = np.zeros((1, NROW), np.float32)
    rows[0, R_SGU:R_SGU + 512] = g["ev_sgu_norm"][0].reshape(512)
    rows[0, R_SUB:R_SUB + 128] = g["ev_sub_norm"][0]
    rows[0, R_LAM:R_LAM + 256] = np.concatenate([g["ev_lam_q1"][0], g["ev_lam_k1"][0], g["ev_lam_q2"][0], g["ev_lam_k2"][0]])
    out["rows"] = rows
    return out


_NP = {F32: np.float32, BF16: ml_dtypes.bfloat16}


def run_stage(cfg, stages, fused, shared, percore, runner=None):
    b = Builder(cfg, stages, fused)
    nc = b.build()
    in_maps = []
    for c in range(8):
        m = {}
        for name, (shape, dt) in b.din.items():
            a = percore[c][name] if name in percore[c] else shared[name]
            assert tuple(a.shape) == tuple(shape), (name, a.shape, shape)
            m[name] = np.ascontiguousarray(a, dtype=_NP[dt])
        in_maps.append(m)
    if runner is None:
        res = run_bass_kernel_spmd(nc, in_maps, core_ids=list(range(8)))
        return res.results
    return runner(nc, in_maps, {k: (s, _NP[d]) for k, (s, d) in b.dout.items()})


def run_model(inp, T=2048, TG=512, fused=False, runner=None):
    cfg = Cfg(T, TG)
    x = np.asarray(inp["x"], dtype=np.float32)
    B, S, _ = x.shape
    assert B == 2 and S == 4 * T
    shared = prep_shared(inp)
    xt = x.reshape(8, T, D)
    percore = []
    for c in range(8):
        cmat, cs_e, cs_m, cs_g = host_consts(T, S, c)
        percore.append({"xT": np.ascontiguousarray(xt[c].T), "cmat": cmat, "cs_e": cs_e, "cs_m": cs_m, "cs_g": cs_g})
    if fused:
        res = run_stage(cfg, (1, 2, 3), True, shared, percore, runner)
    else:
        def regroup(res, key):
            full = [np.concatenate([res[g * 4 + r][key] for r in range(4)], axis=0) for g in range(2)]
            return [full[c // 4] for c in range(8)]
        r1 = run_stage(cfg, (1,), False, shared, percore, runner)
        kva = regroup(r1, "kv_e")
        for c in range(8):
            percore[c].update({"kvall_e": kva[c], "xs_in": r1[c]["xs_out"], "q_in": r1[c]["q_out"], "m_in": r1[c]["m_out"]})
        r2 = run_stage(cfg, (2,), False, shared, percore, runner)
        kva = regroup(r2, "kv_o")
        for c in range(8):
            percore[c].update({"kvall_o": kva[c], "xs_in": r2[c]["xs_out"], "qo_in": r2[c]["qo_out"]})
        res = run_stage(cfg, (3,), False, shared, percore, runner)
    out = np.stack([np.asarray(res[c]["outT"], dtype=np.float32).T for c in range(8)])
    return np.ascontiguousarray(out.reshape(B, S, D))


def kernel(**inputs):
    return run_model(inputs, T=2048, TG=512, fused=True)
```

```python
import contextlib
import numpy as np
import ml_dtypes
import concourse.bass as bass
import concourse.mybir as mybir
from concourse.bass_utils import run_bass_kernel_spmd

F32 = mybir.dt.float32
BF16 = mybir.dt.bfloat16
AF = mybir.ActivationFunctionType
ALU = mybir.AluOpType
AX = mybir.AxisListType

D = 1024
DFF = 2816
NJ = 22
NJH = 11
EPS = 1e-6
THETA = 10000.0
GRID_W = 64
ENGS = ("pe", "act", "dve", "pool", "sp")


class Res:
    __slots__ = ("name", "w", "rs")

    def __init__(self, name=""):
        self.name = name
        self.w = None
        self.rs = []


class Op:
    __slots__ = ("eng", "fn", "deps", "pos", "dma", "signal", "sem", "target", "prev_target", "inc", "cc")

    def __init__(self, eng, fn, dma, inc):
        self.eng = eng
        self.fn = fn
        self.dma = dma
        self.deps = []
        self.pos = -1
        self.signal = False
        self.sem = None
        self.target = 0
        self.prev_target = 0
        self.inc = inc
        self.cc = False


class Prog:
    NDMA_SEMS = 8

    def __init__(self, nc):
        self.nc = nc
        self.ops = {e: [] for e in ENGS}
        self.pending_barrier = {e: [] for e in ENGS}

    def op(self, eng, fn, reads=(), writes=(), dma=False, inc=16):
        o = Op(eng, fn, dma, inc)
        o.pos = len(self.ops[eng])
        deps = set(self.pending_barrier[eng])
        self.pending_barrier[eng] = []
        rawset = set()
        for r in reads:
            if r.w is not None:
                deps.add(r.w)
                rawset.add(r.w)
        for w in writes:
            if w.w is not None:
                deps.add(w.w)
            for rd in w.rs:
                deps.add(rd)
        best = {}
        red = []
        for d in deps:
            if d.dma:
                red.append(d)
            elif d.eng not in best or best[d.eng].pos < d.pos:
                best[d.eng] = d
        red.extend(best.values())
        for d in red:
            if d is o:
                continue
            if d.dma or o.dma or d.eng != eng:
                o.deps.append(d)
                d.signal = True
            elif eng != "pe" and (o.pos - d.pos) <= 3 and d in rawset:
                o.deps.append(d)
                d.signal = True
        for r in reads:
            r.rs.append(o)
        for w in writes:
            w.w = o
            w.rs = []
        self.ops[eng].append(o)
        return o

    def barrier(self):
        lasts = []
        for e in ENGS:
            comp = [o for o in self.ops[e] if not o.dma]
            if comp:
                lasts.append(comp[-1])
            dmas = [o for o in self.ops[e] if o.dma and not o.cc]
            lasts.extend(dmas[-self.NDMA_SEMS:])
            lasts.extend([o for o in self.ops[e] if o.cc])
        for e in ENGS:
            self.pending_barrier[e] = list(lasts)

    def emit(self):
        nc = self.nc
        with contextlib.ExitStack() as st:
            esem = {e: st.enter_context(nc.semaphore("s_" + e)) for e in ENGS}
            dsem = {}
            for q in ENGS:
                nd = sum(1 for o in self.ops[q] if o.dma and not o.cc)
                if nd:
                    dsem[q] = [st.enter_context(nc.semaphore("d_%s%d" % (q, i)))
                               for i in range(min(nd, self.NDMA_SEMS))]
            for e in ENGS:
                c = 0
                rr = 0
                nsem = len(dsem.get(e, []))
                tot = [0] * max(nsem, 1)
                for o in self.ops[e]:
                    if o.cc:
                        o.sem = st.enter_context(nc.semaphore("cc_%d" % o.pos))
                        o.prev_target = 0
                        o.target = o.inc
                    elif o.dma:
                        o.sem = dsem[e][rr]
                        o.prev_target = tot[rr]
                        tot[rr] += o.inc
                        o.target = tot[rr]
                        rr = (rr + 1) % nsem
                    elif o.signal:
                        c += 1
                        o.sem = esem[e]
                        o.target = c
            block = st.enter_context(nc.Block())

            def run(e, eng):
                seen = {}
                for o in self.ops[e]:
                    waits = {}
                    for d in o.deps:
                        key = id(d.sem)
                        if seen.get(key, 0) >= d.target:
                            continue
                        if key not in waits or waits[key][1] < d.target:
                            waits[key] = (d.sem, d.target)
                    if o.dma and o.prev_target > 0:
                        key = id(o.sem)
                        if seen.get(key, 0) < o.prev_target:
                            if key not in waits or waits[key][1] < o.prev_target:
                                waits[key] = (o.sem, o.prev_target)
                    for key, (s, v) in waits.items():
                        eng.wait_ge(s, v)
                        seen[key] = v
                    ins = o.fn(eng)
                    if o.dma:
                        ins.then_inc(o.sem, o.inc)
                    elif o.signal:
                        ins.then_inc(o.sem, 1)
                if e in dsem or any(o.cc for o in self.ops[e]):
                    tot = {}
                    for o in self.ops[e]:
                        if o.dma:
                            tot[id(o.sem)] = (o.sem, o.target)
                    for key, (s, v) in tot.items():
                        if seen.get(key, 0) < v:
                            eng.wait_ge(s, v)

            block.tensor(lambda eng: run("pe", eng))
            block.scalar(lambda eng: run("act", eng))
            block.vector(lambda eng: run("dve", eng))
            block.gpsimd(lambda eng: run("pool", eng))
            block.sync(lambda eng: run("sp", eng))


def _chunk_fm(w, cols):
    sub = w[:, cols]
    m = sub.shape[1]
    return np.ascontiguousarray(sub.reshape(8, 128, m).transpose(1, 0, 2).reshape(128, 8 * m))


def _rope_inv(dim):
    return (np.float32(THETA) ** (-np.arange(0, dim, 2, dtype=np.float32) / np.float32(dim))).astype(np.float32)


def host_consts(T, S, core):
    r = core % 4
    pos = (r * T + np.arange(T)).astype(np.int32)
    eye = np.eye(128, dtype=np.float32)
    ones = np.ones((128, 128), np.float32)
    bd = np.zeros((128, 128), np.float32)
    bd[:64, :64] = 1
    bd[64:, 64:] = 1
    ones96 = np.zeros((128, 128), np.float32)
    ones96[:96, :96] = 1

    def rot(blocks):
        m = np.zeros((128, 128), np.float32)
        for (b0, half) in blocks:
            for d in range(half):
                m[b0 + d + half, b0 + d] = -1.0
                m[b0 + d, b0 + d + half] = 1.0
        return m
    rm_e = rot([(0, 32), (64, 32)])
    rm_mla = rot([(64, 16)])
    rm_gqa = rot([(0, 16), (32, 16), (64, 16), (96, 16)])
    sel = np.zeros((128, 128), np.float32)
    for i in range(32):
        sel[i, 64 + i] = 1.0
    cmat = np.concatenate([eye, ones, bd, ones96, rm_e, rm_mla, rm_gqa, sel], axis=1)

    def angles(p, dim):
        inv = _rope_inv(dim)
        return p.astype(np.float32)[:, None] * inv[None, :]
    a = angles(pos, 64)
    idx = (np.arange(128) % 64) % 32
    cs_e = np.stack([np.cos(a)[:, idx].T, np.sin(a)[:, idx].T]).astype(np.float32)
    a = angles(pos, 32)
    cs_m = np.zeros((2, 128, T), np.float32)
    cs_m[0, :64] = 1.0
    idx = (np.arange(32)) % 16
    cs_m[0, 64:96] = np.cos(a)[:, idx].T
    cs_m[1, 64:96] = np.sin(a)[:, idx].T
    ar = angles(pos // GRID_W, 32)
    ac = angles(pos % GRID_W, 32)
    cs_g = np.zeros((2, 128, T), np.float32)
    for p in range(128):
        d = p % 64
        src = ar if d < 32 else ac
        f = (d % 32) % 16
        cs_g[0, p] = np.cos(src)[:, f]
        cs_g[1, p] = np.sin(src)[:, f]
    return cmat, cs_e, cs_m, cs_g


class Cfg:
    def __init__(self, T, TG):
        self.T = T
        self.S = 4 * T
        self.TG = TG
        self.NT = T // 128
        self.NG = T // TG
        self.TPG = TG // 128
        self.KT = self.S // 128
        self.arena = max(34 * T + 2048, 44000)


V_FFN = 0
V_EQ = 48
V_EK = 49
V_CQ = 50
V_CKV = 52
V_MQ = 53
V_MK = 54
V_GQ = 55
V_GK = 56
NVEC = 57
R_SGU = 0
R_SUB = 512
R_LAM = 640
NROW = 896
C_ID, C_ONES, C_BD, C_O96, C_RME, C_RMM, C_RMG, C_SEL = range(8)


class KVScr:
    def __init__(self, segs):
        self.segs = list(segs)
        self.b0 = [0]
        for n in self.segs:
            self.b0.append(self.b0[-1] + n)
        self.loc_t = [None] * len(self.segs)
        self.gat_t = [None] * len(self.segs)

    def _find(self, a, b):
        for i, n in enumerate(self.segs):
            if self.b0[i] <= a and b <= self.b0[i + 1]:
                return i
        raise AssertionError(("kv rows cross a segment", a, b))

    def loc(self, a, b):
        i = self._find(a, b)
        return self.loc_t[i][a - self.b0[i]: b - self.b0[i], :]

    def gat(self, r, a, b):
        i = self._find(a, b)
        n = self.segs[i]
        return self.gat_t[i][r * n + a - self.b0[i]: r * n + b - self.b0[i], :]


class Builder:
    def __init__(self, cfg, stages, fused):
        self.cfg = cfg
        self.stages = stages
        self.fused = fused
        self.nc = bass.Bass("TRN2", target_bir_lowering=False)
        self.P = Prog(self.nc)
        self.st = contextlib.ExitStack()
        self.din = {}
        self.dout = {}
        self.res = {}

    def R(self, name):
        if name not in self.res:
            self.res[name] = Res(name)
        return self.res[name]

    def inp(self, name, shape, dt=F32):
        t = self.nc.dram_tensor(name, list(shape), dt, kind="ExternalInput")
        self.din[name] = (tuple(shape), dt)
        return t.ap()

    def outp(self, name, shape, dt=F32):
        t = self.nc.dram_tensor(name, list(shape), dt, kind="ExternalOutput")
        self.dout[name] = (tuple(shape), dt)
        return t.ap()

    def sb(self, name, shape, dt):
        return self.st.enter_context(self.nc.sbuf_tensor(name, list(shape), dt))

    def mm(self, out, lhsT, rhs, start, stop, reads, writes):
        self.P.op("pe", lambda e: e.matmul(out, lhsT=lhsT, rhs=rhs, start=start, stop=stop),
                  reads, writes)

    def act(self, out, in_, func, reads, writes, scale=1.0, bias=None, accum_out=None, eng="act"):
        kw = {}
        if bias is not None:
            kw["bias"] = bias
        if accum_out is not None:
            kw["accum_out"] = accum_out
        self.P.op("act", lambda e: e.activation(out=out, in_=in_, func=func, scale=scale, **kw),
                  reads, writes)

    def tt(self, eng, out, in0, in1, op, reads, writes):
        self.P.op(eng, lambda e: e.tensor_tensor(out=out, in0=in0, in1=in1, op=op), reads, writes)

    def stt(self, eng, out, in0, scalar, in1, op0, op1, reads, writes):
        self.P.op(eng, lambda e: e.scalar_tensor_tensor(out=out, in0=in0, scalar=scalar, in1=in1,
                                                        op0=op0, op1=op1), reads, writes)

    def ts(self, eng, out, in0, s1, op0, reads, writes, s2=None, op1=None):
        if op1 is None:
            self.P.op(eng, lambda e: e.tensor_scalar(out=out, in0=in0, scalar1=s1, scalar2=None, op0=op0),
                      reads, writes)
        else:
            self.P.op(eng, lambda e: e.tensor_scalar(out=out, in0=in0, scalar1=s1, scalar2=s2, op0=op0,
                                                     op1=op1), reads, writes)

    def cp(self, eng, out, in_, reads, writes):
        if eng == "act":
            self.P.op("act", lambda e: e.copy(out=out, in_=in_), reads, writes)
        else:
            self.P.op(eng, lambda e: e.tensor_copy(out=out, in_=in_), reads, writes)

    def recip(self, out, in_, reads, writes):
        self.P.op("dve", lambda e: e.reciprocal(out=out, in_=in_), reads, writes)

    def dma(self, q, out, in_, reads, writes):
        self.P.op(q, lambda e: e.dma_start(out=out, in_=in_), reads, writes, dma=True)

    def arena_reset(self, lo=0, hi=None):
        self.a_lo = lo
        self.a_hi = self.cfg.arena if hi is None else hi

    def take(self, n, dt=BF16):
        nb = n * (2 if dt == F32 else 1)
        nb = (nb + 15) // 16 * 16
        off = self.a_lo
        self.a_lo += nb
        assert self.a_lo <= self.a_hi, ("arena overflow", self.a_lo, self.a_hi)
        ap = self.arena[:, off:off + n * (2 if dt == F32 else 1)]
        if dt == F32:
            ap = ap.bitcast(F32)
        return ap

    def fixed(self, off, n):
        return self.arena[:, off:off + n]

    def setup(self):
        cfg = self.cfg
        T = cfg.T
        self.xres = self.sb("xres", [128, 8, T], F32)
        self.arena = self.sb("arena", [128, cfg.arena], BF16)[:]
        self.cmat = self.sb("cmat_sb", [128, 8 * 128], BF16)
        self.vecs = self.sb("vecs_sb", [128, NVEC], F32)
        self.epsc = self.sb("epsc", [128, 1], F32)
        self.onescol = self.sb("onescol", [128, 1], BF16)
        self.bank = [self.st.enter_context(self.nc.psum_tensor("bank%d" % i, [128, 512], F32))
                     for i in range(8)]
        self.bres = [self.R("bank%d" % i) for i in range(8)]
        self.Rx = [[self.R("x_%d_%d" % (c, g)) for g in range(cfg.NG)] for c in range(8)]
        d_cmat = self.inp("cmat", [128, 8 * 128])
        d_vecs = self.inp("vecs", [128, NVEC])
        self.d_rows = self.inp("rows", [1, NROW])
        self.dma("pool", self.cmat[:], d_cmat, [], [self.R("cmat")])
        self.dma("sp", self.vecs[:], d_vecs, [], [self.R("vecs")])
        self.P.op("dve", lambda e: e.memset(self.epsc[:], EPS), [], [self.R("epsc")])
        self.P.op("dve", lambda e: e.memset(self.onescol[:], 1.0), [], [self.R("onescol")])

    def cm(self, idx, rows=128, cols=128):
        return self.cmat[0:rows, idx * 128: idx * 128 + cols]

    def norm_to_hT(self, hT, vcol, tmp_sq, tmp_rstd, tmp_sd):
        cfg = self.cfg
        T, TG, NG = cfg.T, cfg.TG, cfg.NG
        Rh = self.R("hT")
        Rrs = self.R("rstd")
        for c in range(8):
            sq = tmp_sq[c % 2]
            Rsq = self.R("sq%d" % (c % 2))
            self.act(sq, self.xres[:, c, :], AF.Square, self.Rx[c], [Rsq])
            for g in range(NG):
                self.mm(self.bank[g][:, 0:TG], self.cm(C_ONES), sq[:, g * TG:(g + 1) * TG],
                        c == 0, c == 7, [Rsq, self.R("cmat")], [self.bres[g]])
        for g in range(NG):
            sl = slice(g * TG, (g + 1) * TG)
            self.act(tmp_sd[:, sl], self.bank[g][:, 0:TG], AF.Sqrt, [self.bres[g], self.R("epsc")],
                     [self.R("sd")], scale=1.0 / D, bias=self.epsc[:])
        self.recip(tmp_rstd, tmp_sd, [self.R("sd")], [Rrs])
        for c in range(8):
            self.stt("dve", hT[:, c, :], self.xres[:, c, :], self.vecs[:, vcol + c: vcol + c + 1], tmp_rstd,
                     ALU.mult, ALU.mult, self.Rx[c] + [Rrs, self.R("vecs")], [Rh])

    def ffn(self, fi, vcol):
        cfg = self.cfg
        T, TG, NG = cfg.T, cfg.TG, cfg.NG
        P = self.P
        P.barrier()
        self.arena_reset()
        hT = self.take(8 * T).rearrange("p (c t) -> p c t", c=8)
        actA = self.take(NJH * T).rearrange("p (j t) -> p j t", j=NJH)
        wg = [self.take(2048).rearrange("p (c m) -> p c m", c=8) for _ in range(3)]
        wdb = [self.take(NJH * 128).rearrange("p (j m) -> p j m", j=NJH) for _ in range(3)]
        rstd = self.take(T, F32)
        sd = self.take(T, F32)
        sq = [self.take(T) for _ in range(2)]
        sg = [self.take(TG, F32) for _ in range(2)]
        Rwg = [self.R("wg%d" % i) for i in range(3)]
        Rwd = [self.R("wd%d" % i) for i in range(3)]
        Rsg = [self.R("sg%d" % i) for i in range(2)]
        Rh, Ra = self.R("hT"), self.R("actA")
        self.norm_to_hT(hT, vcol, sq, rstd, sd)
        d_wgu = self.d_wgu
        d_wd = self.d_wd
        step = 0
        wi = 0
        di = 0
        for half in range(2):
            for j in range(NJH):
                jj = half * NJH + j
                b = wi % 3
                wi += 1
                self.dma("pool", wg[b].rearrange("p c m -> p (c m)"), d_wgu[fi, jj], [], [Rwg[b]])
                for g in range(NG):
                    sl = slice(g * TG, (g + 1) * TG)
                    pb = (step % 2) * 2
                    step += 1
                    for c in range(8):
                        self.mm(self.bank[pb][:, 0:TG], wg[b][:, c, 0:128], hT[:, c, sl], c == 0, c == 7,
                                [Rwg[b], Rh], [self.bres[pb]])
                    for c in range(8):
                        self.mm(self.bank[pb + 1][:, 0:TG], wg[b][:, c, 128:256], hT[:, c, sl], c == 0, c == 7,
                                [Rwg[b], Rh], [self.bres[pb + 1]])
                    s = sg[g % 2]
                    self.act(s, self.bank[pb][:, 0:TG], AF.Silu, [self.bres[pb]], [Rsg[g % 2]])
                    self.tt("dve", actA[:, j, sl], s, self.bank[pb + 1][:, 0:TG], ALU.mult,
                            [Rsg[g % 2], self.bres[pb + 1]], [Ra])
            for dc in range(8):
                b = di % 3
                di += 1
                self.dma("pool", wdb[b].rearrange("p j m -> p (j m)"), d_wd[fi, half, dc], [], [Rwd[b]])
                for g in range(NG):
                    sl = slice(g * TG, (g + 1) * TG)
                    pb = 4 + (step % 2)
                    step += 1
                    for j in range(NJH):
                        self.mm(self.bank[pb][:, 0:TG], wdb[b][:, j, :], actA[:, j, sl], j == 0, j == NJH - 1,
                                [Rwd[b], Ra], [self.bres[pb]])
                    self.stt("dve", self.xres[:, dc, sl], self.bank[pb][:, 0:TG], 0.5, self.xres[:, dc, sl],
                             ALU.mult, ALU.add, [self.bres[pb], self.Rx[dc][g]], [self.Rx[dc][g]])

    def nr_alloc(self):
        TG = self.cfg.TG
        self.nr = []
        for k in range(3):
            self.nr.append(dict(qf=self.take(TG, F32), sd=self.take(TG, F32), t2=self.take(TG, F32),
                                sqb=self.take(TG), qnb=self.take(TG)))
        self.nrk = 0
        self.nrq = []

    def nr_flush(self):
        while self.nrq:
            self._nr_advance()

    def _nr_advance(self):
        q = self.nrq
        for ent in list(q):
            if ent["stage"] == 2:
                ent["s3"]()
                q.remove(ent)
            elif ent["stage"] == 1:
                ent["s2"]()
                ent["stage"] = 2

    def normrope(self, ps, Rps, R, onesidx, nnorm, gcol, rmidx, cos_ap, sin_ap, Rtab, out, Rout, then=None):
        TG = self.cfg.TG
        i = self.nrk
        self.nrk += 1
        k = i % 3
        t = self.nr[k]
        Rn = lambda n: self.R("nr_%s%d" % (n, k))
        bs = 2 + i % 2
        br = 4 + i % 2
        qf, sd, t2, sqb, qnb = (t["qf"][0:R], t["sd"][0:R], t["t2"][0:R], t["sqb"][0:R], t["qnb"][0:R])
        self.cp("act", qf, ps, [Rps], [Rn("qf")])
        self.act(sqb, ps, AF.Square, [Rps], [Rn("sqb")])
        self.mm(self.bank[bs][0:R, 0:TG], self.cm(onesidx, R, R), sqb, True, True,
                [Rn("sqb"), self.R("cmat")], [self.bres[bs]])
        self.act(sd, self.bank[bs][0:R, 0:TG], AF.Sqrt, [self.bres[bs], self.R("epsc")], [Rn("sd")],
                 scale=1.0 / nnorm, bias=self.epsc[0:R, :])

        def s2():
            self.recip(sd, sd, [Rn("sd")], [Rn("sd")])
            self.stt("dve", qf, qf, self.vecs[0:R, gcol:gcol + 1], sd, ALU.mult, ALU.mult,
                     [Rn("qf"), Rn("sd"), self.R("vecs")], [Rn("qf")])
            self.cp("pool", qnb, qf, [Rn("qf")], [Rn("qnb")])
            self.mm(self.bank[br][0:R, 0:TG], self.cm(rmidx, R, R), qnb, True, True,
                    [Rn("qnb"), self.R("cmat")], [self.bres[br]])

        def s3():
            self.tt("dve", t2, self.bank[br][0:R, 0:TG], sin_ap, ALU.mult, [self.bres[br], Rtab], [Rn("t2")])
            self.tt("pool", qf, qf, cos_ap, ALU.mult, [Rn("qf"), Rtab], [Rn("qf")])
            self.tt("pool", out, qf, t2, ALU.add, [Rn("qf"), Rn("t2")], [Rout])
            if then is not None:
                then()
        self._nr_advance()
        self.nrq.append(dict(stage=1, s2=s2, s3=s3))

    def off_qT(self, odd):
        return self.cfg.arena - (12 if odd else 4) * self.cfg.T

    def off_mixT(self):
        return self.cfg.arena - 12 * self.cfg.T

    def off_otok(self, odd):
        return self.off_mixT() - (8 if odd else 4) * self.cfg.T

    def even_prep(self, vcol, kv_scr):
        cfg = self.cfg
        T, TG, NG, NT = cfg.T, cfg.TG, cfg.NG, cfg.NT
        P = self.P
        P.barrier()
        qT = self.fixed(self.off_qT(False), 4 * T).rearrange("p (c t) -> p c t", c=4)
        mixT = self.fixed(self.off_mixT(), 8 * T).rearrange("p (c t) -> p c t", c=8)
        Rq, Rm_, Rh = self.R("qT"), self.R("mixT"), self.R("hT")
        self.arena_reset(0, self.off_mixT())
        hT = self.take(8 * T).rearrange("p (c t) -> p c t", c=8)
        mark = self.a_lo
        rstd = self.take(T, F32)
        sd = self.take(T, F32)
        sq = [self.take(T) for _ in range(2)]
        self.norm_to_hT(hT, vcol, sq, rstd, sd)
        P.barrier()
        self.arena_reset(mark, self.off_mixT())
        cosT = self.take(T, F32)
        sinT = self.take(T, F32)
        self.nr_alloc()
        wb = [self.take(1024).rearrange("p (c m) -> p c m", c=8) for _ in range(3)]
        kst = [self.take(TG) for _ in range(4)]
        Rwb = [self.R("wb%d" % i) for i in range(3)]
        Rks = [self.R("kst%d" % i) for i in range(4)]
        Rtab = self.R("tab")
        self.kvres = []

        def Rkv_new():
            r = Res("kvw")
            self.kvres.append(r)
            return r
        self.dma("sp", cosT, self.d_cs_e[0], [], [Rtab])
        self.dma("sp", sinT, self.d_cs_e[1], [], [Rtab])
        step = 0
        ks = 0
        for ci in range(12):
            b = ci % 3
            self.dma("pool", wb[b].rearrange("p c m -> p (c m)"), self.d_evw_fm[ci], [], [Rwb[b]])
            kind, h = ("u", "q", "k")[ci // 4], ci % 4
            for g in range(NG):
                sl = slice(g * TG, (g + 1) * TG)
                pb = step % 2
                step += 1
                for c in range(8):
                    self.mm(self.bank[pb][:, 0:TG], wb[b][:, c, :], hT[:, c, sl], c == 0, c == 7,
                            [Rwb[b], Rh], [self.bres[pb]])
                ps = self.bank[pb][:, 0:TG]
                if kind == "u":
                    self.act(mixT[:, h, sl], ps, AF.Gelu_apprx_tanh, [self.bres[pb]], [Rm_])
                elif kind == "q":
                    self.normrope(ps, self.bres[pb], 128, C_BD, 64, V_EQ, C_RME, cosT[:, sl], sinT[:, sl], Rtab,
                                  qT[:, h, sl], Rq)
                else:
                    kb = ks % 4
                    ks += 1
                    self.normrope(ps, self.bres[pb], 128, C_BD, 64, V_EK, C_RME, cosT[:, sl], sinT[:, sl], Rtab,
                                  kst[kb], Rks[kb],
                                  then=(lambda h=h, sl=sl, kb=kb: self.dma(
                                      "sp", kv_scr.loc(h * 128, (h + 1) * 128)[:, sl], kst[kb], [Rks[kb]], [Rkv_new()])))
        self.nr_flush()
        P.barrier()
        self.arena_reset(mark, self.off_mixT())
        wtm = self.take(4096).rearrange("p (c m) -> p c m", c=8)
        wsT = self.take(1024).rearrange("p (g i) -> p g i", g=8)
        Gt = self.take(512, F32)
        biast = self.take(512, F32)
        vg = [self.take(512, F32) for _ in range(2)]
        sqv = self.take(512, F32)
        ss8 = [self.take(8, F32) for _ in range(2)]
        vc = [self.take(512) for _ in range(2)]
        tmpm = [self.take(512, F32) for _ in range(2)]
        vst = [self.take(512) for _ in range(2)]
        Rwtm, Rws, Rg = self.R("wtm"), self.R("wsT"), self.R("Gt")
        self.dma("pool", wtm.rearrange("p c m -> p (c m)").rearrange("p (a b) -> p a b", b=2048),
                 self.d_evw_tm[0].rearrange("p (a b) -> p a b", b=2048), [], [Rwtm])
        self.dma("pool", wsT.rearrange("p g i -> p (g i)"), self.d_wsT, [], [Rws])
        self.dma("sp", Gt, self.d_rows[0:1, R_SGU:R_SGU + 512].partition_broadcast(128), [], [Rg])
        self.dma("sp", biast, self.d_bias_t, [], [Rg])
        for tt_ in range(NT):
            k = tt_ % 2
            tsl = slice(tt_ * 128, (tt_ + 1) * 128)
            pb = 6 + k
            Rvg, Rss, Rvc, Rtm = (self.R("vg%d" % k), self.R("ss8%d" % k), self.R("vc%d" % k), self.R("tmpm%d" % k))
            for c in range(8):
                self.mm(self.bank[pb][:, :], hT[:, c, tsl], wtm[:, c, :], c == 0, c == 7, [Rh, Rwtm], [self.bres[pb]])
            self.act(vg[k], self.bank[pb][:, :], AF.Gelu_apprx_tanh, [self.bres[pb]], [Rvg])
            self.tt("dve", sqv, vg[k], vg[k], ALU.mult, [Rvg], [self.R("sqv")])
            self.P.op("dve", (lambda o, i: (lambda e: e.reduce_sum(out=o, in_=i, axis=AX.X)))(
                ss8[k], sqv.rearrange("p (g d) -> p g d", g=8)), [self.R("sqv")], [Rss])
            self.act(ss8[k], ss8[k], AF.Sqrt, [Rss, self.R("epsc")], [Rss], scale=1.0 / 64, bias=self.epsc[:])
            self.recip(ss8[k], ss8[k], [Rss], [Rss])
            vg3 = vg[k].rearrange("p (g d) -> p g d", g=8)
            self.tt("dve", vg3, vg3, ss8[k].unsqueeze(2).to_broadcast([128, 8, 64]), ALU.mult, [Rvg, Rss], [Rvg])
            self.tt("pool", vc[k], vg[k], Gt, ALU.mult, [Rvg, Rg], [Rvc])
            bm = 4 + k
            for g8 in range(8):
                po = (g8 % 2) * 64
                self.mm(self.bank[bm][po:po + 64, (g8 // 2) * 128:(g8 // 2) * 128 + 128],
                        vc[k][:, g8 * 64:(g8 + 1) * 64], wsT[:, g8, :], True, True, [Rvc, Rws], [self.bres[bm]])
            self.tt("dve", tmpm[k], self.bank[bm][:, :], biast, ALU.add, [self.bres[bm], Rg], [Rtm])
            self.tt("pool", mixT[:, 0:4, tsl], tmpm[k].rearrange("p (c i) -> p c i", c=4), mixT[:, 0:4, tsl],
                    ALU.mult, [Rtm, Rm_], [Rm_])
        self.dma("pool", wtm.rearrange("p c m -> p (c m)").rearrange("p (a b) -> p a b", b=2048),
                 self.d_evw_tm[1].rearrange("p (a b) -> p a b", b=2048), [], [Rwtm])
        vs = [kv_scr.loc(512 + 256 * s_, 768 + 256 * s_).rearrange("(h p) (t c) -> h p t c", h=2, c=128) for s_ in range(2)]
        for tt_ in range(NT):
            k = tt_ % 2
            tsl = slice(tt_ * 128, (tt_ + 1) * 128)
            pb = 6 + k
            Rvs = self.R("vst%d" % k)
            for c in range(8):
                self.mm(self.bank[pb][:, :], hT[:, c, tsl], wtm[:, c, :], c == 0, c == 7, [Rh, Rwtm], [self.bres[pb]])
            self.cp("act", vst[k], self.bank[pb][:, :], [self.bres[pb]], [Rvs])
            for s_ in range(2):
                self.dma("sp", vs[s_][:, :, tt_, :].rearrange("h p c -> p h c"),
                         vst[k][:, s_ * 256:(s_ + 1) * 256].rearrange("p (h c) -> p h c", h=2), [Rvs], [Rkv_new()])

    def attn_alloc(self, dvp1, n_pt=3):
        cfg = self.cfg
        self.Kb = [self.take(cfg.S) for _ in range(2)]
        self.Vb = [self.take(cfg.KT * dvp1).rearrange("p (k c) -> p k c", c=dvp1) for _ in range(2)]
        self.pT = [self.take(cfg.TG) for _ in range(n_pt)]
        self.RK = [self.R("Kb%d" % i) for i in range(2)]
        self.RV = [self.R("Vb%d" % i) for i in range(2)]
        self.RpT = [self.R("pT%d" % i) for i in range(n_pt)]
        self.maps = []
        for i in range(2):
            self.P.op("pool", (lambda v: (lambda e: e.memset(v, 1.0)))(self.Vb[i][:, :, dvp1 - 1:dvp1]),
                      [], [self.RV[i]])

    def attn_map(self, kb, krows, qap, Rq, scale, dvp1, acc_of_j, accres_of_j, first_of_bank, last_of_bank,
                 pre=None, post=None):
        self.maps.append(dict(kb=kb, krows=krows, qap=qap, Rq=Rq, scale=scale, dvp1=dvp1, acc=acc_of_j,
                              accres=accres_of_j, first=first_of_bank, last=last_of_bank, pre=pre, post=post))

    def run_attn(self):
        cfg = self.cfg
        TG, KT, TPG = cfg.TG, cfg.KT, cfg.TPG
        maps = self.maps
        tiles = [(mi, kt) for mi in range(len(maps)) for kt in range(KT)]
        N = len(tiles)
        delay = min(4, KT - 1)
        pending = []

        def rec_qk(i):
            mi, kt = tiles[i]
            m = maps[mi]
            if kt == 0 and m["pre"] is not None:
                m["pre"]()
            sb_ = i % 2
            self.mm(self.bank[sb_][:, 0:TG], self.Kb[m["kb"]][m["krows"], kt * 128:(kt + 1) * 128], m["qap"], True, True,
                    [self.RK[m["kb"]], m["Rq"]], [self.bres[sb_]])

        def rec_exp_pv(i):
            mi, kt = tiles[i]
            m = maps[mi]
            sb_ = i % 2
            pi = i % len(self.pT)
            self.act(self.pT[pi], self.bank[sb_][:, 0:TG], AF.Exp, [self.bres[sb_]], [self.RpT[pi]], scale=m["scale"])
            for j in range(TPG):
                self.mm(m["acc"](j), self.pT[pi][:, j * 128:(j + 1) * 128], self.Vb[m["kb"]][:, kt, 0:m["dvp1"]],
                        kt == 0 and m["first"](j), kt == KT - 1 and m["last"](j),
                        [self.RpT[pi], self.RV[m["kb"]]], [m["accres"](j)])
            if kt == KT - 1 and m["post"] is not None:
                pending.append((i + delay, m["post"]))

        rec_qk(0)
        for i in range(N):
            if i + 1 < N:
                rec_qk(i + 1)
            rec_exp_pv(i)
            while pending and pending[0][0] <= i:
                pending.pop(0)[1]()
        for _, p in pending:
            p()
        self.maps = []

    def even_attn(self, kvall):
        cfg = self.cfg
        T, TG, NG, NT, KT, TPG = cfg.T, cfg.TG, cfg.NG, cfg.NT, cfg.KT, cfg.TPG
        P = self.P
        P.barrier()
        lam_init = 0.8 - 0.6 * float(np.exp(-0.3 * 0))
        qT = self.fixed(self.off_qT(False), 4 * T).rearrange("p (c t) -> p c t", c=4)
        otok = self.fixed(self.off_otok(False), 4 * T).rearrange("p (t c) -> p t c", c=512)
        Rq, Ro = self.R("qT"), self.R("otok")
        self.arena_reset(0, self.off_otok(False))
        self.attn_alloc(129)
        lamt = self.take(256, F32)
        lp = self.take(128, F32)
        subg = self.take(128, F32)
        sm = self.take(8, F32)
        fin = [dict(r0=self.take(1, F32), r1=self.take(1, F32), O0=self.take(128, F32), od=self.take(128, F32),
                    ss=self.take(1, F32), junk=self.take(128)) for _ in range(2)]
        Rl = self.R("lam")
        self.dma("sp", lamt, self.d_rows[0:1, R_LAM:R_LAM + 256].partition_broadcast(128), [], [Rl])
        self.dma("sp", subg, self.d_rows[0:1, R_SUB:R_SUB + 128].partition_broadcast(128), [], [self.R("subg")])
        self.tt("dve", lp[:, 0:64], lamt[:, 0:64], lamt[:, 64:128], ALU.mult, [Rl], [self.R("lp")])
        self.tt("dve", lp[:, 64:128], lamt[:, 128:192], lamt[:, 192:256], ALU.mult, [Rl], [self.R("lp")])
        self.P.op("dve", lambda e: e.reduce_sum(out=sm[:, 0:2], in_=lp.rearrange("p (a d) -> p a d", a=2), axis=AX.X),
                  [self.R("lp")], [self.R("sm")])
        self.act(sm[:, 2:4], sm[:, 0:2], AF.Exp, [self.R("sm")], [self.R("sm2")])
        self.tt("dve", sm[:, 4:5], sm[:, 3:4], sm[:, 2:3], ALU.subtract, [self.R("sm2")], [self.R("sm3")])
        self.ts("dve", sm[:, 5:6], sm[:, 4:5], -lam_init, ALU.add, [self.R("sm3")], [self.R("neglam")])
        neglam = sm[:, 5:6]
        self.ts("dve", subg, subg, 1.0 - lam_init, ALU.mult, [self.R("subg")], [self.R("subg")])
        accsets = [(2, 3), (4, 5), (6, 7)]
        ai = 0
        fi_ = 0
        def kvload(h, kb):
            for r in range(4):
                self.dma("sp", self.Kb[kb][:, r * T:(r + 1) * T], kvall.gat(r, h * 128, (h + 1) * 128),
                         [self.R("kvall")], [self.RK[kb]])
                self.dma("sp", self.Vb[kb][:, r * NT:(r + 1) * NT, 0:128],
                         kvall.gat(r, 512 + h * 128, 512 + (h + 1) * 128).rearrange("p (t c) -> p t c", c=128),
                         [self.R("kvall")], [self.RV[kb]])

        def finalize(h, g, sets):
            nonlocal fi_
            if True:
                for j in range(TPG):
                    f = fin[fi_ % 2]
                    k = fi_ % 2
                    fi_ += 1
                    Rf = lambda n: self.R("fin_%s%d" % (n, k))
                    (a0, r0b), (a1, r1b) = [((self.bank[s[0]][:, j * 129:(j + 1) * 129], self.bres[s[0]]) if j < 3
                                             else (self.bank[s[1]][:, 0:129], self.bres[s[1]])) for s in sets]
                    self.recip(f["r0"], a0[:, 128:129], [r0b], [Rf("r0")])
                    self.recip(f["r1"], a1[:, 128:129], [r1b], [Rf("r1")])
                    self.ts("dve", f["O0"], a0[:, 0:128], f["r0"][:, 0:1], ALU.mult, [r0b, Rf("r0")], [Rf("O0")])
                    self.tt("dve", f["r1"], f["r1"], neglam, ALU.mult, [Rf("r1"), self.R("neglam")], [Rf("r1")])
                    self.stt("dve", f["od"], a1[:, 0:128], f["r1"][:, 0:1], f["O0"], ALU.mult, ALU.add,
                             [r1b, Rf("r1"), Rf("O0")], [Rf("od")])
                    self.act(f["junk"], f["od"], AF.Square, [Rf("od")], [Rf("junk"), Rf("ss")], accum_out=f["ss"])
                    self.act(f["ss"], f["ss"], AF.Sqrt, [Rf("ss"), self.R("epsc")], [Rf("ss")], scale=1.0 / 128,
                             bias=self.epsc[:])
                    self.recip(f["ss"], f["ss"], [Rf("ss")], [Rf("ss")])
                    tt_ = g * TPG + j
                    self.stt("dve", otok[:, tt_, h * 128:(h + 1) * 128], f["od"], f["ss"][:, 0:1], subg,
                             ALU.mult, ALU.mult, [Rf("od"), Rf("ss"), self.R("subg")], [Ro])

        for h in range(4):
            kb = h % 2
            for g in range(NG):
                sl = slice(g * TG, (g + 1) * TG)
                sets = []
                for m in range(2):
                    bA, bB = accsets[ai % 3]
                    ai += 1
                    sets.append((bA, bB))
                    rows = slice(m * 64, (m + 1) * 64)
                    self.attn_map(kb, rows, qT[rows, h, sl], Rq, 0.125, 129,
                                  lambda j, bA=bA, bB=bB: (self.bank[bA][:, j * 129:(j + 1) * 129] if j < 3
                                                           else self.bank[bB][:, 0:129]),
                                  lambda j, bA=bA, bB=bB: self.bres[bA] if j < 3 else self.bres[bB],
                                  lambda j: j == 0 or j == 3, lambda j: j == min(TPG, 3) - 1 or j == 3,
                                  pre=(lambda h=h, kb=kb: kvload(h, kb)) if (g == 0 and m == 0) else None,
                                  post=(lambda h=h, g=g, sets=list(sets): finalize(h, g, sets)) if m == 1 else None)
        self.run_attn()

    def out_proj(self, odd, d_wo):
        cfg = self.cfg
        T, TG, NG, NT, TPG = cfg.T, cfg.TG, cfg.NG, cfg.NT, cfg.TPG
        P = self.P
        P.barrier()
        ncol = 1024 if odd else 512
        otok = self.fixed(self.off_otok(odd), (8 if odd else 4) * T).rearrange("p (t c) -> p t c", c=ncol)
        mixT = self.fixed(self.off_mixT(), 8 * T).rearrange("p (c t) -> p c t", c=8)
        Ro, Rm_ = self.R("otok"), self.R("mixT")
        self.arena_reset(0, self.off_otok(odd))
        wo = [self.take(1024).rearrange("p (c m) -> p c m", c=8) for _ in range(3)]
        Rwo = [self.R("wo%d" % i) for i in range(3)]
        c0 = 0 if odd else 4
        step = 0
        for c in range(ncol // 128):
            for g in range(NG):
                pb = step % 2
                step += 1
                pst = self.bank[pb].bitcast(BF16)
                for j in range(TPG):
                    tt_ = g * TPG + j
                    self.P.op("pe", (lambda o, i: (lambda e: e.transpose(o, i, self.cm(C_ID))))(
                        pst[:, j * 128:(j + 1) * 128], otok[:, tt_, c * 128:(c + 1) * 128]),
                        [Ro, self.R("cmat")], [self.bres[pb]])
                eng = "act" if step % 2 == 0 else "dve"
                self.cp(eng, mixT[:, c0 + c, g * TG:(g + 1) * TG], pst[:, 0:TG], [self.bres[pb]], [Rm_])
        for dc in range(8):
            b = dc % 3
            self.dma("pool", wo[b].rearrange("p c m -> p (c m)"), d_wo[dc], [], [Rwo[b]])
            for g in range(NG):
                sl = slice(g * TG, (g + 1) * TG)
                pb = 2 + step % 2
                step += 1
                for hc in range(8):
                    self.mm(self.bank[pb][:, 0:TG], wo[b][:, hc, :], mixT[:, hc, sl], hc == 0, hc == 7,
                            [Rwo[b], Rm_], [self.bres[pb]])
                self.tt("dve", self.xres[:, dc, sl], self.bank[pb][:, 0:TG], self.xres[:, dc, sl], ALU.add,
                        [self.bres[pb], self.Rx[dc][g]], [self.Rx[dc][g]])

    def odd_prep(self, vcol, kv_scr):
        cfg = self.cfg
        T, TG, NG, NT, TPG = cfg.T, cfg.TG, cfg.NG, cfg.NT, cfg.TPG
        P = self.P
        P.barrier()
        oq = self.off_qT(True)
        qTm = self.fixed(oq, 8 * T).rearrange("p (c t) -> p c t", c=8)
        qTg = self.fixed(oq + 8 * T, 4 * T).rearrange("p (c t) -> p c t", c=4)
        Rq, Rh = self.R("qT"), self.R("hT")
        self.arena_reset(0, oq)
        hT = self.take(8 * T).rearrange("p (c t) -> p c t", c=8)
        mark = self.a_lo
        rstd = self.take(T, F32)
        sd = self.take(T, F32)
        sq = [self.take(T) for _ in range(2)]
        self.norm_to_hT(hT, vcol, sq, rstd, sd)
        self.kvres = []

        def Rkv_new():
            r = Res("kvw")
            self.kvres.append(r)
            return r
        P.barrier()
        self.arena_reset(mark, oq)
        cosG = [self.take(TG, F32) for _ in range(2)]
        sinG = [self.take(TG, F32) for _ in range(2)]
        self.nr_alloc()
        wfm = [self.take(1024).rearrange("p (c m) -> p c m", c=8) for _ in range(3)]
        wkpe = self.take(256).rearrange("p (c m) -> p c m", c=8)
        wuq = self.take(1536).rearrange("p (c m) -> p c m", c=2)
        wkp = self.take(768)
        wvm = self.take(512)
        cqf = [self.take(TG, F32) for _ in range(2)]
        cqs = [self.take(TG) for _ in range(2)]
        csd = self.take(TG, F32)
        cqn = self.take(2 * TG).rearrange("p (c t) -> p c t", c=2)
        ckvn = self.take(TG)
        kpeb = self.take(TG)
        kst = [self.take(TG) for _ in range(2)]
        vst = [self.take(512) for _ in range(2)]
        Rw, Rtab = self.R("odw"), self.R("tab")
        for i in range(3):
            self.dma("pool", wfm[i].rearrange("p c m -> p (c m)"), self.d_odw_fm[i], [], [Rw])
        self.dma("pool", wkpe.rearrange("p c m -> p (c m)"), self.d_odw_kpe, [], [Rw])
        self.dma("pool", wuq.rearrange("p c m -> p (c m)"), self.d_wuq, [], [Rw])
        self.dma("pool", wkp, self.d_wkp, [], [Rw])
        self.dma("pool", wvm, self.d_wvm, [], [Rw])
        vsm = [kv_scr.loc(896 + 256 * s_, 1152 + 256 * s_).rearrange("r (b x) -> (r b) x", b=2).rearrange(
            "(h p) (t c) -> h p t c", h=4, c=64) for s_ in range(2)]
        step = 0
        ks = 0
        vi = 0
        for g in range(NG):
            sl = slice(g * TG, (g + 1) * TG)
            Rtg = self.R("tabg%d" % (g % 2))
            cosT, sinT = cosG[g % 2], sinG[g % 2]
            self.dma("sp", cosT, self.d_cs_m[0][:, sl], [], [Rtg])
            self.dma("sp", sinT, self.d_cs_m[1][:, sl], [], [Rtg])
            for c2 in range(2):
                for c in range(8):
                    self.mm(self.bank[c2][:, 0:TG], wfm[c2][:, c, :], hT[:, c, sl], c == 0, c == 7, [Rw, Rh],
                            [self.bres[c2]])
                self.cp("act", cqf[c2], self.bank[c2][:, 0:TG], [self.bres[c2]], [self.R("cqf%d" % c2)])
                self.act(cqs[c2], self.bank[c2][:, 0:TG], AF.Square, [self.bres[c2]], [self.R("cqs%d" % c2)])
            for c2 in range(2):
                self.mm(self.bank[6][:, 0:TG], self.cm(C_ONES), cqs[c2], c2 == 0, c2 == 1,
                        [self.R("cqs%d" % c2), self.R("cmat")], [self.bres[6]])
            self.act(csd, self.bank[6][:, 0:TG], AF.Sqrt, [self.bres[6], self.R("epsc")], [self.R("csd")],
                     scale=1.0 / 256, bias=self.epsc[:])
            self.recip(csd, csd, [self.R("csd")], [self.R("csd")])
            for c2 in range(2):
                self.stt("dve", cqn[:, c2, :], cqf[c2], self.vecs[:, V_CQ + c2:V_CQ + c2 + 1], csd, ALU.mult, ALU.mult,
                         [self.R("cqf%d" % c2), self.R("csd"), self.R("vecs")], [self.R("cqn")])
            for c in range(8):
                self.mm(self.bank[7][:, 0:TG], wfm[2][:, c, :], hT[:, c, sl], c == 0, c == 7, [Rw, Rh], [self.bres[7]])
            self.cp("act", cqf[0], self.bank[7][:, 0:TG], [self.bres[7]], [self.R("cqf0")])
            self.act(cqs[0], self.bank[7][:, 0:TG], AF.Square, [self.bres[7]], [self.R("cqs0")])
            self.mm(self.bank[6][:, 0:TG], self.cm(C_ONES), cqs[0], True, True, [self.R("cqs0"), self.R("cmat")],
                    [self.bres[6]])
            self.act(csd, self.bank[6][:, 0:TG], AF.Sqrt, [self.bres[6], self.R("epsc")], [self.R("csd")],
                     scale=1.0 / 128, bias=self.epsc[:])
            self.recip(csd, csd, [self.R("csd")], [self.R("csd")])
            self.stt("dve", ckvn, cqf[0], self.vecs[:, V_CKV:V_CKV + 1], csd, ALU.mult, ALU.mult,
                     [self.R("cqf0"), self.R("csd"), self.R("vecs")], [self.R("ckvn")])
            for c in range(8):
                self.mm(self.bank[7][0:32, 0:TG], wkpe[:, c, :], hT[:, c, sl], c == 0, c == 7, [Rw, Rh], [self.bres[7]])
            self.cp("act", kpeb[0:32], self.bank[7][0:32, 0:TG], [self.bres[7]], [self.R("kpeb")])
            for h in range(8):
                pb = step % 2
                step += 1
                for c2 in range(2):
                    self.mm(self.bank[pb][0:96, 0:TG], wuq[:, c2, h * 96:(h + 1) * 96], cqn[:, c2, :], c2 == 0, c2 == 1,
                            [Rw, self.R("cqn")], [self.bres[pb]])
                self.normrope(self.bank[pb][0:96, 0:TG], self.bres[pb], 96, C_O96, 96, V_MQ, C_RMM, cosT[0:96], sinT[0:96],
                              Rtg, qTm[0:96, h, sl], Rq)
            for h in range(8):
                pb = step % 2
                step += 1
                self.mm(self.bank[pb][0:96, 0:TG], wkp[:, h * 96:(h + 1) * 96], ckvn, True, False,
                        [Rw, self.R("ckvn")], [self.bres[pb]])
                self.mm(self.bank[pb][0:96, 0:TG], self.cm(C_SEL, 32, 96), kpeb[0:32], False, True,
                        [self.R("cmat"), self.R("kpeb")], [self.bres[pb]])
                kb = ks % 2
                ks += 1
                self.normrope(self.bank[pb][0:96, 0:TG], self.bres[pb], 96, C_O96, 96, V_MK, C_RMM, cosT[0:96], sinT[0:96],
                              Rtg, kst[kb][0:96], self.R("kst%d" % kb),
                              then=(lambda h=h, sl=sl, kb=kb: self.dma(
                                  "sp", kv_scr.loc(h * 96, (h + 1) * 96)[:, sl], kst[kb][0:96], [self.R("kst%d" % kb)],
                                  [Rkv_new()])))
            for j in range(TPG):
                tt_ = g * TPG + j
                k = vi % 2
                vi += 1
                self.mm(self.bank[6][:, :], ckvn[:, j * 128:(j + 1) * 128], wvm, True, True, [self.R("ckvn"), Rw],
                        [self.bres[6]])
                self.cp("act", vst[k], self.bank[6][:, :], [self.bres[6]], [self.R("vst%d" % k)])
                for s_ in range(2):
                    self.dma("sp", vsm[s_][:, :, tt_, :].rearrange("h p c -> p h c"),
                             vst[k][:, s_ * 256:(s_ + 1) * 256].rearrange("p (h c) -> p h c", h=4),
                             [self.R("vst%d" % k)], [Rkv_new()])
        self.nr_flush()
        P.barrier()
        self.arena_reset(mark, oq)
        cosT = self.take(T, F32)
        sinT = self.take(T, F32)
        self.nr_alloc()
        wb = [self.take(1024).rearrange("p (c m) -> p c m", c=8) for _ in range(3)]
        wgv = self.take(1024).rearrange("p (c m) -> p c m", c=8)
        kst = [self.take(TG) for _ in range(4)]
        vst = [self.take(128) for _ in range(2)]
        Rwb = [self.R("wb%d" % i) for i in range(3)]
        self.dma("sp", cosT, self.d_cs_g[0], [], [Rtab])
        self.dma("sp", sinT, self.d_cs_g[1], [], [Rtab])
        self.dma("pool", wgv.rearrange("p c m -> p (c m)"), self.d_odw_gv, [], [self.R("wgv")])
        for ci in range(5):
            b = ci % 3
            self.dma("pool", wb[b].rearrange("p c m -> p (c m)"), self.d_odw_fm[3 + ci], [], [Rwb[b]])
            for g in range(NG):
                sl = slice(g * TG, (g + 1) * TG)
                pb = step % 2
                step += 1
                for c in range(8):
                    self.mm(self.bank[pb][:, 0:TG], wb[b][:, c, :], hT[:, c, sl], c == 0, c == 7, [Rwb[b], Rh],
                            [self.bres[pb]])
                if ci < 4:
                    self.normrope(self.bank[pb][:, 0:TG], self.bres[pb], 128, C_BD, 64, V_GQ, C_RMG, cosT[:, sl], sinT[:, sl],
                                  Rtab, qTg[:, ci, sl], Rq)
                else:
                    kb = ks % 4
                    ks += 1
                    self.normrope(self.bank[pb][:, 0:TG], self.bres[pb], 128, C_BD, 64, V_GK, C_RMG, cosT[:, sl], sinT[:, sl],
                                  Rtab, kst[kb], self.R("kstg%d" % kb),
                                  then=(lambda sl=sl, kb=kb: self.dma(
                                      "sp", kv_scr.loc(768, 896)[:, sl], kst[kb], [self.R("kstg%d" % kb)], [Rkv_new()])))
        self.nr_flush()
        vsg = kv_scr.loc(1408, 1536).rearrange("r (b x) -> (r b) x", b=2).rearrange("(h p) (t c) -> h p t c", h=2, c=64)
        for tt_ in range(NT):
            k = tt_ % 2
            tsl = slice(tt_ * 128, (tt_ + 1) * 128)
            pb = 6 + k
            for c in range(8):
                self.mm(self.bank[pb][:, 0:128], hT[:, c, tsl], wgv[:, c, :], c == 0, c == 7, [Rh, self.R("wgv")],
                        [self.bres[pb]])
            self.cp("act", vst[k], self.bank[pb][:, 0:128], [self.bres[pb]], [self.R("vstg%d" % k)])
            self.dma("sp", vsg[:, :, tt_, :].rearrange("h p c -> p h c"), vst[k].rearrange("p (h c) -> p h c", h=2),
                     [self.R("vstg%d" % k)], [Rkv_new()])

    def odd_attn(self, kvall):
        cfg = self.cfg
        T, TG, NG, NT, KT, TPG = cfg.T, cfg.TG, cfg.NG, cfg.NT, cfg.KT, cfg.TPG
        P = self.P
        P.barrier()
        oq = self.off_qT(True)
        qTm = self.fixed(oq, 8 * T).rearrange("p (c t) -> p c t", c=8)
        qTg = self.fixed(oq + 8 * T, 4 * T).rearrange("p (c t) -> p c t", c=4)
        otok = self.fixed(self.off_otok(True), 8 * T).rearrange("p (t c) -> p t c", c=1024)
        Rq, Ro = self.R("qT"), self.R("otok")
        self.arena_reset(0, self.off_otok(True))
        self.attn_alloc(65)
        r4 = [self.take(4, F32) for _ in range(2)]
        NR = 1536
        accb = [2, 3, 4, 5, 6, 7]
        ai = 0

        def run_map(kb, krows, qap, scale, col0, pre):
            nonlocal ai
            for g in range(NG):
                sl = slice(g * TG, (g + 1) * TG)
                a = accb[ai % 6]
                k = ai % 2
                ai += 1

                def post(a=a, k=k, g=g, col0=col0):
                    acc3 = self.bank[a][:, 0:TPG * 65].rearrange("p (j c) -> p j c", c=65)
                    self.recip(r4[k][:, 0:TPG], acc3[:, :, 64], [self.bres[a]], [self.R("r4%d" % k)])
                    self.tt("dve", otok[:, g * TPG:(g + 1) * TPG, col0:col0 + 64], acc3[:, :, 0:64],
                            r4[k][:, 0:TPG].unsqueeze(2).to_broadcast([128, TPG, 64]), ALU.mult,
                            [self.bres[a], self.R("r4%d" % k)], [Ro])
                self.attn_map(kb, krows, qap(sl), Rq, scale, 65,
                              lambda j, a=a: self.bank[a][:, j * 65:(j + 1) * 65],
                              lambda j, a=a: self.bres[a], lambda j: j == 0, lambda j: j == TPG - 1,
                              pre=pre if g == 0 else None, post=post)

        def load_mla(h, kb):
            for r in range(4):
                self.dma("sp", self.Kb[kb][0:96, r * T:(r + 1) * T], kvall.gat(r, h * 96, (h + 1) * 96),
                         [self.R("kvall")], [self.RK[kb]])
                vseg = 896 + 256 * (h // 4)
                vsrc = kvall.gat(r, vseg, vseg + 256).rearrange("r (b x) -> (r b) x", b=2)[(h % 4) * 128:(h % 4 + 1) * 128, :]
                self.dma("sp", self.Vb[kb][:, r * NT:(r + 1) * NT, 0:64], vsrc.rearrange("p (t c) -> p t c", c=64),
                         [self.R("kvall")], [self.RV[kb]])

        def load_gqa(hk, kb):
            for r in range(4):
                for dup in range(2):
                    self.dma("sp", self.Kb[kb][dup * 64:(dup + 1) * 64, r * T:(r + 1) * T],
                             kvall.gat(r, 768 + hk * 64, 768 + (hk + 1) * 64),
                             [self.R("kvall")], [self.RK[kb]])
                vsrc = kvall.gat(r, 1408, 1536).rearrange("r (b x) -> (r b) x", b=2)[hk * 128:(hk + 1) * 128, :]
                self.dma("sp", self.Vb[kb][:, r * NT:(r + 1) * NT, 0:64], vsrc.rearrange("p (t c) -> p t c", c=64),
                         [self.R("kvall")], [self.RV[kb]])

        for u in range(10):
            kb = u % 2
            if u < 8:
                h = u
                run_map(kb, slice(0, 96), lambda sl, h=h: qTm[0:96, h, sl], 96 ** -0.5, h * 64,
                        lambda h=h, kb=kb: load_mla(h, kb))
            else:
                hk = u - 8
                for g4 in range(4):
                    qh = hk * 4 + g4
                    rows = slice((qh % 2) * 64, (qh % 2) * 64 + 64)
                    run_map(kb, rows, lambda sl, rows=rows, qh=qh: qTg[rows, qh // 2, sl], 0.125, 512 + qh * 64,
                            (lambda hk=hk, kb=kb: load_gqa(hk, kb)) if g4 == 0 else None)
        self.run_attn()

    def build(self):
        cfg = self.cfg
        T = cfg.T
        stages = self.stages
        self.setup()
        self.d_wgu = self.inp("wgu", [4, NJ, 128, 2048])
        self.d_wd = self.inp("wd", [4, 2, 8, 128, NJH * 128])
        xv = lambda ap: ap.rearrange("(c p) t -> p c t", p=128)
        allx = [r for row in self.Rx for r in row]
        if 1 in stages:
            d_x = self.inp("xT", [D, T])
            self.d_evw_fm = self.inp("evw_fm", [12, 128, 1024])
            self.d_evw_tm = self.inp("evw_tm", [2, 128, 4096])
            self.d_wsT = self.inp("wsT", [128, 1024])
            self.d_bias_t = self.inp("bias_t", [128, 512])
            self.d_cs_e = self.inp("cs_e", [2, 128, T])
            for c in range(8):
                self.dma("sp", self.xres[:, c, :], d_x[c * 128:(c + 1) * 128, :], [], self.Rx[c])
            self.ffn(0, V_FFN + 0)
            kv_e = self.make_kv("e", [256] * 4 if self.fused else [1024])
            self.even_prep(V_FFN + 8, kv_e)
        if 2 in stages:
            self.d_evwo = self.inp("evwo", [8, 128, 1024])
            self.d_odw_fm = self.inp("odw_fm", [8, 128, 1024])
            self.d_odw_kpe = self.inp("odw_kpe", [128, 256])
            self.d_odw_gv = self.inp("odw_gv", [128, 1024])
            self.d_wuq = self.inp("wuq", [128, 1536])
            self.d_wkp = self.inp("wkp", [128, 768])
            self.d_wvm = self.inp("wvm", [128, 512])
            self.d_cs_m = self.inp("cs_m", [2, 128, T])
            self.d_cs_g = self.inp("cs_g", [2, 128, T])
            if self.fused:
                kvall_e = self.gather(kv_e)
            else:
                kvall_e = KVScr([1024])
                kvall_e.gat_t[0] = self.inp("kvall_e", [4 * 1024, T], BF16)
                self.P.barrier()
                d_xs = self.inp("xs_in", [D, T])
                d_q = self.inp("q_in", [512, T], BF16)
                d_m = self.inp("m_in", [512, T], BF16)
                for c in range(8):
                    self.dma("sp", self.xres[:, c, :], d_xs[c * 128:(c + 1) * 128, :], [], self.Rx[c])
                qT = self.fixed(self.off_qT(False), 4 * T).rearrange("p (c t) -> p c t", c=4)
                mixT = self.fixed(self.off_mixT(), 8 * T).rearrange("p (c t) -> p c t", c=8)
                self.dma("sp", qT, xv(d_q), [], [self.R("qT")])
                self.dma("sp", mixT[:, 0:4, :], xv(d_m), [], [self.R("mixT")])
            self.even_attn(kvall_e)
            self.out_proj(False, self.d_evwo)
            self.ffn(1, V_FFN + 16)
            self.ffn(2, V_FFN + 24)
            kv_o = self.make_kv("o", [192] * 4 + [128] + [256] * 2 + [128] if self.fused else [1536])
            self.odd_prep(V_FFN + 32, kv_o)
        if 3 in stages:
            self.d_odwo = self.inp("odwo", [8, 128, 1024])
            if self.fused:
                kvall_o = self.gather(kv_o)
            else:
                kvall_o = KVScr([1536])
                kvall_o.gat_t[0] = self.inp("kvall_o", [4 * 1536, T], BF16)
                self.P.barrier()
                d_xs = self.inp("xs_in", [D, T])
                d_q = self.inp("qo_in", [12 * 128, T], BF16)
                for c in range(8):
                    self.dma("sp", self.xres[:, c, :], d_xs[c * 128:(c + 1) * 128, :], [], self.Rx[c])
                qTo = self.fixed(self.off_qT(True), 12 * T).rearrange("p (c t) -> p c t", c=12)
                self.dma("sp", qTo, xv(d_q), [], [self.R("qT")])
            self.odd_attn(kvall_o)
            self.out_proj(True, self.d_odwo)
            self.ffn(3, V_FFN + 40)
        self.P.barrier()
        last = max(stages)
        if last == 3:
            d_o = self.outp("outT", [D, T])
            for c in range(8):
                self.dma("sp", d_o[c * 128:(c + 1) * 128, :], self.xres[:, c, :], self.Rx[c], [Res()])
        else:
            d_o = self.outp("xs_out", [D, T])
            for c in range(8):
                self.dma("sp", d_o[c * 128:(c + 1) * 128, :], self.xres[:, c, :], self.Rx[c], [Res()])
            if last == 1:
                qT = self.fixed(self.off_qT(False), 4 * T).rearrange("p (c t) -> p c t", c=4)
                mixT = self.fixed(self.off_mixT(), 8 * T).rearrange("p (c t) -> p c t", c=8)
                self.dma("sp", xv(self.outp("q_out", [512, T], BF16)), qT, [self.R("qT")], [Res()])
                self.dma("sp", xv(self.outp("m_out", [512, T], BF16)), mixT[:, 0:4, :], [self.R("mixT")], [Res()])
            else:
                qTo = self.fixed(self.off_qT(True), 12 * T).rearrange("p (c t) -> p c t", c=12)
                self.dma("sp", xv(self.outp("qo_out", [12 * 128, T], BF16)), qTo, [self.R("qT")], [Res()])
        self.P.emit()
        self.st.close()
        return self.nc

    def make_kv(self, tag, segs):
        T = self.cfg.T
        kv = KVScr(segs)
        for i, n in enumerate(segs):
            if self.fused:
                kv.loc_t[i] = self.nc.dram_tensor("kv%s%d" % (tag, i), [n, T], BF16).ap()
                kv.gat_t[i] = self.nc.dram_tensor("kvall%s%d" % (tag, i), [4 * n, T], BF16).ap()
            else:
                kv.loc_t[i] = self.outp("kv_" + tag, [n, T], BF16)
        return kv

    def gather(self, kv):
        for i in range(len(kv.segs)):
            o = self.P.op("pool", (lambda a, b_: (lambda e: e.collective_compute(
                "AllGather", ALU.bypass, replica_groups=[[0, 1, 2, 3], [4, 5, 6, 7]],
                ins=[a.opt()], outs=[b_.opt()])))(kv.loc_t[i], kv.gat_t[i]),
                list(self.kvres), [self.R("kvall")], dma=True, inc=1)
            o.cc = True
        return kv


def prep_shared(inp):
    f = lambda a: np.ascontiguousarray(np.asarray(a, dtype=np.float32))
    g = {k: f(v) for k, v in inp.items() if k != "x"}
    out = {}
    wgu = []
    wd = []
    for l in range(2):
        for nm in ("ffn1", "ffn2"):
            w = g[nm + "_w_gu"][l]
            wgu.append(w.reshape(8, 128, 2, NJ, 128).transpose(3, 1, 0, 2, 4).reshape(NJ, 128, 2048))
            w = g[nm + "_w_down"][l]
            wd.append(w.reshape(2, NJH, 128, 8, 128).transpose(0, 3, 2, 1, 4).reshape(2, 8, 128, NJH * 128))
    out["wgu"] = np.ascontiguousarray(np.stack(wgu))
    out["wd"] = np.ascontiguousarray(np.stack(wd))
    w = g["ev_w_in"][0]
    cols = [np.arange(i * 128, (i + 1) * 128) for i in range(4)]
    cols += [np.arange(1024 + h * 128, 1024 + (h + 1) * 128) for h in range(4)]
    cols += [np.arange(1536 + h * 128, 1536 + (h + 1) * 128) for h in range(4)]
    out["evw_fm"] = np.stack([_chunk_fm(w, c) for c in cols])
    out["evw_tm"] = np.stack([_chunk_fm(w, np.arange(512, 1024)), _chunk_fm(w, np.arange(2048, 2560))])
    out["evwo"] = np.stack([_chunk_fm(g["ev_w_out"][0], np.arange(dc * 128, (dc + 1) * 128)) for dc in range(8)])
    out["wsT"] = np.ascontiguousarray(g["ev_w_s"][0].transpose(2, 0, 1).reshape(128, 1024))
    bs = g["ev_b_s"][0]
    bt = np.zeros((128, 4, 128), np.float32)
    for cg in range(4):
        bt[:64, cg, :] = bs[2 * cg][None, :]
        bt[64:, cg, :] = bs[2 * cg + 1][None, :]
    out["bias_t"] = bt.reshape(128, 512)
    w = g["od_w_in"][0]
    cols = [np.arange(0, 128), np.arange(128, 256), np.arange(256, 384)]
    cols += [np.arange(416 + i * 128, 416 + (i + 1) * 128) for i in range(4)]
    cols += [np.arange(928, 1056)]
    out["odw_fm"] = np.stack([_chunk_fm(w, c) for c in cols])
    out["odw_kpe"] = _chunk_fm(w, np.arange(384, 416))
    out["odw_gv"] = _chunk_fm(w, np.arange(1056, 1184))
    out["wuq"] = np.ascontiguousarray(g["od_w_uq"][0].reshape(2, 128, 768).transpose(1, 0, 2).reshape(128, 1536))
    wukv = g["od_w_ukv"][0]
    wkp = np.zeros((128, 8, 96), np.float32)
    wvm = np.zeros((128, 8, 64), np.float32)
    for h in range(8):
        wkp[:, h, :64] = wukv[:, h * 128: h * 128 + 64]
        wvm[:, h, :] = wukv[:, h * 128 + 64: h * 128 + 128]
    out["wkp"] = wkp.reshape(128, 768)
    out["wvm"] = wvm.reshape(128, 512)
    out["odwo"] = np.stack([_chunk_fm(g["od_w_out"][0], np.arange(dc * 128, (dc + 1) * 128)) for dc in range(8)])
    vecs = np.zeros((128, NVEC), np.float32)
    norms = [g["ffn1_norm"][0], g["ev_norm"][0], g["ffn2_norm"][0], g["ffn1_norm"][1], g["od_norm"][0], g["ffn2_norm"][1]]
    for i, nv in enumerate(norms):
        vecs[:, V_FFN + 8 * i: V_FFN + 8 * i + 8] = nv.reshape(8, 128).T
    vecs[:, V_EQ] = np.tile(g["ev_q_norm"][0], 2)
    vecs[:, V_EK] = np.tile(g["ev_k_norm"][0], 2)
    vecs[:, V_CQ:V_CQ + 2] = g["od_cq_norm"][0].reshape(2, 128).T
    vecs[:, V_CKV] = g["od_ckv_norm"][0]
    vecs[:96, V_MQ] = g["od_mla_q_norm"][0]
    vecs[:96, V_MK] = g["od_mla_k_norm"][0]
    vecs[:, V_GQ] = np.tile(g["od_gqa_q_norm"][0], 2)
    vecs[:, V_GK] = np.tile(g["od_gqa_k_norm"][0], 2)
    out["vecs"] = vecs
    rows = np.zeros((1, NROW), np.float32)
    rows[0, R_SGU:R_SGU + 512] = g["ev_sgu_norm"][0].reshape(512)
    rows[0, R_SUB:R_SUB + 128] = g["ev_sub_norm"][0]
    rows[0, R_LAM:R_LAM + 256] = np.concatenate([g["ev_lam_q1"][0], g["ev_lam_k1"][0], g["ev_lam_q2"][0], g["ev_lam_k2"][0]])
    out["rows"] = rows
    return out


_NP = {F32: np.float32, BF16: ml_dtypes.bfloat16}


def run_stage(cfg, stages, fused, shared, percore, runner=None):
    b = Builder(cfg, stages, fused)
    nc = b.build()
    in_maps = []
    for c in range(8):
        m = {}
        for name, (shape, dt) in b.din.items():
            a = percore[c][name] if name in percore[c] else shared[name]
            assert tuple(a.shape) == tuple(shape), (name, a.shape, shape)
            m[name] = np.ascontiguousarray(a, dtype=_NP[dt])
        in_maps.append(m)
    if runner is None:
        res = run_bass_kernel_spmd(nc, in_maps, core_ids=list(range(8)))
        return res.results
    return runner(nc, in_maps, {k: (s, _NP[d]) for k, (s, d) in b.dout.items()})


def run_model(inp, T=2048, TG=512, fused=False, runner=None):
    cfg = Cfg(T, TG)
    x = np.asarray(inp["x"], dtype=np.float32)
    B, S, _ = x.shape
    assert B == 2 and S == 4 * T
    shared = prep_shared(inp)
    xt = x.reshape(8, T, D)
    percore = []
    for c in range(8):
        cmat, cs_e, cs_m, cs_g = host_consts(T, S, c)
        percore.append({"xT": np.ascontiguousarray(xt[c].T), "cmat": cmat, "cs_e": cs_e, "cs_m": cs_m, "cs_g": cs_g})
    if fused:
        res = run_stage(cfg, (1, 2, 3), True, shared, percore, runner)
    else:
        def regroup(res, key):
            full = [np.concatenate([res[g * 4 + r][key] for r in range(4)], axis=0) for g in range(2)]
            return [full[c // 4] for c in range(8)]
        r1 = run_stage(cfg, (1,), False, shared, percore, runner)
        kva = regroup(r1, "kv_e")
        for c in range(8):
            percore[c].update({"kvall_e": kva[c], "xs_in": r1[c]["xs_out"], "q_in": r1[c]["q_out"], "m_in": r1[c]["m_out"]})
        r2 = run_stage(cfg, (2,), False, shared, percore, runner)
        kva = regroup(r2, "kv_o")
        for c in range(8):
            percore[c].update({"kvall_o": kva[c], "xs_in": r2[c]["xs_out"], "qo_in": r2[c]["qo_out"]})
        res = run_stage(cfg, (3,), False, shared, percore, runner)
    out = np.stack([np.asarray(res[c]["outT"], dtype=np.float32).T for c in range(8)])
    return np.ascontiguousarray(out.reshape(B, S, D))


def kernel(**inputs):
    return run_model(inputs, T=2048, TG=512, fused=True)
```

```python
import contextlib
import numpy as np
import ml_dtypes
import concourse.bass as bass
import concourse.mybir as mybir
from concourse.bass_utils import run_bass_kernel_spmd

F32 = mybir.dt.float32
BF16 = mybir.dt.bfloat16
AF = mybir.ActivationFunctionType
ALU = mybir.AluOpType
AX = mybir.AxisListType

D = 1024
DFF = 2816
NJ = 22
NJH = 11
EPS = 1e-6
THETA = 10000.0
GRID_W = 64
ENGS = ("pe", "act", "dve", "pool", "sp")


class Res:
    __slots__ = ("name", "w", "rs")

    def __init__(self, name=""):
        self.name = name
        self.w = None
        self.rs = []


class Op:
    __slots__ = ("eng", "fn", "deps", "pos", "dma", "signal", "sem", "target", "prev_target", "inc", "cc")

    def __init__(self, eng, fn, dma, inc):
        self.eng = eng
        self.fn = fn
        self.dma = dma
        self.deps = []
        self.pos = -1
        self.signal = False
        self.sem = None
        self.target = 0
        self.prev_target = 0
        self.inc = inc
        self.cc = False


class Prog:
    NDMA_SEMS = 8

    def __init__(self, nc):
        self.nc = nc
        self.ops = {e: [] for e in ENGS}
        self.pending_barrier = {e: [] for e in ENGS}

    def op(self, eng, fn, reads=(), writes=(), dma=False, inc=16):
        o = Op(eng, fn, dma, inc)
        o.pos = len(self.ops[eng])
        deps = set(self.pending_barrier[eng])
        self.pending_barrier[eng] = []
        rawset = set()
        for r in reads:
            if r.w is not None:
                deps.add(r.w)
                rawset.add(r.w)
        for w in writes:
            if w.w is not None:
                deps.add(w.w)
            for rd in w.rs:
                deps.add(rd)
        best = {}
        red = []
        for d in deps:
            if d.dma:
                red.append(d)
            elif d.eng not in best or best[d.eng].pos < d.pos:
                best[d.eng] = d
        red.extend(best.values())
        for d in red:
            if d is o:
                continue
            if d.dma or o.dma or d.eng != eng:
                o.deps.append(d)
                d.signal = True
            elif eng != "pe" and (o.pos - d.pos) <= 3 and d in rawset:
                o.deps.append(d)
                d.signal = True
        for r in reads:
            r.rs.append(o)
        for w in writes:
            w.w = o
            w.rs = []
        self.ops[eng].append(o)
        return o

    def barrier(self):
        lasts = []
        for e in ENGS:
            comp = [o for o in self.ops[e] if not o.dma]
            if comp:
                lasts.append(comp[-1])
            dmas = [o for o in self.ops[e] if o.dma and not o.cc]
            lasts.extend(dmas[-self.NDMA_SEMS:])
        for e in ENGS:
            self.pending_barrier[e] = list(lasts)

    def emit(self):
        nc = self.nc
        with contextlib.ExitStack() as st:
            esem = {e: st.enter_context(nc.semaphore("s_" + e)) for e in ENGS}
            dsem = {}
            for q in ENGS:
                nd = sum(1 for o in self.ops[q] if o.dma and not o.cc)
                if nd:
                    dsem[q] = [st.enter_context(nc.semaphore("d_%s%d" % (q, i)))
                               for i in range(min(nd, self.NDMA_SEMS))]
            for e in ENGS:
                c = 0
                rr = 0
                nsem = len(dsem.get(e, []))
                tot = [0] * max(nsem, 1)
                for o in self.ops[e]:
                    if o.cc:
                        o.sem = st.enter_context(nc.semaphore("cc_%d" % o.pos))
                        o.prev_target = 0
                        o.target = o.inc
                    elif o.dma:
                        o.sem = dsem[e][rr]
                        o.prev_target = tot[rr]
                        tot[rr] += o.inc
                        o.target = tot[rr]
                        rr = (rr + 1) % nsem
                    elif o.signal:
                        c += 1
                        o.sem = esem[e]
                        o.target = c
            block = st.enter_context(nc.Block())

            def run(e, eng):
                seen = {}
                for o in self.ops[e]:
                    waits = {}
                    for d in o.deps:
                        key = id(d.sem)
                        if seen.get(key, 0) >= d.target:
                            continue
                        if key not in waits or waits[key][1] < d.target:
                            waits[key] = (d.sem, d.target)
                    if o.dma and o.prev_target > 0:
                        key = id(o.sem)
                        if seen.get(key, 0) < o.prev_target:
                            if key not in waits or waits[key][1] < o.prev_target:
                                waits[key] = (o.sem, o.prev_target)
                    for key, (s, v) in waits.items():
                        eng.wait_ge(s, v)
                        seen[key] = v
                    ins = o.fn(eng)
                    if o.dma:
                        ins.then_inc(o.sem, o.inc)
                    elif o.signal:
                        ins.then_inc(o.sem, 1)
                if e in dsem or any(o.cc for o in self.ops[e]):
                    tot = {}
                    for o in self.ops[e]:
                        if o.dma:
                            tot[id(o.sem)] = (o.sem, o.target)
                    for key, (s, v) in tot.items():
                        if seen.get(key, 0) < v:
                            eng.wait_ge(s, v)

            block.tensor(lambda eng: run("pe", eng))
            block.scalar(lambda eng: run("act", eng))
            block.vector(lambda eng: run("dve", eng))
            block.gpsimd(lambda eng: run("pool", eng))
            block.sync(lambda eng: run("sp", eng))


def _chunk_fm(w, cols):
    sub = w[:, cols]
    m = sub.shape[1]
    return np.ascontiguousarray(sub.reshape(8, 128, m).transpose(1, 0, 2).reshape(128, 8 * m))


def _rope_inv(dim):
    return (np.float32(THETA) ** (-np.arange(0, dim, 2, dtype=np.float32) / np.float32(dim))).astype(np.float32)


def host_consts(T, S, core):
    r = core % 4
    pos = (r * T + np.arange(T)).astype(np.int32)
    eye = np.eye(128, dtype=np.float32)
    ones = np.ones((128, 128), np.float32)
    bd = np.zeros((128, 128), np.float32)
    bd[:64, :64] = 1
    bd[64:, 64:] = 1
    ones96 = np.zeros((128, 128), np.float32)
    ones96[:96, :96] = 1

    def rot(blocks):
        m = np.zeros((128, 128), np.float32)
        for (b0, half) in blocks:
            for d in range(half):
                m[b0 + d + half, b0 + d] = -1.0
                m[b0 + d, b0 + d + half] = 1.0
        return m
    rm_e = rot([(0, 32), (64, 32)])
    rm_mla = rot([(64, 16)])
    rm_gqa = rot([(0, 16), (32, 16), (64, 16), (96, 16)])
    sel = np.zeros((128, 128), np.float32)
    for i in range(32):
        sel[i, 64 + i] = 1.0
    cmat = np.concatenate([eye, ones, bd, ones96, rm_e, rm_mla, rm_gqa, sel], axis=1)

    def angles(p, dim):
        inv = _rope_inv(dim)
        return p.astype(np.float32)[:, None] * inv[None, :]
    a = angles(pos, 64)
    idx = (np.arange(128) % 64) % 32
    cs_e = np.stack([np.cos(a)[:, idx].T, np.sin(a)[:, idx].T]).astype(np.float32)
    a = angles(pos, 32)
    cs_m = np.zeros((2, 128, T), np.float32)
    cs_m[0, :64] = 1.0
    idx = (np.arange(32)) % 16
    cs_m[0, 64:96] = np.cos(a)[:, idx].T
    cs_m[1, 64:96] = np.sin(a)[:, idx].T
    ar = angles(pos // GRID_W, 32)
    ac = angles(pos % GRID_W, 32)
    cs_g = np.zeros((2, 128, T), np.float32)
    for p in range(128):
        d = p % 64
        src = ar if d < 32 else ac
        f = (d % 32) % 16
        cs_g[0, p] = np.cos(src)[:, f]
        cs_g[1, p] = np.sin(src)[:, f]
    return cmat, cs_e, cs_m, cs_g


class Cfg:
    def __init__(self, T, TG):
        self.T = T
        self.S = 4 * T
        self.TG = TG
        self.NT = T // 128
        self.NG = T // TG
        self.TPG = TG // 128
        self.KT = self.S // 128
        self.arena = max(34 * T + 2048, 44000)


V_FFN = 0
V_EQ = 48
V_EK = 49
V_CQ = 50
V_CKV = 52
V_MQ = 53
V_MK = 54
V_GQ = 55
V_GK = 56
NVEC = 57
R_SGU = 0
R_SUB = 512
R_LAM = 640
NROW = 896
C_ID, C_ONES, C_BD, C_O96, C_RME, C_RMM, C_RMG, C_SEL = range(8)


class KVScr:
    def __init__(self, segs):
        self.segs = list(segs)
        self.b0 = [0]
        for n in self.segs:
            self.b0.append(self.b0[-1] + n)
        self.loc_t = [None] * len(self.segs)
        self.gat_t = [None] * len(self.segs)
        self.res = [Res("kvseg%d" % i) for i in range(len(self.segs))]
        self.done = set()

    def rres(self, a, b):
        return self.res[self._find(a, b)]

    def _find(self, a, b):
        for i, n in enumerate(self.segs):
            if self.b0[i] <= a and b <= self.b0[i + 1]:
                return i
        raise AssertionError(("kv rows cross a segment", a, b))

    def loc(self, a, b):
        i = self._find(a, b)
        return self.loc_t[i][a - self.b0[i]: b - self.b0[i], :]

    def gat(self, r, a, b):
        i = self._find(a, b)
        n = self.segs[i]
        return self.gat_t[i][r * n + a - self.b0[i]: r * n + b - self.b0[i], :]


class Builder:
    def __init__(self, cfg, stages, fused):
        self.cfg = cfg
        self.stages = stages
        self.fused = fused
        self.nc = bass.Bass("TRN2", target_bir_lowering=False)
        self.P = Prog(self.nc)
        self.st = contextlib.ExitStack()
        self.din = {}
        self.dout = {}
        self.res = {}

    def R(self, name):
        if name not in self.res:
            self.res[name] = Res(name)
        return self.res[name]

    def inp(self, name, shape, dt=F32):
        t = self.nc.dram_tensor(name, list(shape), dt, kind="ExternalInput")
        self.din[name] = (tuple(shape), dt)
        return t.ap()

    def outp(self, name, shape, dt=F32):
        t = self.nc.dram_tensor(name, list(shape), dt, kind="ExternalOutput")
        self.dout[name] = (tuple(shape), dt)
        return t.ap()

    def sb(self, name, shape, dt):
        return self.st.enter_context(self.nc.sbuf_tensor(name, list(shape), dt))

    def mm(self, out, lhsT, rhs, start, stop, reads, writes):
        self.P.op("pe", lambda e: e.matmul(out, lhsT=lhsT, rhs=rhs, start=start, stop=stop),
                  reads, writes)

    def act(self, out, in_, func, reads, writes, scale=1.0, bias=None, accum_out=None, eng="act"):
        kw = {}
        if bias is not None:
            kw["bias"] = bias
        if accum_out is not None:
            kw["accum_out"] = accum_out
        self.P.op("act", lambda e: e.activation(out=out, in_=in_, func=func, scale=scale, **kw),
                  reads, writes)

    def tt(self, eng, out, in0, in1, op, reads, writes):
        self.P.op(eng, lambda e: e.tensor_tensor(out=out, in0=in0, in1=in1, op=op), reads, writes)

    def stt(self, eng, out, in0, scalar, in1, op0, op1, reads, writes):
        self.P.op(eng, lambda e: e.scalar_tensor_tensor(out=out, in0=in0, scalar=scalar, in1=in1,
                                                        op0=op0, op1=op1), reads, writes)

    def ts(self, eng, out, in0, s1, op0, reads, writes, s2=None, op1=None):
        if op1 is None:
            self.P.op(eng, lambda e: e.tensor_scalar(out=out, in0=in0, scalar1=s1, scalar2=None, op0=op0),
                      reads, writes)
        else:
            self.P.op(eng, lambda e: e.tensor_scalar(out=out, in0=in0, scalar1=s1, scalar2=s2, op0=op0,
                                                     op1=op1), reads, writes)

    def cp(self, eng, out, in_, reads, writes):
        if eng == "act":
            self.P.op("act", lambda e: e.copy(out=out, in_=in_), reads, writes)
        else:
            self.P.op(eng, lambda e: e.tensor_copy(out=out, in_=in_), reads, writes)

    def recip(self, out, in_, reads, writes):
        self.P.op("dve", lambda e: e.reciprocal(out=out, in_=in_), reads, writes)

    def dma(self, q, out, in_, reads, writes):
        self.P.op(q, lambda e: e.dma_start(out=out, in_=in_), reads, writes, dma=True)

    def arena_reset(self, lo=0, hi=None):
        self.a_lo = lo
        self.a_hi = self.cfg.arena if hi is None else hi

    def take(self, n, dt=BF16):
        nb = n * (2 if dt == F32 else 1)
        nb = (nb + 15) // 16 * 16
        off = self.a_lo
        self.a_lo += nb
        assert self.a_lo <= self.a_hi, ("arena overflow", self.a_lo, self.a_hi)
        ap = self.arena[:, off:off + n * (2 if dt == F32 else 1)]
        if dt == F32:
            ap = ap.bitcast(F32)
        return ap

    def fixed(self, off, n):
        return self.arena[:, off:off + n]

    def setup(self):
        cfg = self.cfg
        T = cfg.T
        self.xres = self.sb("xres", [128, 8, T], F32)
        self.arena = self.sb("arena", [128, cfg.arena], BF16)[:]
        self.cmat = self.sb("cmat_sb", [128, 8 * 128], BF16)
        self.vecs = self.sb("vecs_sb", [128, NVEC], F32)
        self.epsc = self.sb("epsc", [128, 1], F32)
        self.onescol = self.sb("onescol", [128, 1], BF16)
        self.bank = [self.st.enter_context(self.nc.psum_tensor("bank%d" % i, [128, 512], F32))
                     for i in range(8)]
        self.bres = [self.R("bank%d" % i) for i in range(8)]
        self.Rx = [[self.R("x_%d_%d" % (c, g)) for g in range(cfg.NG)] for c in range(8)]
        d_cmat = self.inp("cmat", [128, 8 * 128])
        d_vecs = self.inp("vecs", [128, NVEC])
        self.d_rows = self.inp("rows", [1, NROW])
        self.dma("pool", self.cmat[:], d_cmat, [], [self.R("cmat")])
        self.dma("sp", self.vecs[:], d_vecs, [], [self.R("vecs")])
        self.P.op("dve", lambda e: e.memset(self.epsc[:], EPS), [], [self.R("epsc")])
        self.P.op("dve", lambda e: e.memset(self.onescol[:], 1.0), [], [self.R("onescol")])

    def cm(self, idx, rows=128, cols=128):
        return self.cmat[0:rows, idx * 128: idx * 128 + cols]

    def norm_to_hT(self, hT, vcol, tmp_sq, tmp_rstd, tmp_sd):
        cfg = self.cfg
        T, TG, NG = cfg.T, cfg.TG, cfg.NG
        Rh = self.R("hT")
        Rrs = self.R("rstd")
        for c in range(8):
            sq = tmp_sq[c % 2]
            Rsq = self.R("sq%d" % (c % 2))
            self.act(sq, self.xres[:, c, :], AF.Square, self.Rx[c], [Rsq])
            for g in range(NG):
                self.mm(self.bank[g][:, 0:TG], self.cm(C_ONES), sq[:, g * TG:(g + 1) * TG],
                        c == 0, c == 7, [Rsq, self.R("cmat")], [self.bres[g]])
        for g in range(NG):
            sl = slice(g * TG, (g + 1) * TG)
            self.act(tmp_sd[:, sl], self.bank[g][:, 0:TG], AF.Sqrt, [self.bres[g], self.R("epsc")],
                     [self.R("sd")], scale=1.0 / D, bias=self.epsc[:])
        self.recip(tmp_rstd, tmp_sd, [self.R("sd")], [Rrs])
        for c in range(8):
            self.stt("dve", hT[:, c, :], self.xres[:, c, :], self.vecs[:, vcol + c: vcol + c + 1], tmp_rstd,
                     ALU.mult, ALU.mult, self.Rx[c] + [Rrs, self.R("vecs")], [Rh])

    def ffn(self, fi, vcol):
        cfg = self.cfg
        T, TG, NG = cfg.T, cfg.TG, cfg.NG
        P = self.P
        P.barrier()
        self.arena_reset()
        hT = self.take(8 * T).rearrange("p (c t) -> p c t", c=8)
        actA = self.take(NJH * T).rearrange("p (j t) -> p j t", j=NJH)
        wg = [self.take(2048).rearrange("p (c m) -> p c m", c=8) for _ in range(3)]
        wdb = [self.take(NJH * 128).rearrange("p (j m) -> p j m", j=NJH) for _ in range(3)]
        rstd = self.take(T, F32)
        sd = self.take(T, F32)
        sq = [self.take(T) for _ in range(2)]
        sg = [self.take(TG, F32) for _ in range(2)]
        Rwg = [self.R("wg%d" % i) for i in range(3)]
        Rwd = [self.R("wd%d" % i) for i in range(3)]
        Rsg = [self.R("sg%d" % i) for i in range(2)]
        Rh, Ra = self.R("hT"), self.R("actA")
        self.norm_to_hT(hT, vcol, sq, rstd, sd)
        d_wgu = self.d_wgu
        d_wd = self.d_wd
        step = 0
        wi = 0
        di = 0
        for half in range(2):
            for j in range(NJH):
                jj = half * NJH + j
                b = wi % 3
                wi += 1
                self.dma("pool", wg[b].rearrange("p c m -> p (c m)"), d_wgu[fi, jj], [], [Rwg[b]])
                for g in range(NG):
                    sl = slice(g * TG, (g + 1) * TG)
                    pb = (step % 2) * 2
                    step += 1
                    for c in range(8):
                        self.mm(self.bank[pb][:, 0:TG], wg[b][:, c, 0:128], hT[:, c, sl], c == 0, c == 7,
                                [Rwg[b], Rh], [self.bres[pb]])
                    for c in range(8):
                        self.mm(self.bank[pb + 1][:, 0:TG], wg[b][:, c, 128:256], hT[:, c, sl], c == 0, c == 7,
                                [Rwg[b], Rh], [self.bres[pb + 1]])
                    s = sg[g % 2]
                    self.act(s, self.bank[pb][:, 0:TG], AF.Silu, [self.bres[pb]], [Rsg[g % 2]])
                    self.tt("dve", actA[:, j, sl], s, self.bank[pb + 1][:, 0:TG], ALU.mult,
                            [Rsg[g % 2], self.bres[pb + 1]], [Ra])
            for dc in range(8):
                b = di % 3
                di += 1
                self.dma("pool", wdb[b].rearrange("p j m -> p (j m)"), d_wd[fi, half, dc], [], [Rwd[b]])
                for g in range(NG):
                    sl = slice(g * TG, (g + 1) * TG)
                    pb = 4 + (step % 2)
                    step += 1
                    for j in range(NJH):
                        self.mm(self.bank[pb][:, 0:TG], wdb[b][:, j, :], actA[:, j, sl], j == 0, j == NJH - 1,
                                [Rwd[b], Ra], [self.bres[pb]])
                    self.stt("dve", self.xres[:, dc, sl], self.bank[pb][:, 0:TG], 0.5, self.xres[:, dc, sl],
                             ALU.mult, ALU.add, [self.bres[pb], self.Rx[dc][g]], [self.Rx[dc][g]])

    def nr_alloc(self):
        TG = self.cfg.TG
        self.nr = []
        for k in range(3):
            self.nr.append(dict(qf=self.take(TG, F32), sd=self.take(TG, F32), t2=self.take(TG, F32),
                                sqb=self.take(TG), qnb=self.take(TG)))
        self.nrk = 0
        self.nrq = []

    def nr_flush(self):
        while self.nrq:
            self._nr_advance()

    def _nr_advance(self):
        q = self.nrq
        for ent in list(q):
            if ent["stage"] == 2:
                ent["s3"]()
                q.remove(ent)
            elif ent["stage"] == 1:
                ent["s2"]()
                ent["stage"] = 2

    def normrope(self, ps, Rps, R, onesidx, nnorm, gcol, rmidx, cos_ap, sin_ap, Rtab, out, Rout, then=None):
        TG = self.cfg.TG
        i = self.nrk
        self.nrk += 1
        k = i % 3
        t = self.nr[k]
        Rn = lambda n: self.R("nr_%s%d" % (n, k))
        bs = 2 + i % 2
        br = 4 + i % 2
        qf, sd, t2, sqb, qnb = (t["qf"][0:R], t["sd"][0:R], t["t2"][0:R], t["sqb"][0:R], t["qnb"][0:R])
        self.cp("act", qf, ps, [Rps], [Rn("qf")])
        self.act(sqb, ps, AF.Square, [Rps], [Rn("sqb")])
        self.mm(self.bank[bs][0:R, 0:TG], self.cm(onesidx, R, R), sqb, True, True,
                [Rn("sqb"), self.R("cmat")], [self.bres[bs]])
        self.act(sd, self.bank[bs][0:R, 0:TG], AF.Sqrt, [self.bres[bs], self.R("epsc")], [Rn("sd")],
                 scale=1.0 / nnorm, bias=self.epsc[0:R, :])

        def s2():
            self.recip(sd, sd, [Rn("sd")], [Rn("sd")])
            self.stt("dve", qf, qf, self.vecs[0:R, gcol:gcol + 1], sd, ALU.mult, ALU.mult,
                     [Rn("qf"), Rn("sd"), self.R("vecs")], [Rn("qf")])
            self.cp("pool", qnb, qf, [Rn("qf")], [Rn("qnb")])
            self.mm(self.bank[br][0:R, 0:TG], self.cm(rmidx, R, R), qnb, True, True,
                    [Rn("qnb"), self.R("cmat")], [self.bres[br]])

        def s3():
            self.tt("dve", t2, self.bank[br][0:R, 0:TG], sin_ap, ALU.mult, [self.bres[br], Rtab], [Rn("t2")])
            self.tt("pool", qf, qf, cos_ap, ALU.mult, [Rn("qf"), Rtab], [Rn("qf")])
            self.tt("pool", out, qf, t2, ALU.add, [Rn("qf"), Rn("t2")], [Rout])
            if then is not None:
                then()
        self._nr_advance()
        self.nrq.append(dict(stage=1, s2=s2, s3=s3))

    def off_qT(self, odd):
        return self.cfg.arena - (12 if odd else 4) * self.cfg.T

    def off_mixT(self):
        return self.cfg.arena - 12 * self.cfg.T

    def off_otok(self, odd):
        return self.off_mixT() - (8 if odd else 4) * self.cfg.T

    def even_prep(self, vcol, kv_scr):
        cfg = self.cfg
        T, TG, NG, NT = cfg.T, cfg.TG, cfg.NG, cfg.NT
        P = self.P
        P.barrier()
        qT = self.fixed(self.off_qT(False), 4 * T).rearrange("p (c t) -> p c t", c=4)
        mixT = self.fixed(self.off_mixT(), 8 * T).rearrange("p (c t) -> p c t", c=8)
        Rq, Rm_, Rh = self.R("qT"), self.R("mixT"), self.R("hT")
        self.arena_reset(0, self.off_mixT())
        hT = self.take(8 * T).rearrange("p (c t) -> p c t", c=8)
        mark = self.a_lo
        rstd = self.take(T, F32)
        sd = self.take(T, F32)
        sq = [self.take(T) for _ in range(2)]
        self.norm_to_hT(hT, vcol, sq, rstd, sd)
        P.barrier()
        self.arena_reset(mark, self.off_mixT())
        cosT = self.take(T, F32)
        sinT = self.take(T, F32)
        self.nr_alloc()
        wb = [self.take(1024).rearrange("p (c m) -> p c m", c=8) for _ in range(3)]
        kst = [self.take(TG) for _ in range(4)]
        Rwb = [self.R("wb%d" % i) for i in range(3)]
        Rks = [self.R("kst%d" % i) for i in range(4)]
        Rtab = self.R("tab")
        self.kvres = []

        def Rkv_new():
            r = Res("kvw")
            self.kvres.append(r)
            return r
        self.dma("sp", cosT, self.d_cs_e[0], [], [Rtab])
        self.dma("sp", sinT, self.d_cs_e[1], [], [Rtab])
        step = 0
        ks = 0
        for ci in range(12):
            b = ci % 3
            self.dma("pool", wb[b].rearrange("p c m -> p (c m)"), self.d_evw_fm[ci], [], [Rwb[b]])
            kind, h = ("u", "q", "k")[ci // 4], ci % 4
            for g in range(NG):
                sl = slice(g * TG, (g + 1) * TG)
                pb = step % 2
                step += 1
                for c in range(8):
                    self.mm(self.bank[pb][:, 0:TG], wb[b][:, c, :], hT[:, c, sl], c == 0, c == 7,
                            [Rwb[b], Rh], [self.bres[pb]])
                ps = self.bank[pb][:, 0:TG]
                if kind == "u":
                    self.act(mixT[:, h, sl], ps, AF.Gelu_apprx_tanh, [self.bres[pb]], [Rm_])
                elif kind == "q":
                    self.normrope(ps, self.bres[pb], 128, C_BD, 64, V_EQ, C_RME, cosT[:, sl], sinT[:, sl], Rtab,
                                  qT[:, h, sl], Rq)
                else:
                    kb = ks % 4
                    ks += 1
                    self.normrope(ps, self.bres[pb], 128, C_BD, 64, V_EK, C_RME, cosT[:, sl], sinT[:, sl], Rtab,
                                  kst[kb], Rks[kb],
                                  then=(lambda h=h, sl=sl, kb=kb: self.dma(
                                      "sp", kv_scr.loc(h * 128, (h + 1) * 128)[:, sl], kst[kb], [Rks[kb]], [Rkv_new()])))
        self.nr_flush()
        self.gather(kv_scr, [0, 1])
        P.barrier()
        self.arena_reset(mark, self.off_mixT())
        wtm = self.take(4096).rearrange("p (c m) -> p c m", c=8)
        wsT = self.take(1024).rearrange("p (g i) -> p g i", g=8)
        Gt = self.take(512, F32)
        biast = self.take(512, F32)
        vg = [self.take(512, F32) for _ in range(2)]
        sqv = self.take(512, F32)
        ss8 = [self.take(8, F32) for _ in range(2)]
        vc = [self.take(512) for _ in range(2)]
        tmpm = [self.take(512, F32) for _ in range(2)]
        vst = [self.take(512) for _ in range(2)]
        Rwtm, Rws, Rg = self.R("wtm"), self.R("wsT"), self.R("Gt")
        self.dma("pool", wtm.rearrange("p c m -> p (c m)").rearrange("p (a b) -> p a b", b=2048),
                 self.d_evw_tm[0].rearrange("p (a b) -> p a b", b=2048), [], [Rwtm])
        self.dma("pool", wsT.rearrange("p g i -> p (g i)"), self.d_wsT, [], [Rws])
        self.dma("sp", Gt, self.d_rows[0:1, R_SGU:R_SGU + 512].partition_broadcast(128), [], [Rg])
        self.dma("sp", biast, self.d_bias_t, [], [Rg])
        for tt_ in range(NT):
            k = tt_ % 2
            tsl = slice(tt_ * 128, (tt_ + 1) * 128)
            pb = 6 + k
            Rvg, Rss, Rvc, Rtm = (self.R("vg%d" % k), self.R("ss8%d" % k), self.R("vc%d" % k), self.R("tmpm%d" % k))
            for c in range(8):
                self.mm(self.bank[pb][:, :], hT[:, c, tsl], wtm[:, c, :], c == 0, c == 7, [Rh, Rwtm], [self.bres[pb]])
            self.act(vg[k], self.bank[pb][:, :], AF.Gelu_apprx_tanh, [self.bres[pb]], [Rvg])
            self.tt("dve", sqv, vg[k], vg[k], ALU.mult, [Rvg], [self.R("sqv")])
            self.P.op("dve", (lambda o, i: (lambda e: e.reduce_sum(out=o, in_=i, axis=AX.X)))(
                ss8[k], sqv.rearrange("p (g d) -> p g d", g=8)), [self.R("sqv")], [Rss])
            self.act(ss8[k], ss8[k], AF.Sqrt, [Rss, self.R("epsc")], [Rss], scale=1.0 / 64, bias=self.epsc[:])
            self.recip(ss8[k], ss8[k], [Rss], [Rss])
            vg3 = vg[k].rearrange("p (g d) -> p g d", g=8)
            self.tt("dve", vg3, vg3, ss8[k].unsqueeze(2).to_broadcast([128, 8, 64]), ALU.mult, [Rvg, Rss], [Rvg])
            self.tt("pool", vc[k], vg[k], Gt, ALU.mult, [Rvg, Rg], [Rvc])
            bm = 4 + k
            for g8 in range(8):
                po = (g8 % 2) * 64
                self.mm(self.bank[bm][po:po + 64, (g8 // 2) * 128:(g8 // 2) * 128 + 128],
                        vc[k][:, g8 * 64:(g8 + 1) * 64], wsT[:, g8, :], True, True, [Rvc, Rws], [self.bres[bm]])
            self.tt("dve", tmpm[k], self.bank[bm][:, :], biast, ALU.add, [self.bres[bm], Rg], [Rtm])
            self.tt("pool", mixT[:, 0:4, tsl], tmpm[k].rearrange("p (c i) -> p c i", c=4), mixT[:, 0:4, tsl],
                    ALU.mult, [Rtm, Rm_], [Rm_])
        self.dma("pool", wtm.rearrange("p c m -> p (c m)").rearrange("p (a b) -> p a b", b=2048),
                 self.d_evw_tm[1].rearrange("p (a b) -> p a b", b=2048), [], [Rwtm])
        vs = [kv_scr.loc(512 + 256 * s_, 768 + 256 * s_).rearrange("(h p) (t c) -> h p t c", h=2, c=128) for s_ in range(2)]
        for tt_ in range(NT):
            k = tt_ % 2
            tsl = slice(tt_ * 128, (tt_ + 1) * 128)
            pb = 6 + k
            Rvs = self.R("vst%d" % k)
            for c in range(8):
                self.mm(self.bank[pb][:, :], hT[:, c, tsl], wtm[:, c, :], c == 0, c == 7, [Rh, Rwtm], [self.bres[pb]])
            self.cp("act", vst[k], self.bank[pb][:, :], [self.bres[pb]], [Rvs])
            for s_ in range(2):
                self.dma("sp", vs[s_][:, :, tt_, :].rearrange("h p c -> p h c"),
                         vst[k][:, s_ * 256:(s_ + 1) * 256].rearrange("p (h c) -> p h c", h=2), [Rvs], [Rkv_new()])

    def attn_alloc(self, dvp1, n_pt=3):
        cfg = self.cfg
        self.Kb = [self.take(cfg.S) for _ in range(2)]
        self.Vb = [self.take(cfg.KT * dvp1).rearrange("p (k c) -> p k c", c=dvp1) for _ in range(2)]
        self.pT = [self.take(cfg.TG) for _ in range(n_pt)]
        self.RK = [self.R("Kb%d" % i) for i in range(2)]
        self.RV = [self.R("Vb%d" % i) for i in range(2)]
        self.RpT = [self.R("pT%d" % i) for i in range(n_pt)]
        self.maps = []
        for i in range(2):
            self.P.op("pool", (lambda v: (lambda e: e.memset(v, 1.0)))(self.Vb[i][:, :, dvp1 - 1:dvp1]),
                      [], [self.RV[i]])

    def attn_map(self, kb, krows, qap, Rq, scale, dvp1, acc_of_j, accres_of_j, first_of_bank, last_of_bank,
                 pre=None, post=None):
        self.maps.append(dict(kb=kb, krows=krows, qap=qap, Rq=Rq, scale=scale, dvp1=dvp1, acc=acc_of_j,
                              accres=accres_of_j, first=first_of_bank, last=last_of_bank, pre=pre, post=post))

    def run_attn(self):
        cfg = self.cfg
        TG, KT, TPG = cfg.TG, cfg.KT, cfg.TPG
        maps = self.maps
        tiles = [(mi, kt) for mi in range(len(maps)) for kt in range(KT)]
        N = len(tiles)
        delay = min(4, KT - 1)
        pending = []

        def rec_qk(i):
            mi, kt = tiles[i]
            m = maps[mi]
            if kt == 0 and m["pre"] is not None:
                m["pre"]()
            sb_ = i % 2
            self.mm(self.bank[sb_][:, 0:TG], self.Kb[m["kb"]][m["krows"], kt * 128:(kt + 1) * 128], m["qap"], True, True,
                    [self.RK[m["kb"]], m["Rq"]], [self.bres[sb_]])

        def rec_exp_pv(i):
            mi, kt = tiles[i]
            m = maps[mi]
            sb_ = i % 2
            pi = i % len(self.pT)
            self.act(self.pT[pi], self.bank[sb_][:, 0:TG], AF.Exp, [self.bres[sb_]], [self.RpT[pi]], scale=m["scale"])
            for j in range(TPG):
                self.mm(m["acc"](j), self.pT[pi][:, j * 128:(j + 1) * 128], self.Vb[m["kb"]][:, kt, 0:m["dvp1"]],
                        kt == 0 and m["first"](j), kt == KT - 1 and m["last"](j),
                        [self.RpT[pi], self.RV[m["kb"]]], [m["accres"](j)])
            if kt == KT - 1 and m["post"] is not None:
                pending.append((i + delay, m["post"]))

        rec_qk(0)
        for i in range(N):
            if i + 1 < N:
                rec_qk(i + 1)
            rec_exp_pv(i)
            while pending and pending[0][0] <= i:
                pending.pop(0)[1]()
        for _, p in pending:
            p()
        self.maps = []

    def even_attn(self, kvall):
        cfg = self.cfg
        T, TG, NG, NT, KT, TPG = cfg.T, cfg.TG, cfg.NG, cfg.NT, cfg.KT, cfg.TPG
        P = self.P
        P.barrier()
        lam_init = 0.8 - 0.6 * float(np.exp(-0.3 * 0))
        qT = self.fixed(self.off_qT(False), 4 * T).rearrange("p (c t) -> p c t", c=4)
        otok = self.fixed(self.off_otok(False), 4 * T).rearrange("p (t c) -> p t c", c=512)
        Rq, Ro = self.R("qT"), self.R("otok")
        self.arena_reset(0, self.off_otok(False))
        self.attn_alloc(129)
        lamt = self.take(256, F32)
        lp = self.take(128, F32)
        subg = self.take(128, F32)
        sm = self.take(8, F32)
        fin = [dict(r0=self.take(1, F32), r1=self.take(1, F32), O0=self.take(128, F32), od=self.take(128, F32),
                    ss=self.take(1, F32), junk=self.take(128)) for _ in range(2)]
        Rl = self.R("lam")
        self.dma("sp", lamt, self.d_rows[0:1, R_LAM:R_LAM + 256].partition_broadcast(128), [], [Rl])
        self.dma("sp", subg, self.d_rows[0:1, R_SUB:R_SUB + 128].partition_broadcast(128), [], [self.R("subg")])
        self.tt("dve", lp[:, 0:64], lamt[:, 0:64], lamt[:, 64:128], ALU.mult, [Rl], [self.R("lp")])
        self.tt("dve", lp[:, 64:128], lamt[:, 128:192], lamt[:, 192:256], ALU.mult, [Rl], [self.R("lp")])
        self.P.op("dve", lambda e: e.reduce_sum(out=sm[:, 0:2], in_=lp.rearrange("p (a d) -> p a d", a=2), axis=AX.X),
                  [self.R("lp")], [self.R("sm")])
        self.act(sm[:, 2:4], sm[:, 0:2], AF.Exp, [self.R("sm")], [self.R("sm2")])
        self.tt("dve", sm[:, 4:5], sm[:, 3:4], sm[:, 2:3], ALU.subtract, [self.R("sm2")], [self.R("sm3")])
        self.ts("dve", sm[:, 5:6], sm[:, 4:5], -lam_init, ALU.add, [self.R("sm3")], [self.R("neglam")])
        neglam = sm[:, 5:6]
        self.ts("dve", subg, subg, 1.0 - lam_init, ALU.mult, [self.R("subg")], [self.R("subg")])
        accsets = [(2, 3), (4, 5), (6, 7)]
        ai = 0
        fi_ = 0
        def kvload(h, kb):
            for r in range(4):
                self.dma("sp", self.Kb[kb][:, r * T:(r + 1) * T], kvall.gat(r, h * 128, (h + 1) * 128),
                         [kvall.rres(h * 128, (h + 1) * 128)], [self.RK[kb]])
                self.dma("sp", self.Vb[kb][:, r * NT:(r + 1) * NT, 0:128],
                         kvall.gat(r, 512 + h * 128, 512 + (h + 1) * 128).rearrange("p (t c) -> p t c", c=128),
                         [kvall.rres(512 + h * 128, 512 + (h + 1) * 128)], [self.RV[kb]])

        def finalize(h, g, sets):
            nonlocal fi_
            if True:
                for j in range(TPG):
                    f = fin[fi_ % 2]
                    k = fi_ % 2
                    fi_ += 1
                    Rf = lambda n: self.R("fin_%s%d" % (n, k))
                    (a0, r0b), (a1, r1b) = [((self.bank[s[0]][:, j * 129:(j + 1) * 129], self.bres[s[0]]) if j < 3
                                             else (self.bank[s[1]][:, 0:129], self.bres[s[1]])) for s in sets]
                    self.recip(f["r0"], a0[:, 128:129], [r0b], [Rf("r0")])
                    self.recip(f["r1"], a1[:, 128:129], [r1b], [Rf("r1")])
                    self.ts("dve", f["O0"], a0[:, 0:128], f["r0"][:, 0:1], ALU.mult, [r0b, Rf("r0")], [Rf("O0")])
                    self.tt("dve", f["r1"], f["r1"], neglam, ALU.mult, [Rf("r1"), self.R("neglam")], [Rf("r1")])
                    self.stt("dve", f["od"], a1[:, 0:128], f["r1"][:, 0:1], f["O0"], ALU.mult, ALU.add,
                             [r1b, Rf("r1"), Rf("O0")], [Rf("od")])
                    self.act(f["junk"], f["od"], AF.Square, [Rf("od")], [Rf("junk"), Rf("ss")], accum_out=f["ss"])
                    self.act(f["ss"], f["ss"], AF.Sqrt, [Rf("ss"), self.R("epsc")], [Rf("ss")], scale=1.0 / 128,
                             bias=self.epsc[:])
                    self.recip(f["ss"], f["ss"], [Rf("ss")], [Rf("ss")])
                    tt_ = g * TPG + j
                    self.stt("dve", otok[:, tt_, h * 128:(h + 1) * 128], f["od"], f["ss"][:, 0:1], subg,
                             ALU.mult, ALU.mult, [Rf("od"), Rf("ss"), self.R("subg")], [Ro])

        for h in range(4):
            kb = h % 2
            for g in range(NG):
                sl = slice(g * TG, (g + 1) * TG)
                sets = []
                for m in range(2):
                    bA, bB = accsets[ai % 3]
                    ai += 1
                    sets.append((bA, bB))
                    rows = slice(m * 64, (m + 1) * 64)
                    self.attn_map(kb, rows, qT[rows, h, sl], Rq, 0.125, 129,
                                  lambda j, bA=bA, bB=bB: (self.bank[bA][:, j * 129:(j + 1) * 129] if j < 3
                                                           else self.bank[bB][:, 0:129]),
                                  lambda j, bA=bA, bB=bB: self.bres[bA] if j < 3 else self.bres[bB],
                                  lambda j: j == 0 or j == 3, lambda j: j == min(TPG, 3) - 1 or j == 3,
                                  pre=(lambda h=h, kb=kb: kvload(h, kb)) if (g == 0 and m == 0) else None,
                                  post=(lambda h=h, g=g, sets=list(sets): finalize(h, g, sets)) if m == 1 else None)
        self.run_attn()

    def out_proj(self, odd, d_wo):
        cfg = self.cfg
        T, TG, NG, NT, TPG = cfg.T, cfg.TG, cfg.NG, cfg.NT, cfg.TPG
        P = self.P
        P.barrier()
        ncol = 1024 if odd else 512
        otok = self.fixed(self.off_otok(odd), (8 if odd else 4) * T).rearrange("p (t c) -> p t c", c=ncol)
        mixT = self.fixed(self.off_mixT(), 8 * T).rearrange("p (c t) -> p c t", c=8)
        Ro, Rm_ = self.R("otok"), self.R("mixT")
        self.arena_reset(0, self.off_otok(odd))
        wo = [self.take(1024).rearrange("p (c m) -> p c m", c=8) for _ in range(3)]
        Rwo = [self.R("wo%d" % i) for i in range(3)]
        c0 = 0 if odd else 4
        step = 0
        for c in range(ncol // 128):
            for g in range(NG):
                pb = step % 2
                step += 1
                pst = self.bank[pb].bitcast(BF16)
                for j in range(TPG):
                    tt_ = g * TPG + j
                    self.P.op("pe", (lambda o, i: (lambda e: e.transpose(o, i, self.cm(C_ID))))(
                        pst[:, j * 128:(j + 1) * 128], otok[:, tt_, c * 128:(c + 1) * 128]),
                        [Ro, self.R("cmat")], [self.bres[pb]])
                eng = "act" if step % 2 == 0 else "dve"
                self.cp(eng, mixT[:, c0 + c, g * TG:(g + 1) * TG], pst[:, 0:TG], [self.bres[pb]], [Rm_])
        for dc in range(8):
            b = dc % 3
            self.dma("pool", wo[b].rearrange("p c m -> p (c m)"), d_wo[dc], [], [Rwo[b]])
            for g in range(NG):
                sl = slice(g * TG, (g + 1) * TG)
                pb = 2 + step % 2
                step += 1
                for hc in range(8):
                    self.mm(self.bank[pb][:, 0:TG], wo[b][:, hc, :], mixT[:, hc, sl], hc == 0, hc == 7,
                            [Rwo[b], Rm_], [self.bres[pb]])
                self.tt("dve", self.xres[:, dc, sl], self.bank[pb][:, 0:TG], self.xres[:, dc, sl], ALU.add,
                        [self.bres[pb], self.Rx[dc][g]], [self.Rx[dc][g]])

    def odd_prep(self, vcol, kv_scr):
        cfg = self.cfg
        T, TG, NG, NT, TPG = cfg.T, cfg.TG, cfg.NG, cfg.NT, cfg.TPG
        P = self.P
        P.barrier()
        oq = self.off_qT(True)
        qTm = self.fixed(oq, 8 * T).rearrange("p (c t) -> p c t", c=8)
        qTg = self.fixed(oq + 8 * T, 4 * T).rearrange("p (c t) -> p c t", c=4)
        Rq, Rh = self.R("qT"), self.R("hT")
        self.arena_reset(0, oq)
        hT = self.take(8 * T).rearrange("p (c t) -> p c t", c=8)
        mark = self.a_lo
        rstd = self.take(T, F32)
        sd = self.take(T, F32)
        sq = [self.take(T) for _ in range(2)]
        self.norm_to_hT(hT, vcol, sq, rstd, sd)
        self.kvres = []

        def Rkv_new():
            r = Res("kvw")
            self.kvres.append(r)
            return r
        P.barrier()
        self.arena_reset(mark, oq)
        cosG = [self.take(TG, F32) for _ in range(2)]
        sinG = [self.take(TG, F32) for _ in range(2)]
        self.nr_alloc()
        wfm = [self.take(1024).rearrange("p (c m) -> p c m", c=8) for _ in range(3)]
        wkpe = self.take(256).rearrange("p (c m) -> p c m", c=8)
        wuq = self.take(1536).rearrange("p (c m) -> p c m", c=2)
        wkp = self.take(768)
        wvm = self.take(512)
        cqf = [self.take(TG, F32) for _ in range(2)]
        cqs = [self.take(TG) for _ in range(2)]
        csd = self.take(TG, F32)
        cqn = self.take(2 * TG).rearrange("p (c t) -> p c t", c=2)
        ckvn = self.take(TG)
        kpeb = self.take(TG)
        kst = [self.take(TG) for _ in range(2)]
        vst = [self.take(512) for _ in range(2)]
        Rw, Rtab = self.R("odw"), self.R("tab")
        for i in range(3):
            self.dma("pool", wfm[i].rearrange("p c m -> p (c m)"), self.d_odw_fm[i], [], [Rw])
        self.dma("pool", wkpe.rearrange("p c m -> p (c m)"), self.d_odw_kpe, [], [Rw])
        self.dma("pool", wuq.rearrange("p c m -> p (c m)"), self.d_wuq, [], [Rw])
        self.dma("pool", wkp, self.d_wkp, [], [Rw])
        self.dma("pool", wvm, self.d_wvm, [], [Rw])
        vsm = [kv_scr.loc(896 + 256 * s_, 1152 + 256 * s_).rearrange("r (b x) -> (r b) x", b=2).rearrange(
            "(h p) (t c) -> h p t c", h=4, c=64) for s_ in range(2)]
        step = 0
        ks = 0
        vi = 0
        for g in range(NG):
            sl = slice(g * TG, (g + 1) * TG)
            Rtg = self.R("tabg%d" % (g % 2))
            cosT, sinT = cosG[g % 2], sinG[g % 2]
            self.dma("sp", cosT, self.d_cs_m[0][:, sl], [], [Rtg])
            self.dma("sp", sinT, self.d_cs_m[1][:, sl], [], [Rtg])
            for c2 in range(2):
                for c in range(8):
                    self.mm(self.bank[c2][:, 0:TG], wfm[c2][:, c, :], hT[:, c, sl], c == 0, c == 7, [Rw, Rh],
                            [self.bres[c2]])
                self.cp("act", cqf[c2], self.bank[c2][:, 0:TG], [self.bres[c2]], [self.R("cqf%d" % c2)])
                self.act(cqs[c2], self.bank[c2][:, 0:TG], AF.Square, [self.bres[c2]], [self.R("cqs%d" % c2)])
            for c2 in range(2):
                self.mm(self.bank[6][:, 0:TG], self.cm(C_ONES), cqs[c2], c2 == 0, c2 == 1,
                        [self.R("cqs%d" % c2), self.R("cmat")], [self.bres[6]])
            self.act(csd, self.bank[6][:, 0:TG], AF.Sqrt, [self.bres[6], self.R("epsc")], [self.R("csd")],
                     scale=1.0 / 256, bias=self.epsc[:])
            self.recip(csd, csd, [self.R("csd")], [self.R("csd")])
            for c2 in range(2):
                self.stt("dve", cqn[:, c2, :], cqf[c2], self.vecs[:, V_CQ + c2:V_CQ + c2 + 1], csd, ALU.mult, ALU.mult,
                         [self.R("cqf%d" % c2), self.R("csd"), self.R("vecs")], [self.R("cqn")])
            for c in range(8):
                self.mm(self.bank[7][:, 0:TG], wfm[2][:, c, :], hT[:, c, sl], c == 0, c == 7, [Rw, Rh], [self.bres[7]])
            self.cp("act", cqf[0], self.bank[7][:, 0:TG], [self.bres[7]], [self.R("cqf0")])
            self.act(cqs[0], self.bank[7][:, 0:TG], AF.Square, [self.bres[7]], [self.R("cqs0")])
            self.mm(self.bank[6][:, 0:TG], self.cm(C_ONES), cqs[0], True, True, [self.R("cqs0"), self.R("cmat")],
                    [self.bres[6]])
            self.act(csd, self.bank[6][:, 0:TG], AF.Sqrt, [self.bres[6], self.R("epsc")], [self.R("csd")],
                     scale=1.0 / 128, bias=self.epsc[:])
            self.recip(csd, csd, [self.R("csd")], [self.R("csd")])
            self.stt("dve", ckvn, cqf[0], self.vecs[:, V_CKV:V_CKV + 1], csd, ALU.mult, ALU.mult,
                     [self.R("cqf0"), self.R("csd"), self.R("vecs")], [self.R("ckvn")])
            for c in range(8):
                self.mm(self.bank[7][0:32, 0:TG], wkpe[:, c, :], hT[:, c, sl], c == 0, c == 7, [Rw, Rh], [self.bres[7]])
            self.cp("act", kpeb[0:32], self.bank[7][0:32, 0:TG], [self.bres[7]], [self.R("kpeb")])
            for h in range(8):
                pb = step % 2
                step += 1
                for c2 in range(2):
                    self.mm(self.bank[pb][0:96, 0:TG], wuq[:, c2, h * 96:(h + 1) * 96], cqn[:, c2, :], c2 == 0, c2 == 1,
                            [Rw, self.R("cqn")], [self.bres[pb]])
                self.normrope(self.bank[pb][0:96, 0:TG], self.bres[pb], 96, C_O96, 96, V_MQ, C_RMM, cosT[0:96], sinT[0:96],
                              Rtg, qTm[0:96, h, sl], Rq)
            for h in range(8):
                pb = step % 2
                step += 1
                self.mm(self.bank[pb][0:96, 0:TG], wkp[:, h * 96:(h + 1) * 96], ckvn, True, False,
                        [Rw, self.R("ckvn")], [self.bres[pb]])
                self.mm(self.bank[pb][0:96, 0:TG], self.cm(C_SEL, 32, 96), kpeb[0:32], False, True,
                        [self.R("cmat"), self.R("kpeb")], [self.bres[pb]])
                kb = ks % 2
                ks += 1
                self.normrope(self.bank[pb][0:96, 0:TG], self.bres[pb], 96, C_O96, 96, V_MK, C_RMM, cosT[0:96], sinT[0:96],
                              Rtg, kst[kb][0:96], self.R("kst%d" % kb),
                              then=(lambda h=h, sl=sl, kb=kb: self.dma(
                                  "sp", kv_scr.loc(h * 96, (h + 1) * 96)[:, sl], kst[kb][0:96], [self.R("kst%d" % kb)],
                                  [Rkv_new()])))
            for j in range(TPG):
                tt_ = g * TPG + j
                k = vi % 2
                vi += 1
                self.mm(self.bank[6][:, :], ckvn[:, j * 128:(j + 1) * 128], wvm, True, True, [self.R("ckvn"), Rw],
                        [self.bres[6]])
                self.cp("act", vst[k], self.bank[6][:, :], [self.bres[6]], [self.R("vst%d" % k)])
                for s_ in range(2):
                    self.dma("sp", vsm[s_][:, :, tt_, :].rearrange("h p c -> p h c"),
                             vst[k][:, s_ * 256:(s_ + 1) * 256].rearrange("p (h c) -> p h c", h=4),
                             [self.R("vst%d" % k)], [Rkv_new()])
        self.nr_flush()
        self.gather(kv_scr, [0, 5, 1, 2, 3, 6])
        P.barrier()
        self.arena_reset(mark, oq)
        cosT = self.take(T, F32)
        sinT = self.take(T, F32)
        self.nr_alloc()
        wb = [self.take(1024).rearrange("p (c m) -> p c m", c=8) for _ in range(3)]
        wgv = self.take(1024).rearrange("p (c m) -> p c m", c=8)
        kst = [self.take(TG) for _ in range(4)]
        vst = [self.take(128) for _ in range(2)]
        Rwb = [self.R("wb%d" % i) for i in range(3)]
        self.dma("sp", cosT, self.d_cs_g[0], [], [Rtab])
        self.dma("sp", sinT, self.d_cs_g[1], [], [Rtab])
        self.dma("pool", wgv.rearrange("p c m -> p (c m)"), self.d_odw_gv, [], [self.R("wgv")])
        for ci in range(5):
            b = ci % 3
            self.dma("pool", wb[b].rearrange("p c m -> p (c m)"), self.d_odw_fm[3 + ci], [], [Rwb[b]])
            for g in range(NG):
                sl = slice(g * TG, (g + 1) * TG)
                pb = step % 2
                step += 1
                for c in range(8):
                    self.mm(self.bank[pb][:, 0:TG], wb[b][:, c, :], hT[:, c, sl], c == 0, c == 7, [Rwb[b], Rh],
                            [self.bres[pb]])
                if ci < 4:
                    self.normrope(self.bank[pb][:, 0:TG], self.bres[pb], 128, C_BD, 64, V_GQ, C_RMG, cosT[:, sl], sinT[:, sl],
                                  Rtab, qTg[:, ci, sl], Rq)
                else:
                    kb = ks % 4
                    ks += 1
                    self.normrope(self.bank[pb][:, 0:TG], self.bres[pb], 128, C_BD, 64, V_GK, C_RMG, cosT[:, sl], sinT[:, sl],
                                  Rtab, kst[kb], self.R("kstg%d" % kb),
                                  then=(lambda sl=sl, kb=kb: self.dma(
                                      "sp", kv_scr.loc(768, 896)[:, sl], kst[kb], [self.R("kstg%d" % kb)], [Rkv_new()])))
        self.nr_flush()
        vsg = kv_scr.loc(1408, 1536).rearrange("r (b x) -> (r b) x", b=2).rearrange("(h p) (t c) -> h p t c", h=2, c=64)
        for tt_ in range(NT):
            k = tt_ % 2
            tsl = slice(tt_ * 128, (tt_ + 1) * 128)
            pb = 6 + k
            for c in range(8):
                self.mm(self.bank[pb][:, 0:128], hT[:, c, tsl], wgv[:, c, :], c == 0, c == 7, [Rh, self.R("wgv")],
                        [self.bres[pb]])
            self.cp("act", vst[k], self.bank[pb][:, 0:128], [self.bres[pb]], [self.R("vstg%d" % k)])
            self.dma("sp", vsg[:, :, tt_, :].rearrange("h p c -> p h c"), vst[k].rearrange("p (h c) -> p h c", h=2),
                     [self.R("vstg%d" % k)], [Rkv_new()])

    def odd_attn(self, kvall):
        cfg = self.cfg
        T, TG, NG, NT, KT, TPG = cfg.T, cfg.TG, cfg.NG, cfg.NT, cfg.KT, cfg.TPG
        P = self.P
        P.barrier()
        oq = self.off_qT(True)
        qTm = self.fixed(oq, 8 * T).rearrange("p (c t) -> p c t", c=8)
        qTg = self.fixed(oq + 8 * T, 4 * T).rearrange("p (c t) -> p c t", c=4)
        otok = self.fixed(self.off_otok(True), 8 * T).rearrange("p (t c) -> p t c", c=1024)
        Rq, Ro = self.R("qT"), self.R("otok")
        self.arena_reset(0, self.off_otok(True))
        self.attn_alloc(65)
        r4 = [self.take(4, F32) for _ in range(2)]
        NR = 1536
        accb = [2, 3, 4, 5, 6, 7]
        ai = 0

        def run_map(kb, krows, qap, scale, col0, pre):
            nonlocal ai
            for g in range(NG):
                sl = slice(g * TG, (g + 1) * TG)
                a = accb[ai % 6]
                k = ai % 2
                ai += 1

                def post(a=a, k=k, g=g, col0=col0):
                    acc3 = self.bank[a][:, 0:TPG * 65].rearrange("p (j c) -> p j c", c=65)
                    self.recip(r4[k][:, 0:TPG], acc3[:, :, 64], [self.bres[a]], [self.R("r4%d" % k)])
                    self.tt("dve", otok[:, g * TPG:(g + 1) * TPG, col0:col0 + 64], acc3[:, :, 0:64],
                            r4[k][:, 0:TPG].unsqueeze(2).to_broadcast([128, TPG, 64]), ALU.mult,
                            [self.bres[a], self.R("r4%d" % k)], [Ro])
                self.attn_map(kb, krows, qap(sl), Rq, scale, 65,
                              lambda j, a=a: self.bank[a][:, j * 65:(j + 1) * 65],
                              lambda j, a=a: self.bres[a], lambda j: j == 0, lambda j: j == TPG - 1,
                              pre=pre if g == 0 else None, post=post)

        def load_mla(h, kb):
            for r in range(4):
                self.dma("sp", self.Kb[kb][0:96, r * T:(r + 1) * T], kvall.gat(r, h * 96, (h + 1) * 96),
                         [kvall.rres(h * 96, (h + 1) * 96)], [self.RK[kb]])
                vseg = 896 + 256 * (h // 4)
                vsrc = kvall.gat(r, vseg, vseg + 256).rearrange("r (b x) -> (r b) x", b=2)[(h % 4) * 128:(h % 4 + 1) * 128, :]
                self.dma("sp", self.Vb[kb][:, r * NT:(r + 1) * NT, 0:64], vsrc.rearrange("p (t c) -> p t c", c=64),
                         [kvall.rres(vseg, vseg + 256)], [self.RV[kb]])

        def load_gqa(hk, kb):
            for r in range(4):
                for dup in range(2):
                    self.dma("sp", self.Kb[kb][dup * 64:(dup + 1) * 64, r * T:(r + 1) * T],
                             kvall.gat(r, 768 + hk * 64, 768 + (hk + 1) * 64),
                             [kvall.rres(768, 896)], [self.RK[kb]])
                vsrc = kvall.gat(r, 1408, 1536).rearrange("r (b x) -> (r b) x", b=2)[hk * 128:(hk + 1) * 128, :]
                self.dma("sp", self.Vb[kb][:, r * NT:(r + 1) * NT, 0:64], vsrc.rearrange("p (t c) -> p t c", c=64),
                         [kvall.rres(1408, 1536)], [self.RV[kb]])

        for u in range(10):
            kb = u % 2
            if u < 8:
                h = u
                run_map(kb, slice(0, 96), lambda sl, h=h: qTm[0:96, h, sl], 96 ** -0.5, h * 64,
                        lambda h=h, kb=kb: load_mla(h, kb))
            else:
                hk = u - 8
                for g4 in range(4):
                    qh = hk * 4 + g4
                    rows = slice((qh % 2) * 64, (qh % 2) * 64 + 64)
                    run_map(kb, rows, lambda sl, rows=rows, qh=qh: qTg[rows, qh // 2, sl], 0.125, 512 + qh * 64,
                            (lambda hk=hk, kb=kb: load_gqa(hk, kb)) if g4 == 0 else None)
        self.run_attn()

    def build(self):
        cfg = self.cfg
        T = cfg.T
        stages = self.stages
        self.setup()
        self.d_wgu = self.inp("wgu", [4, NJ, 128, 2048])
        self.d_wd = self.inp("wd", [4, 2, 8, 128, NJH * 128])
        xv = lambda ap: ap.rearrange("(c p) t -> p c t", p=128)
        allx = [r for row in self.Rx for r in row]
        if 1 in stages:
            d_x = self.inp("xT", [D, T])
            self.d_evw_fm = self.inp("evw_fm", [12, 128, 1024])
            self.d_evw_tm = self.inp("evw_tm", [2, 128, 4096])
            self.d_wsT = self.inp("wsT", [128, 1024])
            self.d_bias_t = self.inp("bias_t", [128, 512])
            self.d_cs_e = self.inp("cs_e", [2, 128, T])
            for c in range(8):
                self.dma("sp", self.xres[:, c, :], d_x[c * 128:(c + 1) * 128, :], [], self.Rx[c])
            self.ffn(0, V_FFN + 0)
            kv_e = self.make_kv("e", [256] * 4 if self.fused else [1024])
            self.even_prep(V_FFN + 8, kv_e)
        if 2 in stages:
            self.d_evwo = self.inp("evwo", [8, 128, 1024])
            self.d_odw_fm = self.inp("odw_fm", [8, 128, 1024])
            self.d_odw_kpe = self.inp("odw_kpe", [128, 256])
            self.d_odw_gv = self.inp("odw_gv", [128, 1024])
            self.d_wuq = self.inp("wuq", [128, 1536])
            self.d_wkp = self.inp("wkp", [128, 768])
            self.d_wvm = self.inp("wvm", [128, 512])
            self.d_cs_m = self.inp("cs_m", [2, 128, T])
            self.d_cs_g = self.inp("cs_g", [2, 128, T])
            if self.fused:
                kvall_e = self.gather(kv_e, [2, 3])
            else:
                kvall_e = KVScr([1024])
                kvall_e.gat_t[0] = self.inp("kvall_e", [4 * 1024, T], BF16)
                self.P.barrier()
                d_xs = self.inp("xs_in", [D, T])
                d_q = self.inp("q_in", [512, T], BF16)
                d_m = self.inp("m_in", [512, T], BF16)
                for c in range(8):
                    self.dma("sp", self.xres[:, c, :], d_xs[c * 128:(c + 1) * 128, :], [], self.Rx[c])
                qT = self.fixed(self.off_qT(False), 4 * T).rearrange("p (c t) -> p c t", c=4)
                mixT = self.fixed(self.off_mixT(), 8 * T).rearrange("p (c t) -> p c t", c=8)
                self.dma("sp", qT, xv(d_q), [], [self.R("qT")])
                self.dma("sp", mixT[:, 0:4, :], xv(d_m), [], [self.R("mixT")])
            self.even_attn(kvall_e)
            self.out_proj(False, self.d_evwo)
            self.ffn(1, V_FFN + 16)
            self.ffn(2, V_FFN + 24)
            kv_o = self.make_kv("o", [192] * 4 + [128] + [256] * 2 + [128] if self.fused else [1536])
            self.odd_prep(V_FFN + 32, kv_o)
        if 3 in stages:
            self.d_odwo = self.inp("odwo", [8, 128, 1024])
            if self.fused:
                kvall_o = self.gather(kv_o, [4, 7])
            else:
                kvall_o = KVScr([1536])
                kvall_o.gat_t[0] = self.inp("kvall_o", [4 * 1536, T], BF16)
                self.P.barrier()
                d_xs = self.inp("xs_in", [D, T])
                d_q = self.inp("qo_in", [12 * 128, T], BF16)
                for c in range(8):
                    self.dma("sp", self.xres[:, c, :], d_xs[c * 128:(c + 1) * 128, :], [], self.Rx[c])
                qTo = self.fixed(self.off_qT(True), 12 * T).rearrange("p (c t) -> p c t", c=12)
                self.dma("sp", qTo, xv(d_q), [], [self.R("qT")])
            self.odd_attn(kvall_o)
            self.out_proj(True, self.d_odwo)
            self.ffn(3, V_FFN + 40)
        self.P.barrier()
        last = max(stages)
        if last == 3:
            d_o = self.outp("outT", [D, T])
            for c in range(8):
                self.dma("sp", d_o[c * 128:(c + 1) * 128, :], self.xres[:, c, :], self.Rx[c], [Res()])
        else:
            d_o = self.outp("xs_out", [D, T])
            for c in range(8):
                self.dma("sp", d_o[c * 128:(c + 1) * 128, :], self.xres[:, c, :], self.Rx[c], [Res()])
            if last == 1:
                qT = self.fixed(self.off_qT(False), 4 * T).rearrange("p (c t) -> p c t", c=4)
                mixT = self.fixed(self.off_mixT(), 8 * T).rearrange("p (c t) -> p c t", c=8)
                self.dma("sp", xv(self.outp("q_out", [512, T], BF16)), qT, [self.R("qT")], [Res()])
                self.dma("sp", xv(self.outp("m_out", [512, T], BF16)), mixT[:, 0:4, :], [self.R("mixT")], [Res()])
            else:
                qTo = self.fixed(self.off_qT(True), 12 * T).rearrange("p (c t) -> p c t", c=12)
                self.dma("sp", xv(self.outp("qo_out", [12 * 128, T], BF16)), qTo, [self.R("qT")], [Res()])
        self.P.emit()
        self.st.close()
        return self.nc

    def make_kv(self, tag, segs):
        T = self.cfg.T
        kv = KVScr(segs)
        for i, n in enumerate(segs):
            if self.fused:
                kv.loc_t[i] = self.nc.dram_tensor("kv%s%d" % (tag, i), [n, T], BF16).ap()
                kv.gat_t[i] = self.nc.dram_tensor("kvall%s%d" % (tag, i), [4 * n, T], BF16).ap()
            else:
                kv.loc_t[i] = self.outp("kv_" + tag, [n, T], BF16)
        return kv

    def gather(self, kv, idxs=None):
        if not self.fused:
            return kv
        for i in (range(len(kv.segs)) if idxs is None else idxs):
            if i in kv.done:
                continue
            kv.done.add(i)
            o = self.P.op("pool", (lambda a, b_: (lambda e: e.collective_compute(
                "AllGather", ALU.bypass, replica_groups=[[0, 1, 2, 3], [4, 5, 6, 7]],
                ins=[a.opt()], outs=[b_.opt()])))(kv.loc_t[i], kv.gat_t[i]),
                list(self.kvres), [kv.res[i]], dma=True, inc=1)
            o.cc = True
        return kv


def prep_shared(inp):
    f = lambda a: np.ascontiguousarray(np.asarray(a, dtype=np.float32))
    g = {k: f(v) for k, v in inp.items() if k != "x"}
    out = {}
    wgu = []
    wd = []
    for l in range(2):
        for nm in ("ffn1", "ffn2"):
            w = g[nm + "_w_gu"][l]
            wgu.append(w.reshape(8, 128, 2, NJ, 128).transpose(3, 1, 0, 2, 4).reshape(NJ, 128, 2048))
            w = g[nm + "_w_down"][l]
            wd.append(w.reshape(2, NJH, 128, 8, 128).transpose(0, 3, 2, 1, 4).reshape(2, 8, 128, NJH * 128))
    out["wgu"] = np.ascontiguousarray(np.stack(wgu))
    out["wd"] = np.ascontiguousarray(np.stack(wd))
    w = g["ev_w_in"][0]
    cols = [np.arange(i * 128, (i + 1) * 128) for i in range(4)]
    cols += [np.arange(1024 + h * 128, 1024 + (h + 1) * 128) for h in range(4)]
    cols += [np.arange(1536 + h * 128, 1536 + (h + 1) * 128) for h in range(4)]
    out["evw_fm"] = np.stack([_chunk_fm(w, c) for c in cols])
    out["evw_tm"] = np.stack([_chunk_fm(w, np.arange(512, 1024)), _chunk_fm(w, np.arange(2048, 2560))])
    out["evwo"] = np.stack([_chunk_fm(g["ev_w_out"][0], np.arange(dc * 128, (dc + 1) * 128)) for dc in range(8)])
    out["wsT"] = np.ascontiguousarray(g["ev_w_s"][0].transpose(2, 0, 1).reshape(128, 1024))
    bs = g["ev_b_s"][0]
    bt = np.zeros((128, 4, 128), np.float32)
    for cg in range(4):
        bt[:64, cg, :] = bs[2 * cg][None, :]
        bt[64:, cg, :] = bs[2 * cg + 1][None, :]
    out["bias_t"] = bt.reshape(128, 512)
    w = g["od_w_in"][0]
    cols = [np.arange(0, 128), np.arange(128, 256), np.arange(256, 384)]
    cols += [np.arange(416 + i * 128, 416 + (i + 1) * 128) for i in range(4)]
    cols += [np.arange(928, 1056)]
    out["odw_fm"] = np.stack([_chunk_fm(w, c) for c in cols])
    out["odw_kpe"] = _chunk_fm(w, np.arange(384, 416))
    out["odw_gv"] = _chunk_fm(w, np.arange(1056, 1184))
    out["wuq"] = np.ascontiguousarray(g["od_w_uq"][0].reshape(2, 128, 768).transpose(1, 0, 2).reshape(128, 1536))
    wukv = g["od_w_ukv"][0]
    wkp = np.zeros((128, 8, 96), np.float32)
    wvm = np.zeros((128, 8, 64), np.float32)
    for h in range(8):
        wkp[:, h, :64] = wukv[:, h * 128: h * 128 + 64]
        wvm[:, h, :] = wukv[:, h * 128 + 64: h * 128 + 128]
    out["wkp"] = wkp.reshape(128, 768)
    out["wvm"] = wvm.reshape(128, 512)
    out["odwo"] = np.stack([_chunk_fm(g["od_w_out"][0], np.arange(dc * 128, (dc + 1) * 128)) for dc in range(8)])
    vecs = np.zeros((128, NVEC), np.float32)
    norms = [g["ffn1_norm"][0], g["ev_norm"][0], g["ffn2_norm"][0], g["ffn1_norm"][1], g["od_norm"][0], g["ffn2_norm"][1]]
    for i, nv in enumerate(norms):
        vecs[:, V_FFN + 8 * i: V_FFN + 8 * i + 8] = nv.reshape(8, 128).T
    vecs[:, V_EQ] = np.tile(g["ev_q_norm"][0], 2)
    vecs[:, V_EK] = np.tile(g["ev_k_norm"][0], 2)
    vecs[:, V_CQ:V_CQ + 2] = g["od_cq_norm"][0].reshape(2, 128).T
    vecs[:, V_CKV] = g["od_ckv_norm"][0]
    vecs[:96, V_MQ] = g["od_mla_q_norm"][0]
    vecs[:96, V_MK] = g["od_mla_k_norm"][0]
    vecs[:, V_GQ] = np.tile(g["od_gqa_q_norm"][0], 2)
    vecs[:, V_GK] = np.tile(g["od_gqa_k_norm"][0], 2)
    out["vecs"] = vecs
    rows = np.zeros((1, NROW), np.float32)
    rows[0, R_SGU:R_SGU + 512] = g["ev_sgu_norm"][0].reshape(512)
    rows[0, R_SUB:R_SUB + 128] = g["ev_sub_norm"][0]
    rows[0, R_LAM:R_LAM + 256] = np.concatenate([g["ev_lam_q1"][0], g["ev_lam_k1"][0], g["ev_lam_q2"][0], g["ev_lam_k2"][0]])
    out["rows"] = rows
    return out


_NP = {F32: np.float32, BF16: ml_dtypes.bfloat16}


def run_stage(cfg, stages, fused, shared, percore, runner=None):
    b = Builder(cfg, stages, fused)
    nc = b.build()
    in_maps = []
    for c in range(8):
        m = {}
        for name, (shape, dt) in b.din.items():
            a = percore[c][name] if name in percore[c] else shared[name]
            assert tuple(a.shape) == tuple(shape), (name, a.shape, shape)
            m[name] = np.ascontiguousarray(a, dtype=_NP[dt])
        in_maps.append(m)
    if runner is None:
        res = run_bass_kernel_spmd(nc, in_maps, core_ids=list(range(8)))
        return res.results
    return runner(nc, in_maps, {k: (s, _NP[d]) for k, (s, d) in b.dout.items()})


def run_model(inp, T=2048, TG=512, fused=False, runner=None):
    cfg = Cfg(T, TG)
    x = np.asarray(inp["x"], dtype=np.float32)
    B, S, _ = x.shape
    assert B == 2 and S == 4 * T
    shared = prep_shared(inp)
    xt = x.reshape(8, T, D)
    percore = []
    for c in range(8):
        cmat, cs_e, cs_m, cs_g = host_consts(T, S, c)
        percore.append({"xT": np.ascontiguousarray(xt[c].T), "cmat": cmat, "cs_e": cs_e, "cs_m": cs_m, "cs_g": cs_g})
    if fused:
        res = run_stage(cfg, (1, 2, 3), True, shared, percore, runner)
    else:
        def regroup(res, key):
            full = [np.concatenate([res[g * 4 + r][key] for r in range(4)], axis=0) for g in range(2)]
            return [full[c // 4] for c in range(8)]
        r1 = run_stage(cfg, (1,), False, shared, percore, runner)
        kva = regroup(r1, "kv_e")
        for c in range(8):
            percore[c].update({"kvall_e": kva[c], "xs_in": r1[c]["xs_out"], "q_in": r1[c]["q_out"], "m_in": r1[c]["m_out"]})
        r2 = run_stage(cfg, (2,), False, shared, percore, runner)
        kva = regroup(r2, "kv_o")
        for c in range(8):
            percore[c].update({"kvall_o": kva[c], "xs_in": r2[c]["xs_out"], "qo_in": r2[c]["qo_out"]})
        res = run_stage(cfg, (3,), False, shared, percore, runner)
    out = np.stack([np.asarray(res[c]["outT"], dtype=np.float32).T for c in range(8)])
    return np.ascontiguousarray(out.reshape(B, S, D))


def kernel(**inputs):
    return run_model(inputs, T=2048, TG=512, fused=True)
```

```python
import contextlib
import numpy as np
import ml_dtypes
import concourse.bass as bass
import concourse.mybir as mybir
from concourse.bass_utils import run_bass_kernel_spmd

F32 = mybir.dt.float32
BF16 = mybir.dt.bfloat16
AF = mybir.ActivationFunctionType
ALU = mybir.AluOpType
AX = mybir.AxisListType

D = 1024
DFF = 2816
NJ = 22
NJH = 11
EPS = 1e-6
THETA = 10000.0
GRID_W = 64
ENGS = ("pe", "act", "dve", "pool", "sp")


class Res:
    __slots__ = ("name", "w", "rs")

    def __init__(self, name=""):
        self.name = name
        self.w = None
        self.rs = []


class Op:
    __slots__ = ("eng", "fn", "deps", "pos", "dma", "signal", "sem", "target", "prev_target", "inc", "cc")

    def __init__(self, eng, fn, dma, inc):
        self.eng = eng
        self.fn = fn
        self.dma = dma
        self.deps = []
        self.pos = -1
        self.signal = False
        self.sem = None
        self.target = 0
        self.prev_target = 0
        self.inc = inc
        self.cc = False


class Prog:
    NDMA_SEMS = 8

    def __init__(self, nc):
        self.nc = nc
        self.ops = {e: [] for e in ENGS}
        self.pending_barrier = {e: [] for e in ENGS}

    def op(self, eng, fn, reads=(), writes=(), dma=False, inc=16):
        o = Op(eng, fn, dma, inc)
        o.pos = len(self.ops[eng])
        deps = set(self.pending_barrier[eng])
        self.pending_barrier[eng] = []
        rawset = set()
        for r in reads:
            if r.w is not None:
                deps.add(r.w)
                rawset.add(r.w)
        for w in writes:
            if w.w is not None:
                deps.add(w.w)
            for rd in w.rs:
                deps.add(rd)
        best = {}
        red = []
        for d in deps:
            if d.dma:
                red.append(d)
            elif d.eng not in best or best[d.eng].pos < d.pos:
                best[d.eng] = d
        red.extend(best.values())
        for d in red:
            if d is o:
                continue
            if d.dma or o.dma or d.eng != eng:
                o.deps.append(d)
                d.signal = True
            elif eng != "pe" and (o.pos - d.pos) <= 3 and d in rawset:
                o.deps.append(d)
                d.signal = True
        for r in reads:
            r.rs.append(o)
        for w in writes:
            w.w = o
            w.rs = []
        self.ops[eng].append(o)
        return o

    def barrier(self):
        lasts = []
        for e in ENGS:
            comp = [o for o in self.ops[e] if not o.dma]
            if comp:
                lasts.append(comp[-1])
            dmas = [o for o in self.ops[e] if o.dma and not o.cc]
            lasts.extend(dmas[-self.NDMA_SEMS:])
        for e in ENGS:
            self.pending_barrier[e] = list(lasts)

    def emit(self):
        nc = self.nc
        with contextlib.ExitStack() as st:
            esem = {e: st.enter_context(nc.semaphore("s_" + e)) for e in ENGS}
            dsem = {}
            for q in ENGS:
                nd = sum(1 for o in self.ops[q] if o.dma and not o.cc)
                if nd:
                    dsem[q] = [st.enter_context(nc.semaphore("d_%s%d" % (q, i)))
                               for i in range(min(nd, self.NDMA_SEMS))]
            for e in ENGS:
                c = 0
                rr = 0
                nsem = len(dsem.get(e, []))
                tot = [0] * max(nsem, 1)
                for o in self.ops[e]:
                    if o.cc:
                        o.sem = st.enter_context(nc.semaphore("cc_%d" % o.pos))
                        o.prev_target = 0
                        o.target = o.inc
                    elif o.dma:
                        o.sem = dsem[e][rr]
                        o.prev_target = tot[rr]
                        tot[rr] += o.inc
                        o.target = tot[rr]
                        rr = (rr + 1) % nsem
                    elif o.signal:
                        c += 1
                        o.sem = esem[e]
                        o.target = c
            block = st.enter_context(nc.Block())

            def run(e, eng):
                seen = {}
                for o in self.ops[e]:
                    waits = {}
                    for d in o.deps:
                        key = id(d.sem)
                        if seen.get(key, 0) >= d.target:
                            continue
                        if key not in waits or waits[key][1] < d.target:
                            waits[key] = (d.sem, d.target)
                    if o.dma and o.prev_target > 0:
                        key = id(o.sem)
                        if seen.get(key, 0) < o.prev_target:
                            if key not in waits or waits[key][1] < o.prev_target:
                                waits[key] = (o.sem, o.prev_target)
                    for key, (s, v) in waits.items():
                        eng.wait_ge(s, v)
                        seen[key] = v
                    ins = o.fn(eng)
                    if o.dma:
                        ins.then_inc(o.sem, o.inc)
                    elif o.signal:
                        ins.then_inc(o.sem, 1)
                if e in dsem or any(o.cc for o in self.ops[e]):
                    tot = {}
                    for o in self.ops[e]:
                        if o.dma:
                            tot[id(o.sem)] = (o.sem, o.target)
                    for key, (s, v) in tot.items():
                        if seen.get(key, 0) < v:
                            eng.wait_ge(s, v)

            block.tensor(lambda eng: run("pe", eng))
            block.scalar(lambda eng: run("act", eng))
            block.vector(lambda eng: run("dve", eng))
            block.gpsimd(lambda eng: run("pool", eng))
            block.sync(lambda eng: run("sp", eng))


def _chunk_fm(w, cols):
    sub = w[:, cols]
    m = sub.shape[1]
    return np.ascontiguousarray(sub.reshape(8, 128, m).transpose(1, 0, 2).reshape(128, 8 * m))


def _rope_inv(dim):
    return (np.float32(THETA) ** (-np.arange(0, dim, 2, dtype=np.float32) / np.float32(dim))).astype(np.float32)


def host_consts(T, S, core):
    r = core % 4
    pos = (r * T + np.arange(T)).astype(np.int32)
    eye = np.eye(128, dtype=np.float32)
    ones = np.ones((128, 128), np.float32)
    bd = np.zeros((128, 128), np.float32)
    bd[:64, :64] = 1
    bd[64:, 64:] = 1
    ones96 = np.zeros((128, 128), np.float32)
    ones96[:96, :96] = 1

    def rot(blocks):
        m = np.zeros((128, 128), np.float32)
        for (b0, half) in blocks:
            for d in range(half):
                m[b0 + d + half, b0 + d] = -1.0
                m[b0 + d, b0 + d + half] = 1.0
        return m
    rm_e = rot([(0, 32), (64, 32)])
    rm_mla = rot([(64, 16)])
    rm_gqa = rot([(0, 16), (32, 16), (64, 16), (96, 16)])
    sel = np.zeros((128, 128), np.float32)
    for i in range(32):
        sel[i, 64 + i] = 1.0
    cmat = np.concatenate([eye, ones, bd, ones96, rm_e, rm_mla, rm_gqa, sel], axis=1)

    def angles(p, dim):
        inv = _rope_inv(dim)
        return p.astype(np.float32)[:, None] * inv[None, :]
    a = angles(pos, 64)
    idx = (np.arange(128) % 64) % 32
    cs_e = np.stack([np.cos(a)[:, idx].T, np.sin(a)[:, idx].T]).astype(np.float32)
    a = angles(pos, 32)
    cs_m = np.zeros((2, 128, T), np.float32)
    cs_m[0, :64] = 1.0
    idx = (np.arange(32)) % 16
    cs_m[0, 64:96] = np.cos(a)[:, idx].T
    cs_m[1, 64:96] = np.sin(a)[:, idx].T
    ar = angles(pos // GRID_W, 32)
    ac = angles(pos % GRID_W, 32)
    cs_g = np.zeros((2, 128, T), np.float32)
    for p in range(128):
        d = p % 64
        src = ar if d < 32 else ac
        f = (d % 32) % 16
        cs_g[0, p] = np.cos(src)[:, f]
        cs_g[1, p] = np.sin(src)[:, f]
    return cmat, cs_e, cs_m, cs_g


class Cfg:
    def __init__(self, T, TG):
        self.T = T
        self.S = 4 * T
        self.TG = TG
        self.NT = T // 128
        self.NG = T // TG
        self.TPG = TG // 128
        self.KT = self.S // 128
        self.arena = max(34 * T + 2048, 44000)


V_FFN = 0
V_EQ = 48
V_EK = 49
V_CQ = 50
V_CKV = 52
V_MQ = 53
V_MK = 54
V_GQ = 55
V_GK = 56
NVEC = 57
R_SGU = 0
R_SUB = 512
R_LAM = 640
NROW = 896
C_ID, C_ONES, C_BD, C_O96, C_RME, C_RMM, C_RMG, C_SEL = range(8)


class KVScr:
    def __init__(self, segs):
        self.segs = list(segs)
        self.b0 = [0]
        for n in self.segs:
            self.b0.append(self.b0[-1] + n)
        self.loc_t = [None] * len(self.segs)
        self.gat_t = [None] * len(self.segs)
        self.res = [Res("kvseg%d" % i) for i in range(len(self.segs))]
        self.done = set()

    def rres(self, a, b):
        return self.res[self._find(a, b)]

    def _find(self, a, b):
        for i, n in enumerate(self.segs):
            if self.b0[i] <= a and b <= self.b0[i + 1]:
                return i
        raise AssertionError(("kv rows cross a segment", a, b))

    def loc(self, a, b):
        i = self._find(a, b)
        return self.loc_t[i][a - self.b0[i]: b - self.b0[i], :]

    def gat(self, r, a, b):
        i = self._find(a, b)
        n = self.segs[i]
        return self.gat_t[i][r * n + a - self.b0[i]: r * n + b - self.b0[i], :]


class Builder:
    def __init__(self, cfg, stages, fused):
        self.cfg = cfg
        self.stages = stages
        self.fused = fused
        self.nc = bass.Bass("TRN2", target_bir_lowering=False)
        self.P = Prog(self.nc)
        self.st = contextlib.ExitStack()
        self.din = {}
        self.dout = {}
        self.res = {}

    def R(self, name):
        if name not in self.res:
            self.res[name] = Res(name)
        return self.res[name]

    def inp(self, name, shape, dt=F32):
        t = self.nc.dram_tensor(name, list(shape), dt, kind="ExternalInput")
        self.din[name] = (tuple(shape), dt)
        return t.ap()

    def outp(self, name, shape, dt=F32):
        t = self.nc.dram_tensor(name, list(shape), dt, kind="ExternalOutput")
        self.dout[name] = (tuple(shape), dt)
        return t.ap()

    def sb(self, name, shape, dt):
        return self.st.enter_context(self.nc.sbuf_tensor(name, list(shape), dt))

    def mm(self, out, lhsT, rhs, start, stop, reads, writes):
        self.P.op("pe", lambda e: e.matmul(out, lhsT=lhsT, rhs=rhs, start=start, stop=stop),
                  reads, writes)

    def act(self, out, in_, func, reads, writes, scale=1.0, bias=None, accum_out=None, eng="act"):
        kw = {}
        if bias is not None:
            kw["bias"] = bias
        if accum_out is not None:
            kw["accum_out"] = accum_out
        self.P.op("act", lambda e: e.activation(out=out, in_=in_, func=func, scale=scale, **kw),
                  reads, writes)

    def tt(self, eng, out, in0, in1, op, reads, writes):
        self.P.op(eng, lambda e: e.tensor_tensor(out=out, in0=in0, in1=in1, op=op), reads, writes)

    def stt(self, eng, out, in0, scalar, in1, op0, op1, reads, writes):
        self.P.op(eng, lambda e: e.scalar_tensor_tensor(out=out, in0=in0, scalar=scalar, in1=in1,
                                                        op0=op0, op1=op1), reads, writes)

    def ts(self, eng, out, in0, s1, op0, reads, writes, s2=None, op1=None):
        if op1 is None:
            self.P.op(eng, lambda e: e.tensor_scalar(out=out, in0=in0, scalar1=s1, scalar2=None, op0=op0),
                      reads, writes)
        else:
            self.P.op(eng, lambda e: e.tensor_scalar(out=out, in0=in0, scalar1=s1, scalar2=s2, op0=op0,
                                                     op1=op1), reads, writes)

    def cp(self, eng, out, in_, reads, writes):
        if eng == "act":
            self.P.op("act", lambda e: e.copy(out=out, in_=in_), reads, writes)
        else:
            self.P.op(eng, lambda e: e.tensor_copy(out=out, in_=in_), reads, writes)

    def recip(self, out, in_, reads, writes):
        self.P.op("dve", lambda e: e.reciprocal(out=out, in_=in_), reads, writes)

    def dma(self, q, out, in_, reads, writes):
        self.P.op(q, lambda e: e.dma_start(out=out, in_=in_), reads, writes, dma=True)

    def arena_reset(self, lo=0, hi=None):
        self.a_lo = lo
        self.a_hi = self.cfg.arena if hi is None else hi

    def take(self, n, dt=BF16):
        nb = n * (2 if dt == F32 else 1)
        nb = (nb + 15) // 16 * 16
        off = self.a_lo
        self.a_lo += nb
        assert self.a_lo <= self.a_hi, ("arena overflow", self.a_lo, self.a_hi)
        ap = self.arena[:, off:off + n * (2 if dt == F32 else 1)]
        if dt == F32:
            ap = ap.bitcast(F32)
        return ap

    def fixed(self, off, n):
        return self.arena[:, off:off + n]

    def setup(self):
        cfg = self.cfg
        T = cfg.T
        self.xres = self.sb("xres", [128, 8, T], F32)
        self.arena = self.sb("arena", [128, cfg.arena], BF16)[:]
        self.cmat = self.sb("cmat_sb", [128, 8 * 128], BF16)
        self.vecs = self.sb("vecs_sb", [128, NVEC], F32)
        self.epsc = self.sb("epsc", [128, 1], F32)
        self.id32 = self.sb("id32", [128, 128], F32)
        self.onescol = self.sb("onescol", [128, 1], BF16)
        self.bank = [self.st.enter_context(self.nc.psum_tensor("bank%d" % i, [128, 512], F32))
                     for i in range(8)]
        self.bres = [self.R("bank%d" % i) for i in range(8)]
        self.Rx = [[self.R("x_%d_%d" % (c, g)) for g in range(cfg.NG)] for c in range(8)]
        d_cmat = self.inp("cmat", [128, 8 * 128])
        d_vecs = self.inp("vecs", [128, NVEC])
        self.d_rows = self.inp("rows", [1, NROW])
        self.dma("pool", self.cmat[:], d_cmat, [], [self.R("cmat")])
        self.dma("sp", self.vecs[:], d_vecs, [], [self.R("vecs")])
        self.dma("sp", self.id32[:], d_cmat[:, 0:128], [], [self.R("id32")])
        self.P.op("dve", lambda e: e.memset(self.epsc[:], EPS), [], [self.R("epsc")])
        self.P.op("dve", lambda e: e.memset(self.onescol[:], 1.0), [], [self.R("onescol")])

    def cm(self, idx, rows=128, cols=128):
        return self.cmat[0:rows, idx * 128: idx * 128 + cols]

    def norm_to_hT(self, hT, vcol, tmp_sq, tmp_rstd, tmp_sd):
        cfg = self.cfg
        T, TG, NG = cfg.T, cfg.TG, cfg.NG
        Rh = self.R("hT")
        Rrs = self.R("rstd")
        for c in range(8):
            sq = tmp_sq[c % 2]
            Rsq = self.R("sq%d" % (c % 2))
            self.act(sq, self.xres[:, c, :], AF.Square, self.Rx[c], [Rsq])
            for g in range(NG):
                self.mm(self.bank[g][:, 0:TG], self.cm(C_ONES), sq[:, g * TG:(g + 1) * TG],
                        c == 0, c == 7, [Rsq, self.R("cmat")], [self.bres[g]])
        for g in range(NG):
            sl = slice(g * TG, (g + 1) * TG)
            self.act(tmp_sd[:, sl], self.bank[g][:, 0:TG], AF.Sqrt, [self.bres[g], self.R("epsc")],
                     [self.R("sd")], scale=1.0 / D, bias=self.epsc[:])
        self.recip(tmp_rstd, tmp_sd, [self.R("sd")], [Rrs])
        for c in range(8):
            self.stt("dve", hT[:, c, :], self.xres[:, c, :], self.vecs[:, vcol + c: vcol + c + 1], tmp_rstd,
                     ALU.mult, ALU.mult, self.Rx[c] + [Rrs, self.R("vecs")], [Rh])

    def ffn(self, fi, vcol):
        cfg = self.cfg
        T, TG, NG = cfg.T, cfg.TG, cfg.NG
        P = self.P
        P.barrier()
        self.arena_reset()
        hT = self.take(8 * T).rearrange("p (c t) -> p c t", c=8)
        actA = self.take(NJH * T).rearrange("p (j t) -> p j t", j=NJH)
        wg = [self.take(2048).rearrange("p (c m) -> p c m", c=8) for _ in range(3)]
        wdb = [self.take(NJH * 128).rearrange("p (j m) -> p j m", j=NJH) for _ in range(3)]
        rstd = self.take(T, F32)
        sd = self.take(T, F32)
        sq = [self.take(T) for _ in range(2)]
        sg = [self.take(TG, F32) for _ in range(2)]
        Rwg = [self.R("wg%d" % i) for i in range(3)]
        Rwd = [self.R("wd%d" % i) for i in range(3)]
        Rsg = [self.R("sg%d" % i) for i in range(2)]
        Rh, Ra = self.R("hT"), self.R("actA")
        self.norm_to_hT(hT, vcol, sq, rstd, sd)
        d_wgu = self.d_wgu
        d_wd = self.d_wd
        step = 0
        wi = 0
        di = 0
        for half in range(2):
            for j in range(NJH):
                jj = half * NJH + j
                b = wi % 3
                wi += 1
                self.dma("pool", wg[b].rearrange("p c m -> p (c m)"), d_wgu[fi, jj], [], [Rwg[b]])
                for g in range(NG):
                    sl = slice(g * TG, (g + 1) * TG)
                    pb = (step % 2) * 2
                    step += 1
                    for c in range(8):
                        self.mm(self.bank[pb][:, 0:TG], wg[b][:, c, 0:128], hT[:, c, sl], c == 0, c == 7,
                                [Rwg[b], Rh], [self.bres[pb]])
                    for c in range(8):
                        self.mm(self.bank[pb + 1][:, 0:TG], wg[b][:, c, 128:256], hT[:, c, sl], c == 0, c == 7,
                                [Rwg[b], Rh], [self.bres[pb + 1]])
                    s = sg[g % 2]
                    self.act(s, self.bank[pb][:, 0:TG], AF.Silu, [self.bres[pb]], [Rsg[g % 2]])
                    self.tt("dve", actA[:, j, sl], s, self.bank[pb + 1][:, 0:TG], ALU.mult,
                            [Rsg[g % 2], self.bres[pb + 1]], [Ra])
            for dc in range(8):
                b = di % 3
                di += 1
                self.dma("pool", wdb[b].rearrange("p j m -> p (j m)"), d_wd[fi, half, dc], [], [Rwd[b]])
                for g in range(NG):
                    sl = slice(g * TG, (g + 1) * TG)
                    pb = 4 + (step % 2)
                    step += 1
                    for j in range(NJH):
                        self.mm(self.bank[pb][:, 0:TG], wdb[b][:, j, :], actA[:, j, sl], j == 0, j == NJH - 1,
                                [Rwd[b], Ra], [self.bres[pb]])
                    self.stt("dve", self.xres[:, dc, sl], self.bank[pb][:, 0:TG], 0.5, self.xres[:, dc, sl],
                             ALU.mult, ALU.add, [self.bres[pb], self.Rx[dc][g]], [self.Rx[dc][g]])

    def nr_alloc(self):
        TG = self.cfg.TG
        self.nr = []
        for k in range(3):
            self.nr.append(dict(qf=self.take(TG, F32), sd=self.take(TG, F32), t2=self.take(TG, F32),
                                sqb=self.take(TG), qnb=self.take(TG)))
        self.nrk = 0
        self.nrq = []

    def nr_flush(self):
        while self.nrq:
            self._nr_advance()

    def _nr_advance(self):
        q = self.nrq
        for ent in list(q):
            if ent["stage"] == 2:
                ent["s3"]()
                q.remove(ent)
            elif ent["stage"] == 1:
                ent["s2"]()
                ent["stage"] = 2

    def normrope(self, ps, Rps, R, onesidx, nnorm, gcol, rmidx, cos_ap, sin_ap, Rtab, out, Rout, then=None):
        TG = self.cfg.TG
        i = self.nrk
        self.nrk += 1
        k = i % 3
        t = self.nr[k]
        Rn = lambda n: self.R("nr_%s%d" % (n, k))
        bs = 2 + i % 2
        br = 4 + i % 2
        qf, sd, t2, sqb, qnb = (t["qf"][0:R], t["sd"][0:R], t["t2"][0:R], t["sqb"][0:R], t["qnb"][0:R])
        self.cp("act", qf, ps, [Rps], [Rn("qf")])
        self.act(sqb, ps, AF.Square, [Rps], [Rn("sqb")])
        self.mm(self.bank[bs][0:R, 0:TG], self.cm(onesidx, R, R), sqb, True, True,
                [Rn("sqb"), self.R("cmat")], [self.bres[bs]])
        self.act(sd, self.bank[bs][0:R, 0:TG], AF.Sqrt, [self.bres[bs], self.R("epsc")], [Rn("sd")],
                 scale=1.0 / nnorm, bias=self.epsc[0:R, :])

        def s2():
            self.recip(sd, sd, [Rn("sd")], [Rn("sd")])
            self.stt("dve", qf, qf, self.vecs[0:R, gcol:gcol + 1], sd, ALU.mult, ALU.mult,
                     [Rn("qf"), Rn("sd"), self.R("vecs")], [Rn("qf")])
            self.cp("pool", qnb, qf, [Rn("qf")], [Rn("qnb")])
            self.mm(self.bank[br][0:R, 0:TG], self.cm(rmidx, R, R), qnb, True, True,
                    [Rn("qnb"), self.R("cmat")], [self.bres[br]])

        def s3():
            self.tt("dve", t2, self.bank[br][0:R, 0:TG], sin_ap, ALU.mult, [self.bres[br], Rtab], [Rn("t2")])
            self.tt("pool", qf, qf, cos_ap, ALU.mult, [Rn("qf"), Rtab], [Rn("qf")])
            self.tt("pool", out, qf, t2, ALU.add, [Rn("qf"), Rn("t2")], [Rout])
            if then is not None:
                then()
        self._nr_advance()
        self.nrq.append(dict(stage=1, s2=s2, s3=s3))

    def off_qT(self, odd):
        return self.cfg.arena - (12 if odd else 4) * self.cfg.T

    def off_mixT(self):
        return self.cfg.arena - 12 * self.cfg.T

    def off_otok(self, odd):
        return self.off_mixT() - (8 if odd else 4) * self.cfg.T

    def even_prep(self, vcol, kv_scr):
        cfg = self.cfg
        T, TG, NG, NT = cfg.T, cfg.TG, cfg.NG, cfg.NT
        P = self.P
        P.barrier()
        qT = self.fixed(self.off_qT(False), 4 * T).rearrange("p (c t) -> p c t", c=4)
        mixT = self.fixed(self.off_mixT(), 8 * T).rearrange("p (c t) -> p c t", c=8)
        Rq, Rm_, Rh = self.R("qT"), self.R("mixT"), self.R("hT")
        self.arena_reset(0, self.off_mixT())
        hT = self.take(8 * T).rearrange("p (c t) -> p c t", c=8)
        mark = self.a_lo
        rstd = self.take(T, F32)
        sd = self.take(T, F32)
        sq = [self.take(T) for _ in range(2)]
        self.norm_to_hT(hT, vcol, sq, rstd, sd)
        P.barrier()
        self.arena_reset(mark, self.off_mixT())
        cosT = self.take(T, F32)
        sinT = self.take(T, F32)
        self.nr_alloc()
        wb = [self.take(1024).rearrange("p (c m) -> p c m", c=8) for _ in range(3)]
        kst = [self.take(TG) for _ in range(4)]
        Rwb = [self.R("wb%d" % i) for i in range(3)]
        Rks = [self.R("kst%d" % i) for i in range(4)]
        Rtab = self.R("tab")
        self.kvres = []

        def Rkv_new():
            r = Res("kvw")
            self.kvres.append(r)
            return r
        self.dma("sp", cosT, self.d_cs_e[0], [], [Rtab])
        self.dma("sp", sinT, self.d_cs_e[1], [], [Rtab])
        step = 0
        ks = 0
        for ci in range(12):
            b = ci % 3
            self.dma("pool", wb[b].rearrange("p c m -> p (c m)"), self.d_evw_fm[ci], [], [Rwb[b]])
            kind, h = ("u", "q", "k")[ci // 4], ci % 4
            for g in range(NG):
                sl = slice(g * TG, (g + 1) * TG)
                pb = step % 2
                step += 1
                for c in range(8):
                    self.mm(self.bank[pb][:, 0:TG], wb[b][:, c, :], hT[:, c, sl], c == 0, c == 7,
                            [Rwb[b], Rh], [self.bres[pb]])
                ps = self.bank[pb][:, 0:TG]
                if kind == "u":
                    self.act(mixT[:, h, sl], ps, AF.Gelu_apprx_tanh, [self.bres[pb]], [Rm_])
                elif kind == "q":
                    self.normrope(ps, self.bres[pb], 128, C_BD, 64, V_EQ, C_RME, cosT[:, sl], sinT[:, sl], Rtab,
                                  qT[:, h, sl], Rq)
                else:
                    kb = ks % 4
                    ks += 1
                    self.normrope(ps, self.bres[pb], 128, C_BD, 64, V_EK, C_RME, cosT[:, sl], sinT[:, sl], Rtab,
                                  kst[kb], Rks[kb],
                                  then=(lambda h=h, sl=sl, kb=kb: self.dma(
                                      "sp", kv_scr.loc(h * 128, (h + 1) * 128)[:, sl], kst[kb], [Rks[kb]], [Rkv_new()])))
        self.nr_flush()
        self.gather(kv_scr, [0, 1])
        P.barrier()
        self.arena_reset(mark, self.off_mixT())
        wtm = self.take(4096).rearrange("p (c m) -> p c m", c=8)
        wsT = self.take(1024).rearrange("p (g i) -> p g i", g=8)
        Gt = self.take(512, F32)
        biast = self.take(512, F32)
        vg = [self.take(512, F32) for _ in range(2)]
        sqv = self.take(512, F32)
        ss8 = [self.take(8, F32) for _ in range(2)]
        vc = [self.take(512) for _ in range(2)]
        tmpm = [self.take(512, F32) for _ in range(2)]
        vst = [self.take(512) for _ in range(2)]
        Rwtm, Rws, Rg = self.R("wtm"), self.R("wsT"), self.R("Gt")
        self.dma("pool", wtm.rearrange("p c m -> p (c m)").rearrange("p (a b) -> p a b", b=2048),
                 self.d_evw_tm[0].rearrange("p (a b) -> p a b", b=2048), [], [Rwtm])
        self.dma("pool", wsT.rearrange("p g i -> p (g i)"), self.d_wsT, [], [Rws])
        self.dma("sp", Gt, self.d_rows[0:1, R_SGU:R_SGU + 512].partition_broadcast(128), [], [Rg])
        self.dma("sp", biast, self.d_bias_t, [], [Rg])
        for tt_ in range(NT):
            k = tt_ % 2
            tsl = slice(tt_ * 128, (tt_ + 1) * 128)
            pb = 6 + k
            Rvg, Rss, Rvc, Rtm = (self.R("vg%d" % k), self.R("ss8%d" % k), self.R("vc%d" % k), self.R("tmpm%d" % k))
            for c in range(8):
                self.mm(self.bank[pb][:, :], hT[:, c, tsl], wtm[:, c, :], c == 0, c == 7, [Rh, Rwtm], [self.bres[pb]])
            self.act(vg[k], self.bank[pb][:, :], AF.Gelu_apprx_tanh, [self.bres[pb]], [Rvg])
            self.tt("dve", sqv, vg[k], vg[k], ALU.mult, [Rvg], [self.R("sqv")])
            self.P.op("dve", (lambda o, i: (lambda e: e.reduce_sum(out=o, in_=i, axis=AX.X)))(
                ss8[k], sqv.rearrange("p (g d) -> p g d", g=8)), [self.R("sqv")], [Rss])
            self.act(ss8[k], ss8[k], AF.Sqrt, [Rss, self.R("epsc")], [Rss], scale=1.0 / 64, bias=self.epsc[:])
            self.recip(ss8[k], ss8[k], [Rss], [Rss])
            vg3 = vg[k].rearrange("p (g d) -> p g d", g=8)
            self.tt("dve", vg3, vg3, ss8[k].unsqueeze(2).to_broadcast([128, 8, 64]), ALU.mult, [Rvg, Rss], [Rvg])
            self.tt("pool", vc[k], vg[k], Gt, ALU.mult, [Rvg, Rg], [Rvc])
            bm = 4 + k
            for g8 in range(8):
                po = (g8 % 2) * 64
                self.mm(self.bank[bm][po:po + 64, (g8 // 2) * 128:(g8 // 2) * 128 + 128],
                        vc[k][:, g8 * 64:(g8 + 1) * 64], wsT[:, g8, :], True, True, [Rvc, Rws], [self.bres[bm]])
            self.tt("dve", tmpm[k], self.bank[bm][:, :], biast, ALU.add, [self.bres[bm], Rg], [Rtm])
            self.tt("pool", mixT[:, 0:4, tsl], tmpm[k].rearrange("p (c i) -> p c i", c=4), mixT[:, 0:4, tsl],
                    ALU.mult, [Rtm, Rm_], [Rm_])
        self.dma("pool", wtm.rearrange("p c m -> p (c m)").rearrange("p (a b) -> p a b", b=2048),
                 self.d_evw_tm[1].rearrange("p (a b) -> p a b", b=2048), [], [Rwtm])
        vs = [kv_scr.loc(512 + 256 * s_, 768 + 256 * s_).rearrange("(h p) (t c) -> h p t c", h=2, c=128) for s_ in range(2)]
        for tt_ in range(NT):
            k = tt_ % 2
            tsl = slice(tt_ * 128, (tt_ + 1) * 128)
            pb = 6 + k
            Rvs = self.R("vst%d" % k)
            for c in range(8):
                self.mm(self.bank[pb][:, :], hT[:, c, tsl], wtm[:, c, :], c == 0, c == 7, [Rh, Rwtm], [self.bres[pb]])
            self.cp("act", vst[k], self.bank[pb][:, :], [self.bres[pb]], [Rvs])
            for s_ in range(2):
                self.dma("sp", vs[s_][:, :, tt_, :].rearrange("h p c -> p h c"),
                         vst[k][:, s_ * 256:(s_ + 1) * 256].rearrange("p (h c) -> p h c", h=2), [Rvs], [Rkv_new()])

    def attn_alloc(self, dvp1, n_pt=3):
        cfg = self.cfg
        self.Kb = [self.take(cfg.S) for _ in range(2)]
        self.Vb = [self.take(cfg.KT * dvp1).rearrange("p (k c) -> p k c", c=dvp1) for _ in range(2)]
        self.pT = [self.take(cfg.TG) for _ in range(n_pt)]
        self.RK = [self.R("Kb%d" % i) for i in range(2)]
        self.RV = [self.R("Vb%d" % i) for i in range(2)]
        self.RpT = [self.R("pT%d" % i) for i in range(n_pt)]
        self.maps = []
        for i in range(2):
            self.P.op("pool", (lambda v: (lambda e: e.memset(v, 1.0)))(self.Vb[i][:, :, dvp1 - 1:dvp1]),
                      [], [self.RV[i]])

    def attn_map(self, kb, krows, qap, Rq, scale, dvp1, acc_of_j, accres_of_j, first_of_bank, last_of_bank,
                 pre=None, post=None, accT=None, accTres=None):
        self.maps.append(dict(kb=kb, krows=krows, qap=qap, Rq=Rq, scale=scale, dvp1=dvp1, acc=acc_of_j,
                              accres=accres_of_j, first=first_of_bank, last=last_of_bank, pre=pre, post=post,
                              accT=accT, accTres=accTres))

    def run_attn(self):
        cfg = self.cfg
        TG, KT, TPG = cfg.TG, cfg.KT, cfg.TPG
        maps = self.maps
        tiles = [(mi, kt) for mi in range(len(maps)) for kt in range(KT)]
        N = len(tiles)
        delay = min(4, KT - 1)
        pending = []

        def rec_qk(i):
            mi, kt = tiles[i]
            m = maps[mi]
            if kt == 0 and m["pre"] is not None:
                m["pre"]()
            sb_ = i % 2
            self.mm(self.bank[sb_][:, 0:TG], self.Kb[m["kb"]][m["krows"], kt * 128:(kt + 1) * 128], m["qap"], True, True,
                    [self.RK[m["kb"]], m["Rq"]], [self.bres[sb_]])

        def rec_exp_pv(i):
            mi, kt = tiles[i]
            m = maps[mi]
            sb_ = i % 2
            pi = i % len(self.pT)
            self.act(self.pT[pi], self.bank[sb_][:, 0:TG], AF.Exp, [self.bres[sb_]], [self.RpT[pi]], scale=m["scale"])
            if m["accT"] is not None:
                self.mm(m["accT"], self.Vb[m["kb"]][:, kt, 0:m["dvp1"]], self.pT[pi], kt == 0, kt == KT - 1,
                        [self.RpT[pi], self.RV[m["kb"]]], [m["accTres"]])
            else:
                for j in range(TPG):
                    self.mm(m["acc"](j), self.pT[pi][:, j * 128:(j + 1) * 128], self.Vb[m["kb"]][:, kt, 0:m["dvp1"]],
                            kt == 0 and m["first"](j), kt == KT - 1 and m["last"](j),
                            [self.RpT[pi], self.RV[m["kb"]]], [m["accres"](j)])
            if kt == KT - 1 and m["post"] is not None:
                pending.append((i + delay, m["post"]))

        rec_qk(0)
        for i in range(N):
            if i + 1 < N:
                rec_qk(i + 1)
            rec_exp_pv(i)
            while pending and pending[0][0] <= i:
                pending.pop(0)[1]()
        for _, p in pending:
            p()
        self.maps = []

    def even_attn(self, kvall):
        cfg = self.cfg
        T, TG, NG, NT, KT, TPG = cfg.T, cfg.TG, cfg.NG, cfg.NT, cfg.KT, cfg.TPG
        P = self.P
        P.barrier()
        lam_init = 0.8 - 0.6 * float(np.exp(-0.3 * 0))
        qT = self.fixed(self.off_qT(False), 4 * T).rearrange("p (c t) -> p c t", c=4)
        otok = self.fixed(self.off_otok(False), 4 * T).rearrange("p (t c) -> p t c", c=512)
        Rq, Ro = self.R("qT"), self.R("otok")
        self.arena_reset(0, self.off_otok(False))
        self.attn_alloc(129)
        lamt = self.take(256, F32)
        lp = self.take(128, F32)
        subg = self.take(128, F32)
        sm = self.take(8, F32)
        fin = [dict(r0=self.take(1, F32), r1=self.take(1, F32), O0=self.take(128, F32), od=self.take(128, F32),
                    ss=self.take(1, F32), junk=self.take(128)) for _ in range(2)]
        Rl = self.R("lam")
        self.dma("sp", lamt, self.d_rows[0:1, R_LAM:R_LAM + 256].partition_broadcast(128), [], [Rl])
        self.dma("sp", subg, self.d_rows[0:1, R_SUB:R_SUB + 128].partition_broadcast(128), [], [self.R("subg")])
        self.tt("dve", lp[:, 0:64], lamt[:, 0:64], lamt[:, 64:128], ALU.mult, [Rl], [self.R("lp")])
        self.tt("dve", lp[:, 64:128], lamt[:, 128:192], lamt[:, 192:256], ALU.mult, [Rl], [self.R("lp")])
        self.P.op("dve", lambda e: e.reduce_sum(out=sm[:, 0:2], in_=lp.rearrange("p (a d) -> p a d", a=2), axis=AX.X),
                  [self.R("lp")], [self.R("sm")])
        self.act(sm[:, 2:4], sm[:, 0:2], AF.Exp, [self.R("sm")], [self.R("sm2")])
        self.tt("dve", sm[:, 4:5], sm[:, 3:4], sm[:, 2:3], ALU.subtract, [self.R("sm2")], [self.R("sm3")])
        self.ts("dve", sm[:, 5:6], sm[:, 4:5], -lam_init, ALU.add, [self.R("sm3")], [self.R("neglam")])
        neglam = sm[:, 5:6]
        self.ts("dve", subg, subg, 1.0 - lam_init, ALU.mult, [self.R("subg")], [self.R("subg")])
        accsets = [(2, 3), (4, 5), (6, 7)]
        ai = 0
        fi_ = 0
        def kvload(h, kb):
            for r in range(4):
                self.dma("sp", self.Kb[kb][:, r * T:(r + 1) * T], kvall.gat(r, h * 128, (h + 1) * 128),
                         [kvall.rres(h * 128, (h + 1) * 128)], [self.RK[kb]])
                self.dma("sp", self.Vb[kb][:, r * NT:(r + 1) * NT, 0:128],
                         kvall.gat(r, 512 + h * 128, 512 + (h + 1) * 128).rearrange("p (t c) -> p t c", c=128),
                         [kvall.rres(512 + h * 128, 512 + (h + 1) * 128)], [self.RV[kb]])

        def finalize(h, g, sets):
            nonlocal fi_
            if True:
                for j in range(TPG):
                    f = fin[fi_ % 2]
                    k = fi_ % 2
                    fi_ += 1
                    Rf = lambda n: self.R("fin_%s%d" % (n, k))
                    (a0, r0b), (a1, r1b) = [((self.bank[s[0]][:, j * 129:(j + 1) * 129], self.bres[s[0]]) if j < 3
                                             else (self.bank[s[1]][:, 0:129], self.bres[s[1]])) for s in sets]
                    self.recip(f["r0"], a0[:, 128:129], [r0b], [Rf("r0")])
                    self.recip(f["r1"], a1[:, 128:129], [r1b], [Rf("r1")])
                    self.ts("dve", f["O0"], a0[:, 0:128], f["r0"][:, 0:1], ALU.mult, [r0b, Rf("r0")], [Rf("O0")])
                    self.tt("dve", f["r1"], f["r1"], neglam, ALU.mult, [Rf("r1"), self.R("neglam")], [Rf("r1")])
                    self.stt("dve", f["od"], a1[:, 0:128], f["r1"][:, 0:1], f["O0"], ALU.mult, ALU.add,
                             [r1b, Rf("r1"), Rf("O0")], [Rf("od")])
                    self.act(f["junk"], f["od"], AF.Square, [Rf("od")], [Rf("junk"), Rf("ss")], accum_out=f["ss"])
                    self.act(f["ss"], f["ss"], AF.Sqrt, [Rf("ss"), self.R("epsc")], [Rf("ss")], scale=1.0 / 128,
                             bias=self.epsc[:])
                    self.recip(f["ss"], f["ss"], [Rf("ss")], [Rf("ss")])
                    tt_ = g * TPG + j
                    self.stt("dve", otok[:, tt_, h * 128:(h + 1) * 128], f["od"], f["ss"][:, 0:1], subg,
                             ALU.mult, ALU.mult, [Rf("od"), Rf("ss"), self.R("subg")], [Ro])

        for h in range(4):
            kb = h % 2
            for g in range(NG):
                sl = slice(g * TG, (g + 1) * TG)
                sets = []
                for m in range(2):
                    bA, bB = accsets[ai % 3]
                    ai += 1
                    sets.append((bA, bB))
                    rows = slice(m * 64, (m + 1) * 64)
                    self.attn_map(kb, rows, qT[rows, h, sl], Rq, 0.125, 129,
                                  lambda j, bA=bA, bB=bB: (self.bank[bA][:, j * 129:(j + 1) * 129] if j < 3
                                                           else self.bank[bB][:, 0:129]),
                                  lambda j, bA=bA, bB=bB: self.bres[bA] if j < 3 else self.bres[bB],
                                  lambda j: j == 0 or j == 3, lambda j: j == min(TPG, 3) - 1 or j == 3,
                                  pre=(lambda h=h, kb=kb: kvload(h, kb)) if (g == 0 and m == 0) else None,
                                  post=(lambda h=h, g=g, sets=list(sets): finalize(h, g, sets)) if m == 1 else None)
        self.run_attn()

    def out_proj(self, odd, d_wo):
        cfg = self.cfg
        T, TG, NG, NT, TPG = cfg.T, cfg.TG, cfg.NG, cfg.NT, cfg.TPG
        P = self.P
        P.barrier()
        ncol = 1024 if odd else 512
        otok = self.fixed(self.off_otok(odd), (8 if odd else 4) * T).rearrange("p (t c) -> p t c", c=ncol)
        mixT = self.fixed(self.off_mixT(), 8 * T).rearrange("p (c t) -> p c t", c=8)
        Ro, Rm_ = self.R("otok"), self.R("mixT")
        self.arena_reset(0, self.off_otok(odd))
        wo = [self.take(1024).rearrange("p (c m) -> p c m", c=8) for _ in range(3)]
        Rwo = [self.R("wo%d" % i) for i in range(3)]
        c0 = 0 if odd else 4
        step = 0
        for c in range(ncol // 128):
            for g in range(NG):
                pb = step % 2
                step += 1
                pst = self.bank[pb].bitcast(BF16)
                for j in range(TPG):
                    tt_ = g * TPG + j
                    self.P.op("pe", (lambda o, i: (lambda e: e.transpose(o, i, self.cm(C_ID))))(
                        pst[:, j * 128:(j + 1) * 128], otok[:, tt_, c * 128:(c + 1) * 128]),
                        [Ro, self.R("cmat")], [self.bres[pb]])
                eng = "act" if step % 2 == 0 else "dve"
                self.cp(eng, mixT[:, c0 + c, g * TG:(g + 1) * TG], pst[:, 0:TG], [self.bres[pb]], [Rm_])
        for dc in range(8):
            b = dc % 3
            self.dma("pool", wo[b].rearrange("p c m -> p (c m)"), d_wo[dc], [], [Rwo[b]])
            for g in range(NG):
                sl = slice(g * TG, (g + 1) * TG)
                pb = 2 + step % 2
                step += 1
                for hc in range(8):
                    self.mm(self.bank[pb][:, 0:TG], wo[b][:, hc, :], mixT[:, hc, sl], hc == 0, hc == 7,
                            [Rwo[b], Rm_], [self.bres[pb]])
                self.tt("dve", self.xres[:, dc, sl], self.bank[pb][:, 0:TG], self.xres[:, dc, sl], ALU.add,
                        [self.bres[pb], self.Rx[dc][g]], [self.Rx[dc][g]])

    def odd_prep(self, vcol, kv_scr):
        cfg = self.cfg
        T, TG, NG, NT, TPG = cfg.T, cfg.TG, cfg.NG, cfg.NT, cfg.TPG
        P = self.P
        P.barrier()
        oq = self.off_qT(True)
        qTm = self.fixed(oq, 8 * T).rearrange("p (c t) -> p c t", c=8)
        qTg = self.fixed(oq + 8 * T, 4 * T).rearrange("p (c t) -> p c t", c=4)
        Rq, Rh = self.R("qT"), self.R("hT")
        self.arena_reset(0, oq)
        hT = self.take(8 * T).rearrange("p (c t) -> p c t", c=8)
        mark = self.a_lo
        rstd = self.take(T, F32)
        sd = self.take(T, F32)
        sq = [self.take(T) for _ in range(2)]
        self.norm_to_hT(hT, vcol, sq, rstd, sd)
        self.kvres = []

        def Rkv_new():
            r = Res("kvw")
            self.kvres.append(r)
            return r
        P.barrier()
        self.arena_reset(mark, oq)
        cosG = [self.take(TG, F32) for _ in range(2)]
        sinG = [self.take(TG, F32) for _ in range(2)]
        self.nr_alloc()
        wfm = [self.take(1024).rearrange("p (c m) -> p c m", c=8) for _ in range(3)]
        wkpe = self.take(256).rearrange("p (c m) -> p c m", c=8)
        wuq = self.take(1536).rearrange("p (c m) -> p c m", c=2)
        wkp = self.take(768)
        wvm = self.take(512)
        cqf = [self.take(TG, F32) for _ in range(2)]
        cqs = [self.take(TG) for _ in range(2)]
        csd = self.take(TG, F32)
        cqn = self.take(2 * TG).rearrange("p (c t) -> p c t", c=2)
        ckvn = self.take(TG)
        kpeb = self.take(TG)
        kst = [self.take(TG) for _ in range(2)]
        vst = [self.take(512) for _ in range(2)]
        Rw, Rtab = self.R("odw"), self.R("tab")
        for i in range(3):
            self.dma("pool", wfm[i].rearrange("p c m -> p (c m)"), self.d_odw_fm[i], [], [Rw])
        self.dma("pool", wkpe.rearrange("p c m -> p (c m)"), self.d_odw_kpe, [], [Rw])
        self.dma("pool", wuq.rearrange("p c m -> p (c m)"), self.d_wuq, [], [Rw])
        self.dma("pool", wkp, self.d_wkp, [], [Rw])
        self.dma("pool", wvm, self.d_wvm, [], [Rw])
        vsm = [kv_scr.loc(896 + 256 * s_, 1152 + 256 * s_).rearrange("r (b x) -> (r b) x", b=2).rearrange(
            "(h p) (t c) -> h p t c", h=4, c=64) for s_ in range(2)]
        step = 0
        ks = 0
        vi = 0
        for g in range(NG):
            sl = slice(g * TG, (g + 1) * TG)
            Rtg = self.R("tabg%d" % (g % 2))
            cosT, sinT = cosG[g % 2], sinG[g % 2]
            self.dma("sp", cosT, self.d_cs_m[0][:, sl], [], [Rtg])
            self.dma("sp", sinT, self.d_cs_m[1][:, sl], [], [Rtg])
            for c2 in range(2):
                for c in range(8):
                    self.mm(self.bank[c2][:, 0:TG], wfm[c2][:, c, :], hT[:, c, sl], c == 0, c == 7, [Rw, Rh],
                            [self.bres[c2]])
                self.cp("act", cqf[c2], self.bank[c2][:, 0:TG], [self.bres[c2]], [self.R("cqf%d" % c2)])
                self.act(cqs[c2], self.bank[c2][:, 0:TG], AF.Square, [self.bres[c2]], [self.R("cqs%d" % c2)])
            for c2 in range(2):
                self.mm(self.bank[6][:, 0:TG], self.cm(C_ONES), cqs[c2], c2 == 0, c2 == 1,
                        [self.R("cqs%d" % c2), self.R("cmat")], [self.bres[6]])
            self.act(csd, self.bank[6][:, 0:TG], AF.Sqrt, [self.bres[6], self.R("epsc")], [self.R("csd")],
                     scale=1.0 / 256, bias=self.epsc[:])
            self.recip(csd, csd, [self.R("csd")], [self.R("csd")])
            for c2 in range(2):
                self.stt("dve", cqn[:, c2, :], cqf[c2], self.vecs[:, V_CQ + c2:V_CQ + c2 + 1], csd, ALU.mult, ALU.mult,
                         [self.R("cqf%d" % c2), self.R("csd"), self.R("vecs")], [self.R("cqn")])
            for c in range(8):
                self.mm(self.bank[7][:, 0:TG], wfm[2][:, c, :], hT[:, c, sl], c == 0, c == 7, [Rw, Rh], [self.bres[7]])
            self.cp("act", cqf[0], self.bank[7][:, 0:TG], [self.bres[7]], [self.R("cqf0")])
            self.act(cqs[0], self.bank[7][:, 0:TG], AF.Square, [self.bres[7]], [self.R("cqs0")])
            self.mm(self.bank[6][:, 0:TG], self.cm(C_ONES), cqs[0], True, True, [self.R("cqs0"), self.R("cmat")],
                    [self.bres[6]])
            self.act(csd, self.bank[6][:, 0:TG], AF.Sqrt, [self.bres[6], self.R("epsc")], [self.R("csd")],
                     scale=1.0 / 128, bias=self.epsc[:])
            self.recip(csd, csd, [self.R("csd")], [self.R("csd")])
            self.stt("dve", ckvn, cqf[0], self.vecs[:, V_CKV:V_CKV + 1], csd, ALU.mult, ALU.mult,
                     [self.R("cqf0"), self.R("csd"), self.R("vecs")], [self.R("ckvn")])
            for c in range(8):
                self.mm(self.bank[7][0:32, 0:TG], wkpe[:, c, :], hT[:, c, sl], c == 0, c == 7, [Rw, Rh], [self.bres[7]])
            self.cp("act", kpeb[0:32], self.bank[7][0:32, 0:TG], [self.bres[7]], [self.R("kpeb")])
            for h in range(8):
                pb = step % 2
                step += 1
                for c2 in range(2):
                    self.mm(self.bank[pb][0:96, 0:TG], wuq[:, c2, h * 96:(h + 1) * 96], cqn[:, c2, :], c2 == 0, c2 == 1,
                            [Rw, self.R("cqn")], [self.bres[pb]])
                self.normrope(self.bank[pb][0:96, 0:TG], self.bres[pb], 96, C_O96, 96, V_MQ, C_RMM, cosT[0:96], sinT[0:96],
                              Rtg, qTm[0:96, h, sl], Rq)
            for h in range(8):
                pb = step % 2
                step += 1
                self.mm(self.bank[pb][0:96, 0:TG], wkp[:, h * 96:(h + 1) * 96], ckvn, True, False,
                        [Rw, self.R("ckvn")], [self.bres[pb]])
                self.mm(self.bank[pb][0:96, 0:TG], self.cm(C_SEL, 32, 96), kpeb[0:32], False, True,
                        [self.R("cmat"), self.R("kpeb")], [self.bres[pb]])
                kb = ks % 2
                ks += 1
                self.normrope(self.bank[pb][0:96, 0:TG], self.bres[pb], 96, C_O96, 96, V_MK, C_RMM, cosT[0:96], sinT[0:96],
                              Rtg, kst[kb][0:96], self.R("kst%d" % kb),
                              then=(lambda h=h, sl=sl, kb=kb: self.dma(
                                  "sp", kv_scr.loc(h * 96, (h + 1) * 96)[:, sl], kst[kb][0:96], [self.R("kst%d" % kb)],
                                  [Rkv_new()])))
            for j in range(TPG):
                tt_ = g * TPG + j
                k = vi % 2
                vi += 1
                self.mm(self.bank[6][:, :], ckvn[:, j * 128:(j + 1) * 128], wvm, True, True, [self.R("ckvn"), Rw],
                        [self.bres[6]])
                self.cp("act", vst[k], self.bank[6][:, :], [self.bres[6]], [self.R("vst%d" % k)])
                for s_ in range(2):
                    self.dma("sp", vsm[s_][:, :, tt_, :].rearrange("h p c -> p h c"),
                             vst[k][:, s_ * 256:(s_ + 1) * 256].rearrange("p (h c) -> p h c", h=4),
                             [self.R("vst%d" % k)], [Rkv_new()])
        self.nr_flush()
        self.gather(kv_scr, [0, 5, 1, 2, 3, 6])
        P.barrier()
        self.arena_reset(mark, oq)
        cosT = self.take(T, F32)
        sinT = self.take(T, F32)
        self.nr_alloc()
        wb = [self.take(1024).rearrange("p (c m) -> p c m", c=8) for _ in range(3)]
        wgv = self.take(1024).rearrange("p (c m) -> p c m", c=8)
        kst = [self.take(TG) for _ in range(4)]
        vst = [self.take(128) for _ in range(2)]
        Rwb = [self.R("wb%d" % i) for i in range(3)]
        self.dma("sp", cosT, self.d_cs_g[0], [], [Rtab])
        self.dma("sp", sinT, self.d_cs_g[1], [], [Rtab])
        self.dma("pool", wgv.rearrange("p c m -> p (c m)"), self.d_odw_gv, [], [self.R("wgv")])
        for ci in range(5):
            b = ci % 3
            self.dma("pool", wb[b].rearrange("p c m -> p (c m)"), self.d_odw_fm[3 + ci], [], [Rwb[b]])
            for g in range(NG):
                sl = slice(g * TG, (g + 1) * TG)
                pb = step % 2
                step += 1
                for c in range(8):
                    self.mm(self.bank[pb][:, 0:TG], wb[b][:, c, :], hT[:, c, sl], c == 0, c == 7, [Rwb[b], Rh],
                            [self.bres[pb]])
                if ci < 4:
                    self.normrope(self.bank[pb][:, 0:TG], self.bres[pb], 128, C_BD, 64, V_GQ, C_RMG, cosT[:, sl], sinT[:, sl],
                                  Rtab, qTg[:, ci, sl], Rq)
                else:
                    kb = ks % 4
                    ks += 1
                    self.normrope(self.bank[pb][:, 0:TG], self.bres[pb], 128, C_BD, 64, V_GK, C_RMG, cosT[:, sl], sinT[:, sl],
                                  Rtab, kst[kb], self.R("kstg%d" % kb),
                                  then=(lambda sl=sl, kb=kb: self.dma(
                                      "sp", kv_scr.loc(768, 896)[:, sl], kst[kb], [self.R("kstg%d" % kb)], [Rkv_new()])))
        self.nr_flush()
        vsg = kv_scr.loc(1408, 1536).rearrange("r (b x) -> (r b) x", b=2).rearrange("(h p) (t c) -> h p t c", h=2, c=64)
        for tt_ in range(NT):
            k = tt_ % 2
            tsl = slice(tt_ * 128, (tt_ + 1) * 128)
            pb = 6 + k
            for c in range(8):
                self.mm(self.bank[pb][:, 0:128], hT[:, c, tsl], wgv[:, c, :], c == 0, c == 7, [Rh, self.R("wgv")],
                        [self.bres[pb]])
            self.cp("act", vst[k], self.bank[pb][:, 0:128], [self.bres[pb]], [self.R("vstg%d" % k)])
            self.dma("sp", vsg[:, :, tt_, :].rearrange("h p c -> p h c"), vst[k].rearrange("p (h c) -> p h c", h=2),
                     [self.R("vstg%d" % k)], [Rkv_new()])

    def odd_attn(self, kvall):
        cfg = self.cfg
        T, TG, NG, NT, KT, TPG = cfg.T, cfg.TG, cfg.NG, cfg.NT, cfg.KT, cfg.TPG
        P = self.P
        P.barrier()
        oq = self.off_qT(True)
        qTm = self.fixed(oq, 8 * T).rearrange("p (c t) -> p c t", c=8)
        qTg = self.fixed(oq + 8 * T, 4 * T).rearrange("p (c t) -> p c t", c=4)
        otok = self.fixed(self.off_otok(True), 8 * T).rearrange("p (t c) -> p t c", c=1024)
        Rq, Ro = self.R("qT"), self.R("otok")
        self.arena_reset(0, self.off_otok(True))
        self.attn_alloc(65)
        r4 = [self.take(4, F32) for _ in range(2)]
        oTs = [self.take(TG, F32) for _ in range(2)]
        accb = [2, 3, 4, 5]
        ai = 0

        def run_map(kb, krows, qap, scale, col0, pre):
            nonlocal ai
            for g in range(NG):
                sl = slice(g * TG, (g + 1) * TG)
                a = accb[ai % 4]
                k = ai % 2
                b = 6 + k
                ai += 1

                def post(a=a, b=b, k=k, g=g, col0=col0):
                    self.cp("dve", oTs[k][0:65], self.bank[a][0:65, 0:TG], [self.bres[a]], [self.R("oTs%d" % k)])
                    for j in range(TPG):
                        self.P.op("pe", (lambda o, i_: (lambda e: e.transpose(o, i_, self.id32[0:65, 0:65])))(
                            self.bank[b][:, j * 65:(j + 1) * 65], oTs[k][0:65, j * 128:(j + 1) * 128]),
                            [self.R("oTs%d" % k), self.R("id32")], [self.bres[b]])
                    acc3 = self.bank[b][:, 0:TPG * 65].rearrange("p (j c) -> p j c", c=65)
                    self.recip(r4[k][:, 0:TPG], acc3[:, :, 64], [self.bres[b]], [self.R("r4%d" % k)])
                    self.tt("dve", otok[:, g * TPG:(g + 1) * TPG, col0:col0 + 64], acc3[:, :, 0:64],
                            r4[k][:, 0:TPG].unsqueeze(2).to_broadcast([128, TPG, 64]), ALU.mult,
                            [self.bres[b], self.R("r4%d" % k)], [Ro])
                self.attn_map(kb, krows, qap(sl), Rq, scale, 65, None, None, None, None,
                              pre=pre if g == 0 else None, post=post,
                              accT=self.bank[a][0:65, 0:TG], accTres=self.bres[a])

        def load_mla(h, kb):
            for r in range(4):
                self.dma("sp", self.Kb[kb][0:96, r * T:(r + 1) * T], kvall.gat(r, h * 96, (h + 1) * 96),
                         [kvall.rres(h * 96, (h + 1) * 96)], [self.RK[kb]])
                vseg = 896 + 256 * (h // 4)
                vsrc = kvall.gat(r, vseg, vseg + 256).rearrange("r (b x) -> (r b) x", b=2)[(h % 4) * 128:(h % 4 + 1) * 128, :]
                self.dma("sp", self.Vb[kb][:, r * NT:(r + 1) * NT, 0:64], vsrc.rearrange("p (t c) -> p t c", c=64),
                         [kvall.rres(vseg, vseg + 256)], [self.RV[kb]])

        def load_gqa(hk, kb):
            for r in range(4):
                for dup in range(2):
                    self.dma("sp", self.Kb[kb][dup * 64:(dup + 1) * 64, r * T:(r + 1) * T],
                             kvall.gat(r, 768 + hk * 64, 768 + (hk + 1) * 64),
                             [kvall.rres(768, 896)], [self.RK[kb]])
                vsrc = kvall.gat(r, 1408, 1536).rearrange("r (b x) -> (r b) x", b=2)[hk * 128:(hk + 1) * 128, :]
                self.dma("sp", self.Vb[kb][:, r * NT:(r + 1) * NT, 0:64], vsrc.rearrange("p (t c) -> p t c", c=64),
                         [kvall.rres(1408, 1536)], [self.RV[kb]])

        for u in range(10):
            kb = u % 2
            if u < 8:
                h = u
                run_map(kb, slice(0, 96), lambda sl, h=h: qTm[0:96, h, sl], 96 ** -0.5, h * 64,
                        lambda h=h, kb=kb: load_mla(h, kb))
            else:
                hk = u - 8
                for g4 in range(4):
                    qh = hk * 4 + g4
                    rows = slice((qh % 2) * 64, (qh % 2) * 64 + 64)
                    run_map(kb, rows, lambda sl, rows=rows, qh=qh: qTg[rows, qh // 2, sl], 0.125, 512 + qh * 64,
                            (lambda hk=hk, kb=kb: load_gqa(hk, kb)) if g4 == 0 else None)
        self.run_attn()

    def build(self):
        cfg = self.cfg
        T = cfg.T
        stages = self.stages
        self.setup()
        self.d_wgu = self.inp("wgu", [4, NJ, 128, 2048])
        self.d_wd = self.inp("wd", [4, 2, 8, 128, NJH * 128])
        xv = lambda ap: ap.rearrange("(c p) t -> p c t", p=128)
        allx = [r for row in self.Rx for r in row]
        if 1 in stages:
            d_x = self.inp("xT", [D, T])
            self.d_evw_fm = self.inp("evw_fm", [12, 128, 1024])
            self.d_evw_tm = self.inp("evw_tm", [2, 128, 4096])
            self.d_wsT = self.inp("wsT", [128, 1024])
            self.d_bias_t = self.inp("bias_t", [128, 512])
            self.d_cs_e = self.inp("cs_e", [2, 128, T])
            for c in range(8):
                self.dma("sp", self.xres[:, c, :], d_x[c * 128:(c + 1) * 128, :], [], self.Rx[c])
            self.ffn(0, V_FFN + 0)
            kv_e = self.make_kv("e", [256] * 4 if self.fused else [1024])
            self.even_prep(V_FFN + 8, kv_e)
        if 2 in stages:
            self.d_evwo = self.inp("evwo", [8, 128, 1024])
            self.d_odw_fm = self.inp("odw_fm", [8, 128, 1024])
            self.d_odw_kpe = self.inp("odw_kpe", [128, 256])
            self.d_odw_gv = self.inp("odw_gv", [128, 1024])
            self.d_wuq = self.inp("wuq", [128, 1536])
            self.d_wkp = self.inp("wkp", [128, 768])
            self.d_wvm = self.inp("wvm", [128, 512])
            self.d_cs_m = self.inp("cs_m", [2, 128, T])
            self.d_cs_g = self.inp("cs_g", [2, 128, T])
            if self.fused:
                kvall_e = self.gather(kv_e, [2, 3])
            else:
                kvall_e = KVScr([1024])
                kvall_e.gat_t[0] = self.inp("kvall_e", [4 * 1024, T], BF16)
                self.P.barrier()
                d_xs = self.inp("xs_in", [D, T])
                d_q = self.inp("q_in", [512, T], BF16)
                d_m = self.inp("m_in", [512, T], BF16)
                for c in range(8):
                    self.dma("sp", self.xres[:, c, :], d_xs[c * 128:(c + 1) * 128, :], [], self.Rx[c])
                qT = self.fixed(self.off_qT(False), 4 * T).rearrange("p (c t) -> p c t", c=4)
                mixT = self.fixed(self.off_mixT(), 8 * T).rearrange("p (c t) -> p c t", c=8)
                self.dma("sp", qT, xv(d_q), [], [self.R("qT")])
                self.dma("sp", mixT[:, 0:4, :], xv(d_m), [], [self.R("mixT")])
            self.even_attn(kvall_e)
            self.out_proj(False, self.d_evwo)
            self.ffn(1, V_FFN + 16)
            self.ffn(2, V_FFN + 24)
            kv_o = self.make_kv("o", [192] * 4 + [128] + [256] * 2 + [128] if self.fused else [1536])
            self.odd_prep(V_FFN + 32, kv_o)
        if 3 in stages:
            self.d_odwo = self.inp("odwo", [8, 128, 1024])
            if self.fused:
                kvall_o = self.gather(kv_o, [4, 7])
            else:
                kvall_o = KVScr([1536])
                kvall_o.gat_t[0] = self.inp("kvall_o", [4 * 1536, T], BF16)
                self.P.barrier()
                d_xs = self.inp("xs_in", [D, T])
                d_q = self.inp("qo_in", [12 * 128, T], BF16)
                for c in range(8):
                    self.dma("sp", self.xres[:, c, :], d_xs[c * 128:(c + 1) * 128, :], [], self.Rx[c])
                qTo = self.fixed(self.off_qT(True), 12 * T).rearrange("p (c t) -> p c t", c=12)
                self.dma("sp", qTo, xv(d_q), [], [self.R("qT")])
            self.odd_attn(kvall_o)
            self.out_proj(True, self.d_odwo)
            self.ffn(3, V_FFN + 40)
        self.P.barrier()
        last = max(stages)
        if last == 3:
            d_o = self.outp("outT", [D, T])
            for c in range(8):
                self.dma("sp", d_o[c * 128:(c + 1) * 128, :], self.xres[:, c, :], self.Rx[c], [Res()])
        else:
            d_o = self.outp("xs_out", [D, T])
            for c in range(8):
                self.dma("sp", d_o[c * 128:(c + 1) * 128, :], self.xres[:, c, :], self.Rx[c], [Res()])
            if last == 1:
                qT = self.fixed(self.off_qT(False), 4 * T).rearrange("p (c t) -> p c t", c=4)
                mixT = self.fixed(self.off_mixT(), 8 * T).rearrange("p (c t) -> p c t", c=8)
                self.dma("sp", xv(self.outp("q_out", [512, T], BF16)), qT, [self.R("qT")], [Res()])
                self.dma("sp", xv(self.outp("m_out", [512, T], BF16)), mixT[:, 0:4, :], [self.R("mixT")], [Res()])
            else:
                qTo = self.fixed(self.off_qT(True), 12 * T).rearrange("p (c t) -> p c t", c=12)
                self.dma("sp", xv(self.outp("qo_out", [12 * 128, T], BF16)), qTo, [self.R("qT")], [Res()])
        self.P.emit()
        self.st.close()
        return self.nc

    def make_kv(self, tag, segs):
        T = self.cfg.T
        kv = KVScr(segs)
        for i, n in enumerate(segs):
            if self.fused:
                kv.loc_t[i] = self.nc.dram_tensor("kv%s%d" % (tag, i), [n, T], BF16).ap()
                kv.gat_t[i] = self.nc.dram_tensor("kvall%s%d" % (tag, i), [4 * n, T], BF16).ap()
            else:
                kv.loc_t[i] = self.outp("kv_" + tag, [n, T], BF16)
        return kv

    def gather(self, kv, idxs=None):
        if not self.fused:
            return kv
        for i in (range(len(kv.segs)) if idxs is None else idxs):
            if i in kv.done:
                continue
            kv.done.add(i)
            o = self.P.op("pool", (lambda a, b_: (lambda e: e.collective_compute(
                "AllGather", ALU.bypass, replica_groups=[[0, 1, 2, 3], [4, 5, 6, 7]],
                ins=[a.opt()], outs=[b_.opt()])))(kv.loc_t[i], kv.gat_t[i]),
                list(self.kvres), [kv.res[i]], dma=True, inc=1)
            o.cc = True
        return kv


def prep_shared(inp):
    f = lambda a: np.ascontiguousarray(np.asarray(a, dtype=np.float32))
    g = {k: f(v) for k, v in inp.items() if k != "x"}
    out = {}
    wgu = []
    wd = []
    for l in range(2):
        for nm in ("ffn1", "ffn2"):
            w = g[nm + "_w_gu"][l]
            wgu.append(w.reshape(8, 128, 2, NJ, 128).transpose(3, 1, 0, 2, 4).reshape(NJ, 128, 2048))
            w = g[nm + "_w_down"][l]
            wd.append(w.reshape(2, NJH, 128, 8, 128).transpose(0, 3, 2, 1, 4).reshape(2, 8, 128, NJH * 128))
    out["wgu"] = np.ascontiguousarray(np.stack(wgu))
    out["wd"] = np.ascontiguousarray(np.stack(wd))
    w = g["ev_w_in"][0]
    cols = [np.arange(i * 128, (i + 1) * 128) for i in range(4)]
    cols += [np.arange(1024 + h * 128, 1024 + (h + 1) * 128) for h in range(4)]
    cols += [np.arange(1536 + h * 128, 1536 + (h + 1) * 128) for h in range(4)]
    out["evw_fm"] = np.stack([_chunk_fm(w, c) for c in cols])
    out["evw_tm"] = np.stack([_chunk_fm(w, np.arange(512, 1024)), _chunk_fm(w, np.arange(2048, 2560))])
    out["evwo"] = np.stack([_chunk_fm(g["ev_w_out"][0], np.arange(dc * 128, (dc + 1) * 128)) for dc in range(8)])
    out["wsT"] = np.ascontiguousarray(g["ev_w_s"][0].transpose(2, 0, 1).reshape(128, 1024))
    bs = g["ev_b_s"][0]
    bt = np.zeros((128, 4, 128), np.float32)
    for cg in range(4):
        bt[:64, cg, :] = bs[2 * cg][None, :]
        bt[64:, cg, :] = bs[2 * cg + 1][None, :]
    out["bias_t"] = bt.reshape(128, 512)
    w = g["od_w_in"][0]
    cols = [np.arange(0, 128), np.arange(128, 256), np.arange(256, 384)]
    cols += [np.arange(416 + i * 128, 416 + (i + 1) * 128) for i in range(4)]
    cols += [np.arange(928, 1056)]
    out["odw_fm"] = np.stack([_chunk_fm(w, c) for c in cols])
    out["odw_kpe"] = _chunk_fm(w, np.arange(384, 416))
    out["odw_gv"] = _chunk_fm(w, np.arange(1056, 1184))
    out["wuq"] = np.ascontiguousarray(g["od_w_uq"][0].reshape(2, 128, 768).transpose(1, 0, 2).reshape(128, 1536))
    wukv = g["od_w_ukv"][0]
    wkp = np.zeros((128, 8, 96), np.float32)
    wvm = np.zeros((128, 8, 64), np.float32)
    for h in range(8):
        wkp[:, h, :64] = wukv[:, h * 128: h * 128 + 64]
        wvm[:, h, :] = wukv[:, h * 128 + 64: h * 128 + 128]
    out["wkp"] = wkp.reshape(128, 768)
    out["wvm"] = wvm.reshape(128, 512)
    out["odwo"] = np.stack([_chunk_fm(g["od_w_out"][0], np.arange(dc * 128, (dc + 1) * 128)) for dc in range(8)])
    vecs = np.zeros((128, NVEC), np.float32)
    norms = [g["ffn1_norm"][0], g["ev_norm"][0], g["ffn2_norm"][0], g["ffn1_norm"][1], g["od_norm"][0], g["ffn2_norm"][1]]
    for i, nv in enumerate(norms):
        vecs[:, V_FFN + 8 * i: V_FFN + 8 * i + 8] = nv.reshape(8, 128).T
    vecs[:, V_EQ] = np.tile(g["ev_q_norm"][0], 2)
    vecs[:, V_EK] = np.tile(g["ev_k_norm"][0], 2)
    vecs[:, V_CQ:V_CQ + 2] = g["od_cq_norm"][0].reshape(2, 128).T
    vecs[:, V_CKV] = g["od_ckv_norm"][0]
    vecs[:96, V_MQ] = g["od_mla_q_norm"][0]
    vecs[:96, V_MK] = g["od_mla_k_norm"][0]
    vecs[:, V_GQ] = np.tile(g["od_gqa_q_norm"][0], 2)
    vecs[:, V_GK] = np.tile(g["od_gqa_k_norm"][0], 2)
    out["vecs"] = vecs
    rows = np.zeros((1, NROW), np.float32)
    rows[0, R_SGU:R_SGU + 512] = g["ev_sgu_norm"][0].reshape(512)
    rows[0, R_SUB:R_SUB + 128] = g["ev_sub_norm"][0]
    rows[0, R_LAM:R_LAM + 256] = np.concatenate([g["ev_lam_q1"][0], g["ev_lam_k1"][0], g["ev_lam_q2"][0], g["ev_lam_k2"][0]])
    out["rows"] = rows
    return out


_NP = {F32: np.float32, BF16: ml_dtypes.bfloat16}


def run_stage(cfg, stages, fused, shared, percore, runner=None):
    b = Builder(cfg, stages, fused)
    nc = b.build()
    in_maps = []
    for c in range(8):
        m = {}
        for name, (shape, dt) in b.din.items():
            a = percore[c][name] if name in percore[c] else shared[name]
            assert tuple(a.shape) == tuple(shape), (name, a.shape, shape)
            m[name] = np.ascontiguousarray(a, dtype=_NP[dt])
        in_maps.append(m)
    if runner is None:
        res = run_bass_kernel_spmd(nc, in_maps, core_ids=list(range(8)))
        return res.results
    return runner(nc, in_maps, {k: (s, _NP[d]) for k, (s, d) in b.dout.items()})


def run_model(inp, T=2048, TG=512, fused=False, runner=None):
    cfg = Cfg(T, TG)
    x = np.asarray(inp["x"], dtype=np.float32)
    B, S, _ = x.shape
    assert B == 2 and S == 4 * T
    shared = prep_shared(inp)
    xt = x.reshape(8, T, D)
    percore = []
    for c in range(8):
        cmat, cs_e, cs_m, cs_g = host_consts(T, S, c)
        percore.append({"xT": np.ascontiguousarray(xt[c].T), "cmat": cmat, "cs_e": cs_e, "cs_m": cs_m, "cs_g": cs_g})
    if fused:
        res = run_stage(cfg, (1, 2, 3), True, shared, percore, runner)
    else:
        def regroup(res, key):
            full = [np.concatenate([res[g * 4 + r][key] for r in range(4)], axis=0) for g in range(2)]
            return [full[c // 4] for c in range(8)]
        r1 = run_stage(cfg, (1,), False, shared, percore, runner)
        kva = regroup(r1, "kv_e")
        for c in range(8):
            percore[c].update({"kvall_e": kva[c], "xs_in": r1[c]["xs_out"], "q_in": r1[c]["q_out"], "m_in": r1[c]["m_out"]})
        r2 = run_stage(cfg, (2,), False, shared, percore, runner)
        kva = regroup(r2, "kv_o")
        for c in range(8):
            percore[c].update({"kvall_o": kva[c], "xs_in": r2[c]["xs_out"], "qo_in": r2[c]["qo_out"]})
        res = run_stage(cfg, (3,), False, shared, percore, runner)
    out = np.stack([np.asarray(res[c]["outT"], dtype=np.float32).T for c in range(8)])
    return np.ascontiguousarray(out.reshape(B, S, D))


def kernel(**inputs):
    return run_model(inputs, T=2048, TG=512, fused=True)
```
